# Optimizing a Trainium2 kernel written in Bass

```python
import math
import jax, jax.numpy as jnp
from jax import lax
import numpy as np

D_MODEL = 2048
BATCH = 4
SEQ = 4096
DEPTH = 1

MIX_WIDTH = D_MODEL
DIFF_WIDTH = MIX_WIDTH // 2
MLA_WIDTH = MIX_WIDTH - DIFF_WIDTH

DH_DIFF = 64
DV_DIFF = 2 * DH_DIFF
H_DIFF = DIFF_WIDTH // DV_DIFF

QK_NOPE = 128
QK_ROPE = 64
QK_HEAD = QK_NOPE + QK_ROPE
V_MLA = 128
H_MLA = MLA_WIDTH // V_MLA
Q_LORA = 512
KV_LORA = 256
ROPE_THETA = 10000.0

D_FF = ((8 * D_MODEL + 3 * 256 - 1) // (3 * 256)) * 256

REL_BUCKETS = 32
REL_MAX_DIST = 128
Q_BLOCK = 128
EPS = 1e-6

Q_DIFF_COLS = H_DIFF * 2 * DH_DIFF
K_DIFF_COLS = H_DIFF * 2 * DH_DIFF
V_DIFF_COLS = H_DIFF * DV_DIFF
IN_COLS = Q_DIFF_COLS + K_DIFF_COLS + V_DIFF_COLS + Q_LORA + KV_LORA + QK_ROPE

kernel_name = "hybrid_diffattn_mla_adaln_encoder"


def rms_norm(x, g):
    xf = x.astype(jnp.float32)
    y = xf * lax.rsqrt(jnp.mean(xf * xf, axis=-1, keepdims=True) + EPS)
    return (y * g.astype(jnp.float32)).astype(x.dtype)


def t5_bucket(rel):
    nb = REL_BUCKETS // 2
    max_exact = nb // 2
    base = jnp.where(rel > 0, nb, 0)
    n = jnp.abs(rel)
    nf = jnp.maximum(n, 1).astype(jnp.float32)
    large = max_exact + (jnp.log(nf / max_exact) / math.log(REL_MAX_DIST / max_exact)
                         * (nb - max_exact)).astype(jnp.int32)
    large = jnp.minimum(large, nb - 1)
    return base + jnp.where(n < max_exact, n, large)


def rope_tables(seq):
    pos = jnp.arange(seq, dtype=jnp.float32)
    inv = 1.0 / (ROPE_THETA ** (jnp.arange(0, QK_ROPE, 2, dtype=jnp.float32) / QK_ROPE))
    ang = pos[:, None] * inv[None, :]
    return jnp.cos(ang)[:, None, :], jnp.sin(ang)[:, None, :]


def rope_tail(x, cos, sin):
    nope, pe = x[..., :QK_NOPE], x[..., QK_NOPE:]
    half = QK_ROPE // 2
    x1, x2 = pe[..., :half], pe[..., half:]
    c, s = cos.astype(x.dtype), sin.astype(x.dtype)
    return jnp.concatenate([nope, x1 * c - x2 * s, x2 * c + x1 * s], axis=-1)


def to_blocks(t):
    b, s = t.shape[:2]
    return jnp.moveaxis(t.reshape(b, s // Q_BLOCK, Q_BLOCK, *t.shape[2:]), 1, 0)


def from_blocks(t):
    t = jnp.moveaxis(t, 0, 1)
    return t.reshape(t.shape[0], t.shape[1] * t.shape[2], *t.shape[3:])


def differential_attention(q, k, v, lam, rel_bias):
    s = q.shape[1]
    scale = DH_DIFF ** -0.5
    k_pos = jnp.arange(s, dtype=jnp.int32)
    lam32 = lam.astype(jnp.float32)

    def block(args):
        q_blk, i = args
        logits = jnp.einsum('bqhcd,bkhcd->bchqk', q_blk, k).astype(jnp.float32) * scale
        q_pos = i * Q_BLOCK + jnp.arange(Q_BLOCK, dtype=jnp.int32)
        bias = rel_bias[t5_bucket(k_pos[None, :] - q_pos[:, None])]
        logits = logits + jnp.transpose(bias, (2, 0, 1)).astype(jnp.float32)
        p = jax.nn.softmax(logits, axis=-1)
        a = (p[:, 0] - lam32 * p[:, 1]).astype(v.dtype)
        return jnp.einsum('bhqk,bkhe->bqhe', a, v)

    out = lax.map(block, (to_blocks(q), jnp.arange(s // Q_BLOCK, dtype=jnp.int32)))
    return from_blocks(out)


def latent_attention(q, k, v):
    scale = QK_HEAD ** -0.5

    def block(q_blk):
        logits = jnp.einsum('bqhd,bkhd->bhqk', q_blk, k).astype(jnp.float32) * scale
        p = jax.nn.softmax(logits, axis=-1).astype(v.dtype)
        return jnp.einsum('bhqk,bkhd->bqhd', p, v)

    return from_blocks(lax.map(block, to_blocks(q)))


def setup_inputs(seed: int = 0) -> dict:
    key = jax.random.key(seed)
    ks = jax.random.split(key, 24)
    f32 = jnp.float32
    L = DEPTH

    def w(k, shape, fan_in, scale=1.0):
        return jax.random.normal(k, shape, f32) * (scale * fan_in ** -0.5)

    def gain(k, shape):
        return 1.0 + 0.02 * jax.random.normal(k, shape, f32)

    return {
        "x": jax.random.normal(ks[0], (BATCH, SEQ, D_MODEL), f32),
        "c": jax.random.normal(ks[1], (BATCH, D_MODEL), f32),
        "rel_bias": 0.5 * jax.random.normal(ks[2], (REL_BUCKETS, H_DIFF), f32),
        "w_ada": w(ks[3], (L, D_MODEL, 6 * D_MODEL), D_MODEL, 0.5),
        "b_ada": 0.1 * jax.random.normal(ks[4], (L, 6 * D_MODEL), f32),
        "g_norm1": gain(ks[5], (L, D_MODEL)),
        "w_in": w(ks[6], (L, D_MODEL, IN_COLS), D_MODEL),
        "g_q_diff": gain(ks[7], (L, DH_DIFF)),
        "g_k_diff": gain(ks[8], (L, DH_DIFF)),
        "lambda_vecs": 0.1 * jax.random.normal(ks[9], (L, 4, DH_DIFF), f32),
        "g_subln": gain(ks[10], (L, DV_DIFF)),
        "g_q_a": gain(ks[11], (L, Q_LORA)),
        "w_q_b": w(ks[12], (L, Q_LORA, H_MLA * QK_HEAD), Q_LORA),
        "g_kv_a": gain(ks[13], (L, KV_LORA)),
        "w_kv_b": w(ks[14], (L, KV_LORA, H_MLA * (QK_NOPE + V_MLA)), KV_LORA),
        "g_q_mla": gain(ks[15], (L, QK_HEAD)),
        "g_k_mla": gain(ks[16], (L, QK_HEAD)),
        "w_out": w(ks[17], (L, MIX_WIDTH, D_MODEL), MIX_WIDTH),
        "g_norm2": gain(ks[18], (L, D_MODEL)),
        "w_gate": w(ks[19], (L, D_MODEL, D_FF), D_MODEL),
        "w_up": w(ks[20], (L, D_MODEL, D_FF), D_MODEL),
        "w_down": w(ks[21], (L, D_FF, D_MODEL), D_FF),
    }


def reference(x, c, rel_bias, w_ada, b_ada, g_norm1, w_in, g_q_diff, g_k_diff, lambda_vecs,
              g_subln, g_q_a, w_q_b, g_kv_a, w_kv_b, g_q_mla, g_k_mla, w_out,
              g_norm2, w_gate, w_up, w_down):
    b, s, _ = x.shape
    cos, sin = rope_tables(s)
    c_act = jax.nn.silu(c)

    for l in range(DEPTH):
        lambda_init = 0.8 - 0.6 * math.exp(-0.3 * l)
        mod = c_act @ w_ada[l] + b_ada[l]
        sh1, sc1, gt1, sh2, sc2, gt2 = [m[:, None, :] for m in jnp.split(mod, 6, axis=-1)]

        h = rms_norm(x, g_norm1[l]) * (1 + sc1) + sh1
        proj = h @ w_in[l]
        o = 0
        q_d = proj[..., o:o + Q_DIFF_COLS].reshape(b, s, H_DIFF, 2, DH_DIFF); o += Q_DIFF_COLS
        k_d = proj[..., o:o + K_DIFF_COLS].reshape(b, s, H_DIFF, 2, DH_DIFF); o += K_DIFF_COLS
        v_d = proj[..., o:o + V_DIFF_COLS].reshape(b, s, H_DIFF, DV_DIFF); o += V_DIFF_COLS
        cq = proj[..., o:o + Q_LORA]; o += Q_LORA
        ckv = proj[..., o:o + KV_LORA]; o += KV_LORA
        k_pe = proj[..., o:o + QK_ROPE]

        lv = lambda_vecs[l]
        lam = (jnp.exp(jnp.sum(lv[0] * lv[1])) - jnp.exp(jnp.sum(lv[2] * lv[3]))
               + lambda_init)
        q_d = rms_norm(q_d, g_q_diff[l])
        k_d = rms_norm(k_d, g_k_diff[l])
        a_out = differential_attention(q_d, k_d, v_d, lam, rel_bias)
        a_out = (rms_norm(a_out, g_subln[l]) * (1.0 - lambda_init)).reshape(b, s, DIFF_WIDTH)

        q_m = (rms_norm(cq, g_q_a[l]) @ w_q_b[l]).reshape(b, s, H_MLA, QK_HEAD)
        kv = (rms_norm(ckv, g_kv_a[l]) @ w_kv_b[l]).reshape(b, s, H_MLA, QK_NOPE + V_MLA)
        k_nope, v_m = kv[..., :QK_NOPE], kv[..., QK_NOPE:]
        k_m = jnp.concatenate(
            [k_nope, jnp.broadcast_to(k_pe[:, :, None, :], (b, s, H_MLA, QK_ROPE))], axis=-1)
        q_m = rope_tail(rms_norm(q_m, g_q_mla[l]), cos, sin)
        k_m = rope_tail(rms_norm(k_m, g_k_mla[l]), cos, sin)
        b_out = latent_attention(q_m, k_m, v_m).reshape(b, s, MLA_WIDTH)

        mix = jnp.concatenate([a_out, b_out], axis=-1) @ w_out[l]
        x = x + gt1 * mix

        h2 = rms_norm(x, g_norm2[l]) * (1 + sc2) + sh2
        ffn = (jax.nn.silu(h2 @ w_gate[l]) * (h2 @ w_up[l])) @ w_down[l]
        x = x + gt2 * ffn

    return x
```

```python
import math
import numpy as np
import concourse.bass as bass
import concourse.mybir as mybir
from concourse.bass_utils import run_bass_kernel_spmd

F32 = mybir.dt.float32
BF16 = mybir.dt.bfloat16
AF = mybir.ActivationFunctionType
ALU = mybir.AluOpType

D = 2048
S = 4096
SQ = 2048
B = 4
NH = 8
DFF = 5632
NFC = DFF // 128
EPS = 1e-6
LAMBDA_INIT = 0.2
C_DIFF = 8.0
C_MLA = 14.0
FFN_T = 512
GW = 1152
ST = 256
NT = ST // 128

DEBUG = False
STOP_AFTER = 99
SUB = 99
AP_MASK = 31
NHA = 8
NSTL = 99


class Ev:
    __slots__ = ("sem", "val", "eng")

    def __init__(self, sem, val, eng):
        self.sem, self.val, self.eng = sem, val, eng


class Tok:
    __slots__ = ("name", "w", "r", "excl")

    def __init__(self, name, after=(), excl=False):
        self.name = name
        self.w = {}
        self.r = {}
        self.excl = excl
        for i, ev in enumerate(after):
            self.w[("init", i)] = ev


def PTok(name, after=()):
    return Tok(name, after, excl=True)


class DSem:
    _n = [0]

    def __init__(self, nc, name):
        DSem._n[0] += 1
        self.h = nc.alloc_semaphore("%s_%d" % (name, DSem._n[0]))
        self.count = 0


class EngState:
    LIMIT = 30000

    def __init__(self, nc, name, same_sync):
        self.nc, self.name, self.same_sync = nc, name, same_sync
        self.nsem = 0
        self.sem = None
        self.count = 0
        self.known = {}
        self.pending = []
        self.ops = []
        self.last = None
        self._newsem()

    def _newsem(self):
        self.sem = self.nc.alloc_semaphore("e_%s_%d" % (self.name, self.nsem))
        self.nsem += 1
        self.count = 0

    def new_event(self, sig):
        if not sig:
            ev = Ev(None, None, self.name)
            self.pending.append(ev)
            return ev
        if self.count >= self.LIMIT:
            self._newsem()
        self.count += 1
        ev = Ev(self.sem, self.count, self.name)
        for p in self.pending:
            p.sem, p.val = ev.sem, ev.val
        self.pending = []
        self.last = ev
        return ev


class Prog:
    def __init__(self, nc):
        self.nc = nc
        self.E = {
            "pe": EngState(nc, "pe", False),
            "act": EngState(nc, "act", True),
            "dve": EngState(nc, "dve", True),
            "pool": EngState(nc, "pool", True),
            "sp": EngState(nc, "sp", True),
        }
        self.dsems = []
        self.n_waits = 0

    def dsem(self, name):
        s = DSem(self.nc, name)
        self.dsems.append(s)
        return s

    def _need(self, E, eng, ev, waits, is_dma_issue):
        if ev is None:
            return
        if ev.sem is None:
            if ev.eng == eng and not is_dma_issue:
                return
            raise RuntimeError("dependency on unsignaled op (%s <- %s)" % (eng, ev.eng))
        if ev.eng == eng and not E.same_sync and not is_dma_issue:
            return
        k = ev.sem.num
        if E.known.get(k, 0) >= ev.val:
            return
        E.known[k] = ev.val
        waits.append((ev.sem, ev.val))

    def _deps(self, E, eng, reads, writes, is_dma_issue=False):
        waits = []
        for t in reads:
            for ev in t.w.values():
                self._need(E, eng, ev, waits, is_dma_issue)
            if t.excl:
                for k, ev in t.r.items():
                    if k != eng:
                        self._need(E, eng, ev, waits, is_dma_issue)
        for t in writes:
            for ev in t.w.values():
                self._need(E, eng, ev, waits, is_dma_issue)
            for ev in t.r.values():
                self._need(E, eng, ev, waits, is_dma_issue)
        self.n_waits += len(waits)
        return waits

    def op(self, eng, fn, reads=(), writes=(), sig=True):
        E = self.E[eng]
        waits = self._deps(E, eng, reads, writes)
        ev = E.new_event(sig)
        for t in reads:
            t.r[eng] = ev
        for t in writes:
            t.w = {eng: ev}
            t.r = {}
        E.ops.append((waits, fn, ("inc", ev.sem) if sig else None))
        return ev

    def dma(self, q, out, in_, sem, reads=(), writes=(), acc_writes=(), **kw):
        E = self.E[q]
        waits = self._deps(E, q, reads, writes, is_dma_issue=True)
        sem.count += 16
        ev = Ev(sem.h, sem.count, None)
        key = ("dma", sem.h.num)
        for t in reads:
            t.r[key] = ev
        for t in writes:
            t.w = {key: ev}
            t.r = {}
        for t in acc_writes:
            t.w[key] = ev
        E.ops.append((waits, (I("dma_start", out=out, in_=in_, **kw)), ("dma", sem.h)))
        return ev

    def all_events(self):
        evs = []
        for E in self.E.values():
            if E.pending:
                raise RuntimeError("pending unsignaled ops on %s at barrier" % E.name)
            if E.last is not None:
                evs.append(E.last)
        for s in self.dsems:
            if s.count:
                evs.append(Ev(s.h, s.count, None))
        return evs

    def final_wait(self, eng="sp"):
        E = self.E[eng]
        waits = []
        for ev in self.all_events():
            if ev.eng == eng:
                continue
            k = ev.sem.num
            if E.known.get(k, 0) >= ev.val:
                continue
            E.known[k] = ev.val
            waits.append((ev.sem, ev.val))
        E.ops.append((waits, None, None))

    def emit(self):
        nc = self.nc
        hooks = {"pe": "tensor", "act": "scalar", "dve": "vector", "pool": "gpsimd", "sp": "sync"}
        with nc.Block() as block:
            for name, attr in hooks.items():
                ops = self.E[name].ops

                def body(e, ops=ops):
                    for waits, fn, post in ops:
                        for (s, v) in waits:
                            e.wait_ge(s, v)
                        if fn is None:
                            continue
                        name_, args_, kw_ = fn
                        ins = getattr(e, name_)(*args_, **kw_)
                        if post is not None:
                            if post[0] == "inc":
                                ins.then_inc(post[1], 1)
                            else:
                                ins.then_inc(post[1], 16)

                getattr(block, attr)(body)


class Arena:
    def __init__(self, nc, lo=16512, hi=229376):
        self.nc, self.off, self.hi = nc, lo, hi
        self.n = 0
        self.peak = lo

    def alloc(self, name, shape, dt):
        nb = int(np.prod(shape[1:])) * (4 if dt == F32 else 2)
        nb = (nb + 63) // 64 * 64
        if self.off + nb > self.hi:
            raise RuntimeError("SBUF arena overflow allocating %s (%d + %d > %d)" % (name, self.off, nb, self.hi))
        self.n += 1
        t = self.nc.alloc_sbuf_tensor_at("%s_%d" % (name, self.n), list(shape), dt, offset=self.off)
        self.off += nb
        self.peak = max(self.peak, self.off)
        return t

    def mark(self):
        return self.off

    def release(self, m):
        self.off = m


class Ring:
    def __init__(self, P, arena, name, n, shape, dt, after=(), dma=True):
        self.n = n
        self.t = [arena.alloc("%s%d" % (name, i), shape, dt) for i in range(n)]
        self.tok = [Tok("%s%d" % (name, i), after) for i in range(n)]
        self.sem = [P.dsem("d_%s%d" % (name, i)) for i in range(n)] if dma else None
        self.i = -1

    def next(self):
        self.i += 1
        k = self.i % self.n
        return self.t[k], self.tok[k], (self.sem[k] if self.sem else None)


def I(name, *args, **kw):
    return (name, args, kw)


def build_program():
    nc = bass.Bass("TRN2", target_bir_lowering=False)
    P = Prog(nc)
    A = Arena(nc)

    def din(name, shape, dt=F32):
        return nc.dram_tensor(name, list(shape), dt, kind="ExternalInput").ap()

    def dscr(name, shape, dt):
        return nc.dram_tensor(name, list(shape), dt, kind=("ExternalOutput" if DEBUG else "Internal")).ap()

    xk = din("xk", [S, D])
    cT_d = din("cT", [128, 16])
    w_ada = din("w_ada", [D, 6 * D])
    b_ada = din("b_ada", [1, 6 * D])
    g1T_d = din("g1T", [128, 16])
    g2T_d = din("g2T", [128, 16])
    w_kv = din("w_kv", [D, 2432])
    w_q = din("w_q", [D, 1536])
    w_qb = din("w_qb", [512, 2048])
    w_kvb = din("w_kvb", [256, 2048])
    w_out = din("w_out", [D, D])
    w_gate = din("w_gate", [D, DFF])
    w_up = din("w_up", [D, DFF])
    w_down = din("w_down", [DFF, D])
    gcols_d = din("gcols", [128, 16])
    gsub_d = din("gsub_bc", [128, 128])
    lam_d = din("lam_bc", [128, 256])
    cos_d = din("cos_t", [64, S])
    sin_d = din("sin_t", [64, S])
    gtab_d = din("gtab", [128, NH, GW])
    gfar_d = din("gfar", [128, 16])
    cmat_d = din("cmat", [128, 512])
    out_d = nc.dram_tensor("out", [SQ, D], F32, kind="ExternalOutput").ap()

    mod_scr = dscr("mod_scr", [1, 6 * D], F32)
    KdT = dscr("KdT", [NH, 128, S], BF16)
    QdT = dscr("QdT", [NH, 128, SQ], BF16)
    Vd = dscr("Vd", [NH, 128, 32, 130], BF16)
    KmT = dscr("KmT", [NH, 128, S], BF16)
    QmnT = dscr("QmnT", [NH, 128, SQ], BF16)
    QmrT = dscr("QmrT", [NH, 64, SQ], BF16)
    Vm = dscr("Vm", [NH, 128, 32, 130], BF16)
    x1_scr = dscr("x1_scr", [SQ, D], F32)
    h2T_scr = dscr("h2T_scr", [128, 16, SQ], BF16)
    dbg_ksc = nc.dram_tensor("dbg_ksc", [128, 32 * 24 + 64 * 32], F32, kind="ExternalOutput").ap() if DEBUG else None

    banks = [nc.alloc_psum_tensor("bank%d" % i, [128, 512], F32) for i in range(8)]

    cmat = A.alloc("cmat", [128, 512], BF16)
    ident = cmat[:, 0:128]
    ones = cmat[:, 128:256]
    bones = cmat[:, 256:384]
    sel2 = cmat[:, 384:386]
    gcols = A.alloc("gcols", [128, 16], F32)
    cv = A.alloc("cv", [128, 8], F32)
    A1T = A.alloc("A1T", [128, 16], F32)
    B1T = A.alloc("B1T", [128, 16], F32)
    A2T = A.alloc("A2T", [128, 16], F32)
    B2T = A.alloc("B2T", [128, 16], F32)
    ksc_d = A.alloc("ksc_d", [128, 32, 16], F32)
    ksc_m = A.alloc("ksc_m", [128, 32, 8], F32)
    kropeT = A.alloc("kropeT", [64, S], BF16)
    t_const = Tok("const")
    t_cv = Tok("cv")
    t_mod = Tok("mod")
    t_ksc = Tok("ksc")
    t_krope = Tok("krope")
    s_const = P.dsem("d_const")

    CV_EPS, CV_LN8, CV_LN192, CV_LN08, CV_NCD, CV_NCM, CV_ZERO = 0, 1, 2, 3, 4, 5, 6
    cvals = [EPS, math.log(0.125), math.log(192 ** -0.5), math.log(1.0 - LAMBDA_INIT), -C_DIFF, -C_MLA, 0.0, 1.0]

    P.dma("pool", cmat[:], cmat_d, s_const, acc_writes=[t_const])
    s_const2 = P.dsem("d_const2")
    P.dma("sp", gcols[:], gcols_d, s_const2, acc_writes=[t_const])
    for i, v in enumerate(cvals):
        P.op("dve", I("memset", cv[:, i:i + 1], float(v)), writes=[t_cv])

    def cvc(i, n=128):
        return cv[0:n, i:i + 1]

    def gc(i, n=128):
        return gcols[0:n, i:i + 1]

    G_QD, G_KD, G_QA, G_KVA, G_QMN, G_QMR, G_QMP, G_KMN, G_KMR, G_KMP = 0, 1, 2, 6, 8, 9, 10, 11, 12, 13

    def rstd_ops(src, tmp, dst, inv_n, ln_bias_col, reads, tmp_tok, dst_tok, npart=128):
        P.op("act", I("activation", out=tmp, in_=src, func=AF.Ln, scale=inv_n, bias=cvc(CV_EPS, npart)),
             reads=list(reads) + [t_cv], writes=[tmp_tok])
        P.op("act", I("activation", out=dst, in_=tmp, func=AF.Exp, scale=-0.5, bias=cvc(ln_bias_col, npart)),
             reads=[t_cv, tmp_tok], writes=[dst_tok])

    m0 = A.mark()
    cT = A.alloc("cT", [128, 16], F32)
    cact = A.alloc("cact", [128, 16], BF16)
    g1T = A.alloc("g1T", [128, 16], F32)
    g2T = A.alloc("g2T", [128, 16], F32)
    modT = A.alloc("modT", [128, 96], F32)
    modrow = A.alloc("modrow", [1, 6 * D], F32)
    brow = A.alloc("brow", [1, 6 * D], F32)
    wr = Ring(P, A, "wada", 2, [128, 16, 1024], BF16)
    t_c = Tok("c")
    t_cact = Tok("cact")
    t_brow = Tok("brow")
    t_modrow = Tok("modrow")
    s_p0 = P.dsem("d_p0")
    s_p0b = P.dsem("d_p0b")
    s_p0c = P.dsem("d_p0c")
    P.dma("sp", cT[:], cT_d, s_p0, acc_writes=[t_c])
    P.dma("sp", g1T[:], g1T_d, s_p0, acc_writes=[t_c])
    P.dma("sp", g2T[:], g2T_d, s_p0, acc_writes=[t_c])
    P.dma("sp", brow[:], b_ada, s_p0b, writes=[t_brow])
    P.op("act", I("activation", out=cact[:], in_=cT[:], func=AF.Silu), reads=[t_c], writes=[t_cact])
    pb = [PTok("p0bank0"), PTok("p0bank1")]
    for nb in range(12):
        wt, wtok, wsem = wr.next()
        P.dma("pool", wt[:], w_ada[:, nb * 1024:(nb + 1) * 1024].rearrange("(kc p) n -> p kc n", p=128),
              wsem, writes=[wtok])
        for half in range(2):
            i = nb * 2 + half
            bk, btok = banks[i % 2], pb[i % 2]
            for kc in range(16):
                P.op("pe", I("matmul", bk[0:1, :], lhsT=cact[:, kc:kc + 1], rhs=wt[:, kc, half * 512:(half + 1) * 512],
                             start=(kc == 0), stop=(kc == 15)),
                     reads=[wtok, t_cact], writes=[btok], sig=(kc == 15))
            c0 = i * 512
            P.op("dve", I("tensor_tensor", out=modrow[0:1, c0:c0 + 512], in0=bk[0:1, :], in1=brow[0:1, c0:c0 + 512],
                          op=ALU.add), reads=[btok, t_brow], writes=[t_modrow])
    t_modscr = Tok("modscr")
    P.dma("sp", mod_scr, modrow[:], s_p0c, reads=[t_modrow], writes=[t_modscr])
    t_modT = Tok("modT")
    s_p0d = P.dsem("d_p0d")
    for j0 in range(0, 96, 16):
        P.dma("sp", modT[:, j0:j0 + 16],
              mod_scr[:, j0 * 128:(j0 + 16) * 128].rearrange("o (j p) -> p (o j)", p=128),
              s_p0d, reads=[t_modscr], acc_writes=[t_modT], allow_slow_non_contiguous=True)
    P.op("dve", I("scalar_tensor_tensor", out=A1T[:], in0=modT[:, 16:32], scalar=1.0, in1=g1T[:],
                  op0=ALU.add, op1=ALU.mult), reads=[t_modT, t_c], writes=[t_mod])
    P.op("dve", I("tensor_copy", out=B1T[:], in_=modT[:, 0:16]), reads=[t_modT], writes=[t_mod])
    P.op("dve", I("scalar_tensor_tensor", out=A2T[:], in0=modT[:, 64:80], scalar=1.0, in1=g2T[:],
                  op0=ALU.add, op1=ALU.mult), reads=[t_modT, t_c], writes=[t_mod])
    P.op("dve", I("tensor_copy", out=B2T[:], in_=modT[:, 48:64]), reads=[t_modT], writes=[t_mod])
    A.release(m0)

    def norm_supertile(env, xrows0, AT, BT):
        hT, hT_tok, _ = env["hT"].next()
        for tt in range(NT):
            xt, xtok, xsem = env["x"].next()
            r = xrows0 + tt * 128
            P.dma("sp", xt[:], xk[r:r + 128, :], xsem, writes=[xtok])
            st, sttok, _ = env["stat"].next()
            xn, xntok, _ = env["xn"].next()
            P.op("act", I("activation", out=xn[:], in_=xt[:], func=AF.Square, accum_out=st[:, 0:1]),
                 reads=[xtok], writes=[xntok, sttok])
            rstd_ops(st[:, 0:1], st[:, 1:2], st[:, 2:3], 1.0 / D, CV_ZERO, [sttok], sttok, sttok)
            P.op("dve", I("tensor_scalar", out=xn[:], in0=xt[:], scalar1=st[:, 2:3], scalar2=None, op0=ALU.mult),
                 reads=[xtok, sttok], writes=[xntok])
            for g in range(4):
                tp, tptok = env["tp"][g % 2]
                for j in range(4):
                    dc = g * 4 + j
                    P.op("pe", I("transpose", tp[:, j * 128:(j + 1) * 128], xn[:, dc * 128:(dc + 1) * 128], ident),
                         reads=[xntok, t_const], writes=[tptok], sig=(j == 3))
                for j in range(4):
                    dc = g * 4 + j
                    P.op("dve", I("tensor_scalar", out=hT[:, dc, tt * 128:(tt + 1) * 128], in0=tp[:, j * 128:(j + 1) * 128],
                                  scalar1=AT[:, dc:dc + 1], scalar2=BT[:, dc:dc + 1], op0=ALU.mult, op1=ALU.add),
                         reads=[tptok, t_mod], writes=[hT_tok])
        return hT, hT_tok

    def proj_fm(wt, wtok, c0, ncol, rhsT, rhs_tok, nk, out_bank, out_tok):
        for k in range(nk):
            P.op("pe", I("matmul", out_bank[0:ncol, 0:ST], lhsT=wt[:, k, c0:c0 + ncol], rhs=rhsT[:, k, :],
                         start=(k == 0), stop=(k == nk - 1)),
                 reads=[wtok, rhs_tok], writes=[out_tok], sig=(k == nk - 1))

    def make_env(after):
        env = {}
        env["x"] = Ring(P, A, "x", 2, [128, D], F32, after)
        env["stat"] = Ring(P, A, "stat", 2, [128, 4], F32, after, dma=False)
        env["xn"] = Ring(P, A, "xn", 2, [128, D], BF16, after, dma=False)
        env["hT"] = Ring(P, A, "hT", 2, [128, 16, ST], BF16, after, dma=False)
        env["tp"] = [(banks[0][:].bitcast(BF16)[:, 0:512], PTok("tp0", after)),
                     (banks[1][:].bitcast(BF16)[:, 0:512], PTok("tp1", after))]
        env["mm"] = [(banks[2 + i], PTok("mm%d" % i, after)) for i in range(3)]
        env["mmi"] = 0
        t_ssaux = PTok("ssaux", after)
        env["ss"] = (banks[5], t_ssaux)
        env["aux"] = (banks[5][:, 256:512], t_ssaux)
        env["rp"] = (banks[6], PTok("rp", after))
        env["rpp"] = (banks[7], PTok("rpp", after))
        return env

    def next_mm(env):
        b, t = env["mm"][env["mmi"] % 3]
        env["mmi"] += 1
        return b, t

    t_scr = {k: Tok(k) for k in ["KdT", "QdT", "Vd", "KmT", "QmnT", "QmrT", "Vm", "x1", "h2T"]}

    after = P.all_events()
    mA = A.mark()
    envA = make_env(after)
    wkv = A.alloc("wkv", [128, 16, 2432], BF16)
    wkvb = A.alloc("wkvb", [128, 2, 2048], BF16)
    t_wA = Tok("wA")
    s_wA = P.dsem("d_wA")
    dmy = Tok("dummyA", after)
    for i in range(4):
        P.dma("pool", wkv[:, i * 4:(i + 1) * 4, :], w_kv[i * 512:(i + 1) * 512, :].rearrange("(k p) n -> p k n", p=128),
              s_wA, acc_writes=[t_wA], reads=[dmy])
    P.dma("pool", wkvb[:], w_kvb.rearrange("(k p) n -> p k n", p=128), s_wA, acc_writes=[t_wA])
    tabk = Ring(P, A, "tabk", 2, [64, 2, ST], F32, after)
    kd_st = Ring(P, A, "kd_st", 2, [128, NH, ST], BF16, after)
    vd_st = Ring(P, A, "vd_st", 2, [128, NH, NT, 130], BF16, after)
    sqT = Ring(P, A, "sqT", 2, [128, ST], BF16, after, dma=False)
    csb = Ring(P, A, "csb", 1, [128, 2, ST], F32, after, dma=False)
    sqc = Ring(P, A, "sqc", 1, [128, 2, ST], BF16, after, dma=False)
    cn = Ring(P, A, "cn", 1, [128, 2, ST], BF16, after, dma=False)
    rbc = Ring(P, A, "rbc", 2, [128, 2, ST], F32, after, dma=False)
    sqpe = Ring(P, A, "sqpe", 1, [64, ST], BF16, after, dma=False)
    rt = Ring(P, A, "rt", 2, [64, 2, ST], F32, after, dma=False)
    auxs = Ring(P, A, "auxs", 2, [128, 2, 64], F32, after, dma=False)

    for i in range(2):
        t_, tok_ = vd_st.t[i], vd_st.tok[i]
        P.op("dve", I("memset", t_[:, :, :, 128:129], 1.0), writes=[tok_])
        P.op("dve", I("memset", t_[:, :, :, 129:130], 0.0), writes=[tok_])

    def v_tokmajor(env, lhs, lhs_tok, nk, wt, wtok, c0, dram, dram_tok, kt0):
        vst, vtok, vsem = vd_st.next()
        for tt in range(NT):
            for half in range(2):
                bk, btok = next_mm(env)
                for k in range(nk):
                    P.op("pe", I("matmul", bk[:], lhsT=lhs[:, k, tt * 128:(tt + 1) * 128],
                                 rhs=wt[:, k, c0 + half * 512:c0 + (half + 1) * 512], start=(k == 0), stop=(k == nk - 1)),
                         reads=[lhs_tok, wtok], writes=[btok], sig=(k == nk - 1))
                P.op("act", I("activation", out=vst[:, half * 4:(half + 1) * 4, tt, 0:128],
                              in_=bk[:].rearrange("p (h e) -> p h e", e=128), func=AF.Copy),
                     reads=[btok], writes=[vtok])
        P.dma("sp", dram[:, :, kt0:kt0 + NT, :].rearrange("h p k e -> p h k e"), vst[:], vsem,
              reads=[vtok], acc_writes=[dram_tok])

    if STOP_AFTER >= 1:
        for st_i in range(S // ST):
            tok0 = st_i * ST
            kt0 = st_i * NT
            hT, hTtok = norm_supertile(envA, tok0, A1T, B1T)
            axb, axtok = envA["aux"]
            tb, tbtok, tbsem = tabk.next()
            P.dma("sp", tb[:, 0, :], cos_d[:, tok0:tok0 + ST], tbsem, writes=[tbtok])
            ev = P.dma("sp", tb[:, 1, :], sin_d[:, tok0:tok0 + ST], tbsem, acc_writes=[tbtok])
            tbtok.w = {("dma", tbsem.h.num): ev}
            P.op("dve", I("tensor_scalar", out=tb[:, 0, :], in0=tb[:, 0, :], scalar1=gc(G_KMR, 64), scalar2=None, op0=ALU.mult),
                 reads=[t_const], writes=[tbtok])
            P.op("dve", I("tensor_scalar", out=tb[:, 1, :], in0=tb[:, 1, :], scalar1=gc(G_KMP, 64), scalar2=None, op0=ALU.mult),
                 reads=[t_const], writes=[tbtok])
            if SUB < 0 or st_i >= NSTL:
                continue
            kst, ksttok, kstsem = kd_st.next()
            for h in range(NHA):
                bk, btok = next_mm(envA)
                proj_fm(wkv, t_wA, h * 128, 128, hT, hTtok, 16, bk, btok)
                sq, sqtok, _ = sqT.next()
                if AP_MASK & 1:
                    P.op("act", I("activation", out=sq[:], in_=bk[:, 0:ST], func=AF.Square), reads=[btok], writes=[sqtok])
                if AP_MASK & 2:
                    P.op("dve", I("tensor_scalar", out=kst[:, h, :], in0=bk[:, 0:ST], scalar1=gc(G_KD), scalar2=None, op0=ALU.mult),
                         reads=[btok, t_const], writes=[ksttok])
                for tt in range(NT):
                    c = tt * 16 + h * 2
                    if AP_MASK & 4:
                        P.op("pe", I("matmul", axb[:, c:c + 2], lhsT=sq[:, tt * 128:(tt + 1) * 128], rhs=sel2,
                                     start=True, stop=True),
                             reads=[sqtok, t_const], writes=[axtok], sig=(tt == NT - 1))
            if AP_MASK & 8:
                P.dma("sp", KdT[:, :, tok0:tok0 + ST].rearrange("h p t -> p h t"), kst[:], kstsem,
                      reads=[ksttok], acc_writes=[t_scr["KdT"]])
            au, autok, _ = auxs.next()
            if AP_MASK & 16:
                rstd_ops(axb[:, 0:NT * 16], au[:, 0, 0:NT * 16], ksc_d[:, kt0:kt0 + NT, :].rearrange("p k c -> p (k c)"),
                         1.0 / 64, CV_LN8, [axtok], autok, t_ksc)
            if SUB < 1 or st_i >= NSTL:
                continue
            v_tokmajor(envA, hT, hTtok, 16, wkv, t_wA, 1024, Vd, t_scr["Vd"], kt0)
            if SUB < 2 or st_i >= NSTL:
                continue
            cs, cstok, _ = csb.next()
            sc_, sctok, _ = sqc.next()
            for c in range(2):
                bk, btok = next_mm(envA)
                proj_fm(wkv, t_wA, 2048 + c * 128, 128, hT, hTtok, 16, bk, btok)
                P.op("act", I("activation", out=sc_[:, c, :], in_=bk[:, 0:ST], func=AF.Square), reads=[btok], writes=[sctok])
                P.op("dve", I("tensor_copy", out=cs[:, c, :], in_=bk[:, 0:ST]), reads=[btok], writes=[cstok])
            ssb, sstok = envA["ss"]
            for c in range(2):
                P.op("pe", I("matmul", ssb[:, 0:ST], lhsT=ones, rhs=sc_[:, c, :], start=(c == 0), stop=(c == 1)),
                     reads=[sctok, t_const], writes=[sstok], sig=(c == 1))
            rb, rbtok, _ = rbc.next()
            rstd_ops(ssb[:, 0:ST], rb[:, 0, :], rb[:, 1, :], 1.0 / 256, CV_ZERO, [sstok], rbtok, rbtok)
            cnt, cntok, _ = cn.next()
            for c in range(2):
                P.op("dve", I("scalar_tensor_tensor", out=cnt[:, c, :], in0=cs[:, c, :], scalar=gc(G_KVA + c), in1=rb[:, 1, :],
                              op0=ALU.mult, op1=ALU.mult), reads=[cstok, rbtok, t_const], writes=[cntok])
            if SUB < 3 or st_i >= NSTL:
                continue
            rpb, rptok = envA["rp"]
            rppb, rpptok = envA["rpp"]
            proj_fm(wkv, t_wA, 2304, 64, hT, hTtok, 16, rpb, rptok)
            proj_fm(wkv, t_wA, 2368, 64, hT, hTtok, 16, rppb, rpptok)
            sp_, sptok, _ = sqpe.next()
            P.op("act", I("activation", out=sp_[:], in_=rpb[0:64, 0:ST], func=AF.Square), reads=[rptok], writes=[sptok])
            r_, rtok_, _ = rt.next()
            P.op("dve", I("tensor_tensor", out=r_[:, 0, :], in0=rpb[0:64, 0:ST], in1=tb[:, 0, :], op=ALU.mult),
                 reads=[rptok, tbtok], writes=[rtok_])
            P.op("dve", I("tensor_tensor", out=r_[:, 1, :], in0=rppb[0:64, 0:ST], in1=tb[:, 1, :], op=ALU.mult),
                 reads=[rpptok, tbtok], writes=[rtok_])
            P.op("dve", I("tensor_tensor", out=kropeT[:, tok0:tok0 + ST], in0=r_[:, 0, :], in1=r_[:, 1, :], op=ALU.add),
                 reads=[rtok_], writes=[t_krope])
            if SUB < 4 or st_i >= NSTL:
                continue
            kst, ksttok, kstsem = kd_st.next()
            for h in range(NH):
                bk, btok = next_mm(envA)
                proj_fm(wkvb, t_wA, h * 128, 128, cnt, cntok, 2, bk, btok)
                sq, sqtok, _ = sqT.next()
                P.op("act", I("activation", out=sq[:], in_=bk[:, 0:ST], func=AF.Square), reads=[btok], writes=[sqtok])
                P.op("dve", I("tensor_scalar", out=kst[:, h, :], in0=bk[:, 0:ST], scalar1=gc(G_KMN), scalar2=None, op0=ALU.mult),
                     reads=[btok, t_const], writes=[ksttok])
                for tt in range(NT):
                    c = 64 + tt * 8 + h
                    P.op("pe", I("matmul", axb[:, c:c + 1], lhsT=sq[:, tt * 128:(tt + 1) * 128], rhs=ones[:, 0:1],
                                 start=True, stop=False), reads=[sqtok, t_const], writes=[axtok], sig=False)
                    P.op("pe", I("matmul", axb[:, c:c + 1], lhsT=sp_[:, tt * 128:(tt + 1) * 128], rhs=ones[0:64, 0:1],
                                 start=False, stop=True), reads=[sptok, t_const], writes=[axtok], sig=(tt == NT - 1))
            P.dma("sp", KmT[:, :, tok0:tok0 + ST].rearrange("h p t -> p h t"), kst[:], kstsem,
                  reads=[ksttok], acc_writes=[t_scr["KmT"]])
            au, autok, _ = auxs.next()
            rstd_ops(axb[:, 64:64 + NT * 8], au[:, 0, 0:NT * 8], ksc_m[:, kt0:kt0 + NT, :].rearrange("p k c -> p (k c)"),
                     1.0 / 192, CV_LN192, [axtok], autok, t_ksc)
            if SUB < 5 or st_i >= NSTL:
                continue
            v_tokmajor(envA, cnt, cntok, 2, wkvb, t_wA, 1024, Vm, t_scr["Vm"], kt0)
    A.release(mA)

    after = P.all_events()
    mB = A.mark()
    envB = make_env(after)
    wq = A.alloc("wq", [128, 16, 1536], BF16)
    wqb = A.alloc("wqb", [128, 4, 2048], BF16)
    t_wB = Tok("wB")
    s_wB = P.dsem("d_wB")
    dmy = Tok("dummyB", after)
    for i in range(4):
        P.dma("pool", wq[:, i * 4:(i + 1) * 4, :], w_q[i * 512:(i + 1) * 512, :].rearrange("(k p) n -> p k n", p=128),
              s_wB, acc_writes=[t_wB], reads=[dmy])
    P.dma("pool", wqb[:], w_qb.rearrange("(k p) n -> p k n", p=128), s_wB, acc_writes=[t_wB])
    tabq = Ring(P, A, "tabq", 2, [64, 2, ST], F32, after)
    qd_st = Ring(P, A, "qd_st", 2, [128, NH, ST], BF16, after)
    qr_st = Ring(P, A, "qr_st", 2, [64, NH, ST], BF16, after)
    sqTb = Ring(P, A, "sqTb", 2, [128, ST], BF16, after, dma=False)
    sqRb = Ring(P, A, "sqRb", 2, [64, ST], BF16, after, dma=False)
    csbB = Ring(P, A, "csbB", 1, [128, 4, ST], F32, after, dma=False)
    sqcB = Ring(P, A, "sqcB", 1, [128, 4, ST], BF16, after, dma=False)
    cnB = Ring(P, A, "cnB", 1, [128, 4, ST], BF16, after, dma=False)
    rbcB = Ring(P, A, "rbcB", 2, [128, 2, ST], F32, after, dma=False)
    rtB = Ring(P, A, "rtB", 2, [64, 2, ST], F32, after, dma=False)

    if STOP_AFTER >= 2:
        for st_i in range(SQ // ST):
            if st_i >= NSTL:
                continue
            tok0 = st_i * ST
            hT, hTtok = norm_supertile(envB, tok0, A1T, B1T)
            ssb, sstok = envB["ss"]
            tb, tbtok, tbsem = tabq.next()
            P.dma("sp", tb[:, 0, :], cos_d[:, tok0:tok0 + ST], tbsem, writes=[tbtok])
            ev = P.dma("sp", tb[:, 1, :], sin_d[:, tok0:tok0 + ST], tbsem, acc_writes=[tbtok])
            tbtok.w = {("dma", tbsem.h.num): ev}
            P.op("dve", I("tensor_scalar", out=tb[:, 0, :], in0=tb[:, 0, :], scalar1=gc(G_QMR, 64), scalar2=None, op0=ALU.mult),
                 reads=[t_const], writes=[tbtok])
            P.op("dve", I("tensor_scalar", out=tb[:, 1, :], in0=tb[:, 1, :], scalar1=gc(G_QMP, 64), scalar2=None, op0=ALU.mult),
                 reads=[t_const], writes=[tbtok])
            qst, qsttok, qstsem = qd_st.next()
            for h in range(NH):
                bk, btok = next_mm(envB)
                proj_fm(wq, t_wB, h * 128, 128, hT, hTtok, 16, bk, btok)
                sq, sqtok, _ = sqTb.next()
                P.op("act", I("activation", out=sq[:], in_=bk[:, 0:ST], func=AF.Square), reads=[btok], writes=[sqtok])
                P.op("pe", I("matmul", ssb[:, 0:ST], lhsT=bones, rhs=sq[:], start=True, stop=True),
                     reads=[sqtok, t_const], writes=[sstok])
                rb, rbtok, _ = rbcB.next()
                rstd_ops(ssb[:, 0:ST], rb[:, 0, :], rb[:, 1, :], 1.0 / 64, CV_ZERO, [sstok], rbtok, rbtok)
                P.op("dve", I("scalar_tensor_tensor", out=qst[:, h, :], in0=bk[:, 0:ST], scalar=gc(G_QD), in1=rb[:, 1, :],
                              op0=ALU.mult, op1=ALU.mult), reads=[btok, rbtok, t_const], writes=[qsttok])
            P.dma("sp", QdT[:, :, tok0:tok0 + ST].rearrange("h p t -> p h t"), qst[:], qstsem,
                  reads=[qsttok], acc_writes=[t_scr["QdT"]])
            cs, cstok, _ = csbB.next()
            sc_, sctok, _ = sqcB.next()
            for c in range(4):
                bk, btok = next_mm(envB)
                proj_fm(wq, t_wB, 1024 + c * 128, 128, hT, hTtok, 16, bk, btok)
                P.op("act", I("activation", out=sc_[:, c, :], in_=bk[:, 0:ST], func=AF.Square), reads=[btok], writes=[sctok])
                P.op("dve", I("tensor_copy", out=cs[:, c, :], in_=bk[:, 0:ST]), reads=[btok], writes=[cstok])
            for c in range(4):
                P.op("pe", I("matmul", ssb[:, 0:ST], lhsT=ones, rhs=sc_[:, c, :], start=(c == 0), stop=(c == 3)),
                     reads=[sctok, t_const], writes=[sstok], sig=(c == 3))
            rb, rbtok, _ = rbcB.next()
            rstd_ops(ssb[:, 0:ST], rb[:, 0, :], rb[:, 1, :], 1.0 / 512, CV_ZERO, [sstok], rbtok, rbtok)
            cnt, cntok, _ = cnB.next()
            for c in range(4):
                P.op("dve", I("scalar_tensor_tensor", out=cnt[:, c, :], in0=cs[:, c, :], scalar=gc(G_QA + c), in1=rb[:, 1, :],
                              op0=ALU.mult, op1=ALU.mult), reads=[cstok, rbtok, t_const], writes=[cntok])
            qst, qsttok, qstsem = qd_st.next()
            qrs, qrstok, qrssem = qr_st.next()
            rpb, rptok = envB["rp"]
            rppb, rpptok = envB["rpp"]
            for h in range(NH):
                bk, btok = next_mm(envB)
                proj_fm(wqb, t_wB, h * 128, 128, cnt, cntok, 4, bk, btok)
                proj_fm(wqb, t_wB, 1024 + h * 64, 64, cnt, cntok, 4, rpb, rptok)
                proj_fm(wqb, t_wB, 1536 + h * 64, 64, cnt, cntok, 4, rppb, rpptok)
                sq, sqtok, _ = sqTb.next()
                sqr, sqrtok, _ = sqRb.next()
                P.op("act", I("activation", out=sq[:], in_=bk[:, 0:ST], func=AF.Square), reads=[btok], writes=[sqtok])
                P.op("act", I("activation", out=sqr[:], in_=rpb[0:64, 0:ST], func=AF.Square), reads=[rptok], writes=[sqrtok])
                P.op("pe", I("matmul", ssb[:, 0:ST], lhsT=ones, rhs=sq[:], start=True, stop=False),
                     reads=[sqtok, t_const], writes=[sstok], sig=False)
                P.op("pe", I("matmul", ssb[:, 0:ST], lhsT=ones[0:64, :], rhs=sqr[:], start=False, stop=True),
                     reads=[sqrtok, t_const], writes=[sstok])
                rb, rbtok, _ = rbcB.next()
                rstd_ops(ssb[:, 0:ST], rb[:, 0, :], rb[:, 1, :], 1.0 / 192, CV_ZERO, [sstok], rbtok, rbtok)
                P.op("dve", I("scalar_tensor_tensor", out=qst[:, h, :], in0=bk[:, 0:ST], scalar=gc(G_QMN), in1=rb[:, 1, :],
                              op0=ALU.mult, op1=ALU.mult), reads=[btok, rbtok, t_const], writes=[qsttok])
                r_, rtok_, _ = rtB.next()
                P.op("dve", I("tensor_tensor", out=r_[:, 0, :], in0=rpb[0:64, 0:ST], in1=tb[:, 0, :], op=ALU.mult),
                     reads=[rptok, tbtok], writes=[rtok_])
                P.op("dve", I("tensor_tensor", out=r_[:, 1, :], in0=rppb[0:64, 0:ST], in1=tb[:, 1, :], op=ALU.mult),
                     reads=[rpptok, tbtok], writes=[rtok_])
                P.op("dve", I("tensor_tensor", out=r_[:, 0, :], in0=r_[:, 0, :], in1=r_[:, 1, :], op=ALU.add),
                     reads=[], writes=[rtok_])
                P.op("dve", I("tensor_tensor", out=qrs[:, h, :], in0=r_[:, 0, :], in1=rb[0:64, 1, :], op=ALU.mult),
                     reads=[rtok_, rbtok], writes=[qrstok])
            P.dma("sp", QmnT[:, :, tok0:tok0 + ST].rearrange("h p t -> p h t"), qst[:], qstsem,
                  reads=[qsttok], acc_writes=[t_scr["QmnT"]])
            P.dma("sp", QmrT[:, :, tok0:tok0 + ST].rearrange("h p t -> p h t"), qrs[:], qrssem,
                  reads=[qrstok], acc_writes=[t_scr["QmrT"]])
    A.release(mB)

    if DEBUG and NSTL >= 99 and SUB >= 99 and STOP_AFTER >= 1:
        s_dbg = P.dsem("d_dbg")
        P.dma("sp", dbg_ksc[:, 0:512], ksc_d[:].rearrange("p k c -> p (k c)"), s_dbg, reads=[t_ksc])
        P.dma("sp", dbg_ksc[:, 512:768], ksc_m[:].rearrange("p k c -> p (k c)"), s_dbg, reads=[t_ksc])

    after = P.all_events()
    m_mix = A.mark()
    mix = A.alloc("mix", [128, 16, D], BF16)
    t_mix = [Tok("mix%d" % i, after) for i in range(16)]
    m2 = A.mark()
    gtab = A.alloc("gtab", [128, NH, GW], F32)
    gfar = A.alloc("gfar", [128, 16], F32)
    gsub = A.alloc("gsub", [128, 128], F32)
    lam = A.alloc("lam", [128, 256], F32)
    lamw = A.alloc("lamw", [128, 8], F32)
    t_g = Tok("gtab")
    s_g = P.dsem("d_g")
    dmy = Tok("dummy2", after)
    P.dma("sp", gtab[:], gtab_d, s_g, acc_writes=[t_g], reads=[dmy])
    P.dma("sp", gfar[:], gfar_d, s_g, acc_writes=[t_g])
    P.dma("sp", gsub[:], gsub_d, s_g, acc_writes=[t_g])
    P.dma("sp", lam[:], lam_d, s_g, acc_writes=[t_g])
    t_lam = Tok("lam")
    P.op("dve", I("tensor_scalar", out=gfar[:], in0=gfar[:], scalar1=-C_DIFF, scalar2=None, op0=ALU.add),
         reads=[t_g], writes=[t_lam])
    P.op("dve", I("scalar_tensor_tensor", out=lam[:, 0:64], in0=lam[:, 0:64], scalar=1.0, in1=lam[:, 64:128],
                  op0=ALU.mult, op1=ALU.mult, accum_out=lamw[:, 0:1]), reads=[t_g], writes=[t_lam])
    P.op("dve", I("scalar_tensor_tensor", out=lam[:, 128:192], in0=lam[:, 128:192], scalar=1.0, in1=lam[:, 192:256],
                  op0=ALU.mult, op1=ALU.mult, accum_out=lamw[:, 1:2]), reads=[t_g], writes=[t_lam])
    P.op("act", I("activation", out=lamw[:, 2:4], in_=lamw[:, 0:2], func=AF.Exp), reads=[t_lam], writes=[t_lam])
    P.op("dve", I("tensor_tensor", out=lamw[:, 4:5], in0=lamw[:, 3:4], in1=lamw[:, 2:3], op=ALU.subtract),
         reads=[], writes=[t_lam])
    P.op("dve", I("tensor_scalar", out=lamw[:, 5:6], in0=lamw[:, 4:5], scalar1=-LAMBDA_INIT, scalar2=None, op0=ALU.add),
         reads=[], writes=[t_lam])
    NEGLAM = lamw[:, 5:6]

    Kb = Ring(P, A, "Kb", 2, [128, S], BF16, after)
    Qb = Ring(P, A, "Qb", 2, [128, SQ], BF16, after, dma=False)
    Qrb = Ring(P, A, "Qrb", 2, [64, SQ], BF16, after, dma=False)
    Vb = Ring(P, A, "Vb", 2, [128, 32, 130], BF16, after, dma=False)
    PT = Ring(P, A, "PT", 3, [128, 512], BF16, after, dma=False)
    btmp = Ring(P, A, "btmp", 2, [128, 512], F32, after, dma=False)
    o1n = Ring(P, A, "o1n", 2, [128, 4, 128], F32, after, dma=False)
    cmb = Ring(P, A, "cmb", 2, [128, 128], F32, after, dma=False)
    jk2 = Ring(P, A, "jk2", 2, [128, 128], F32, after, dma=False)
    sm = Ring(P, A, "sm", 4, [128, 8], F32, after, dma=False)
    Sb = [(banks[i], PTok("S%d" % i, after)) for i in range(3)]
    Ob = [(banks[3 + i], PTok("O%d" % i, after)) for i in range(4)]
    s_i = [0]

    def attn_qc(kind, h, mp, qc, K, Ktok, Q, Qtok, Qr, Qrtok, V, Vtok):
        r0 = mp * 64
        for kt in range(32):
            sb_, stok = Sb[s_i[0] % 3]
            s_i[0] += 1
            if kind == "d":
                P.op("pe", I("matmul", sb_[:], lhsT=K[r0:r0 + 64, kt * 128:(kt + 1) * 128],
                             rhs=Q[r0:r0 + 64, qc * 512:(qc + 1) * 512], start=True, stop=True),
                     reads=[Ktok, Qtok], writes=[stok])
            else:
                P.op("pe", I("matmul", sb_[:], lhsT=K[:, kt * 128:(kt + 1) * 128], rhs=Q[:, qc * 512:(qc + 1) * 512],
                             start=True, stop=False), reads=[Ktok, Qtok], writes=[stok], sig=False)
                P.op("pe", I("matmul", sb_[:], lhsT=kropeT[:, kt * 128:(kt + 1) * 128], rhs=Qr[:, qc * 512:(qc + 1) * 512],
                             start=False, stop=True), reads=[t_krope, Qrtok], writes=[stok])
            pt, pttok, _ = PT.next()
            if kind == "d":
                scale_ap = ksc_d[:, kt, h * 2 + mp:h * 2 + mp + 1]
                m = kt - 4 * qc
                if -1 <= m <= 4:
                    bt, bttok, _ = btmp.next()
                    g0 = (4 - m) * 128
                    P.op("dve", I("scalar_tensor_tensor", out=bt[:], in0=sb_[:], scalar=scale_ap, in1=gtab[:, h, g0:g0 + 512],
                                  op0=ALU.mult, op1=ALU.add), reads=[stok, t_ksc, t_g], writes=[bttok])
                    P.op("act", I("activation", out=pt[:], in_=bt[:], func=AF.Exp, bias=cvc(CV_NCD), scale=1.0),
                         reads=[bttok, t_cv], writes=[pttok])
                else:
                    side = 0 if m < -1 else 1
                    P.op("act", I("activation", out=pt[:], in_=sb_[:], func=AF.Exp,
                                  bias=gfar[:, h * 2 + side:h * 2 + side + 1], scale=scale_ap),
                         reads=[stok, t_ksc, t_lam], writes=[pttok])
            else:
                scale_ap = ksc_m[:, kt, h:h + 1]
                P.op("act", I("activation", out=pt[:], in_=sb_[:], func=AF.Exp, bias=cvc(CV_NCM), scale=scale_ap),
                     reads=[stok, t_ksc, t_cv], writes=[pttok])
            for j in range(4):
                ob, otok = Ob[j]
                P.op("pe", I("matmul", ob[:, 0:130], lhsT=pt[:, j * 128:(j + 1) * 128], rhs=V[:, kt, :],
                             start=(kt == 0), stop=(kt == 31)),
                     reads=[pttok, Vtok], writes=[otok], sig=(j == 3))

    def load_head(srcs):
        res = []
        sem = None
        ev = None
        for ring, dram_ap, scr_tok in srcs:
            t_, tok_, sem_ = ring.next()
            if sem is None:
                sem = sem_
            ev = P.dma("sp", t_[:], dram_ap, sem, reads=[scr_tok], writes=[tok_])
            res.append((t_, tok_))
        for _, tok_ in res:
            tok_.w = {("dma", sem.h.num): ev}
        return res

    if STOP_AFTER >= 3:
        for h in range(NH):
            (K, Ktok), (Q, Qtok), (V, Vtok) = load_head([(Kb, KdT[h], t_scr["KdT"]), (Qb, QdT[h], t_scr["QdT"]),
                                                         (Vb, Vd[h], t_scr["Vd"])])
            for qc in range(4):
                o1, o1tok, _ = o1n.next()
                for mp in range(2):
                    attn_qc("d", h, mp, qc, K, Ktok, Q, Qtok, None, None, V, Vtok)
                    for j in range(4):
                        ob, otok = Ob[j]
                        s_, stok_, _ = sm.next()
                        P.op("dve", I("reciprocal", out=s_[:, 0:1], in_=ob[:, 128:129]), reads=[otok], writes=[stok_])
                        if mp == 0:
                            P.op("dve", I("tensor_scalar", out=o1[:, j, :], in0=ob[:, 0:128], scalar1=s_[:, 0:1], scalar2=None,
                                          op0=ALU.mult), reads=[otok, stok_], writes=[o1tok])
                        else:
                            cm, cmtok, _ = cmb.next()
                            jk, jktok, _ = jk2.next()
                            tt = qc * 4 + j
                            P.op("dve", I("tensor_scalar", out=cm[:], in0=ob[:, 0:128], scalar1=s_[:, 0:1], scalar2=None,
                                          op0=ALU.mult), reads=[otok, stok_], writes=[cmtok])
                            P.op("dve", I("scalar_tensor_tensor", out=cm[:], in0=cm[:], scalar=NEGLAM, in1=o1[:, j, :],
                                          op0=ALU.mult, op1=ALU.add), reads=[o1tok, t_lam], writes=[cmtok])
                            P.op("dve", I("scalar_tensor_tensor", out=jk[:], in0=cm[:], scalar=1.0, in1=cm[:],
                                          op0=ALU.mult, op1=ALU.mult, accum_out=s_[:, 1:2]),
                                 reads=[cmtok], writes=[jktok, stok_])
                            rstd_ops(s_[:, 1:2], s_[:, 2:3], s_[:, 3:4], 1.0 / 128, CV_LN08, [stok_], stok_, stok_)
                            P.op("dve", I("scalar_tensor_tensor", out=mix[:, tt, h * 128:(h + 1) * 128], in0=cm[:],
                                          scalar=s_[:, 3:4], in1=gsub[:], op0=ALU.mult, op1=ALU.mult),
                                 reads=[cmtok, stok_, t_g], writes=[t_mix[tt]])
        for h in range(NH):
            (K, Ktok), (Q, Qtok), (Qr, Qrtok), (V, Vtok) = load_head(
                [(Kb, KmT[h], t_scr["KmT"]), (Qb, QmnT[h], t_scr["QmnT"]), (Qrb, QmrT[h], t_scr["QmrT"]),
                 (Vb, Vm[h], t_scr["Vm"])])
            for qc in range(4):
                attn_qc("m", h, 0, qc, K, Ktok, Q, Qtok, Qr, Qrtok, V, Vtok)
                for j in range(4):
                    ob, otok = Ob[j]
                    s_, stok_, _ = sm.next()
                    tt = qc * 4 + j
                    P.op("dve", I("reciprocal", out=s_[:, 0:1], in_=ob[:, 128:129]), reads=[otok], writes=[stok_])
                    P.op("dve", I("tensor_scalar", out=mix[:, tt, 1024 + h * 128:1024 + (h + 1) * 128], in0=ob[:, 0:128],
                                  scalar1=s_[:, 0:1], scalar2=None, op0=ALU.mult),
                         reads=[otok, stok_], writes=[t_mix[tt]])
    A.release(m2)

    after = P.all_events()
    m3 = A.mark()
    wo = A.alloc("wo", [128, 16, D], BF16)
    gt1 = A.alloc("gt1", [128, D], F32)
    t_wo = Tok("wo")
    s_wo = P.dsem("d_wo")
    dmy = Tok("dummy3", after)
    for i in range(4):
        P.dma("pool", wo[:, i * 4:(i + 1) * 4, :], w_out[i * 512:(i + 1) * 512, :].rearrange("(k p) n -> p k n", p=128),
              s_wo, acc_writes=[t_wo], reads=[dmy])
    s_gt1 = P.dsem("d_gt1")
    P.dma("sp", gt1[:], mod_scr[:, 2 * D:3 * D].partition_broadcast(128), s_gt1, reads=[t_modscr, dmy], acc_writes=[t_wo])
    x3 = Ring(P, A, "x3", 1, [128, D], F32, after)
    x1r = Ring(P, A, "x1r", 2, [128, D], F32, after)
    mixT = Ring(P, A, "mixT", 2, [128, 16, 128], BF16, after, dma=False)
    st3 = Ring(P, A, "st3", 2, [128, 4], F32, after, dma=False)
    xn3 = Ring(P, A, "xn3", 2, [128, D], BF16, after, dma=False)
    h2s = Ring(P, A, "h2s", 2, [128, 16, 128], BF16, after)
    tp3 = [(banks[0][:].bitcast(BF16)[:, 0:512], PTok("tp3a", after)),
           (banks[5][:].bitcast(BF16)[:, 0:512], PTok("tp3b", after))]
    yb = [(banks[1 + i], PTok("y%d" % i, after)) for i in range(4)]

    if STOP_AFTER >= 4:
        for tt in range(16):
            mt, mttok, _ = mixT.next()
            for g in range(4):
                tp, tptok = tp3[g % 2]
                for j in range(4):
                    fc = g * 4 + j
                    P.op("pe", I("transpose", tp[:, j * 128:(j + 1) * 128], mix[:, tt, fc * 128:(fc + 1) * 128], ident),
                         reads=[t_mix[tt], t_const], writes=[tptok], sig=(j == 3))
                P.op("act", I("activation", out=mt[:, g * 4:(g + 1) * 4, :], in_=tp.rearrange("p (a b) -> p a b", b=128),
                              func=AF.Copy), reads=[tptok], writes=[mttok])
            xt, xtok, xsem = x3.next()
            P.dma("sp", xt[:], xk[tt * 128:(tt + 1) * 128, :], xsem, writes=[xtok])
            x1, x1tok, x1sem = x1r.next()
            for cb in range(4):
                yk, ytok = yb[cb]
                for fc in range(16):
                    P.op("pe", I("matmul", yk[:], lhsT=mt[:, fc, :], rhs=wo[:, fc, cb * 512:(cb + 1) * 512],
                                 start=(fc == 0), stop=(fc == 15)),
                         reads=[mttok, t_wo], writes=[ytok], sig=(fc == 15))
                P.op("dve", I("tensor_tensor", out=x1[:, cb * 512:(cb + 1) * 512], in0=yk[:], in1=gt1[:, cb * 512:(cb + 1) * 512],
                              op=ALU.mult), reads=[ytok, t_wo], writes=[x1tok])
                P.op("dve", I("tensor_tensor", out=x1[:, cb * 512:(cb + 1) * 512], in0=x1[:, cb * 512:(cb + 1) * 512],
                              in1=xt[:, cb * 512:(cb + 1) * 512], op=ALU.add), reads=[xtok], writes=[x1tok])
            P.dma("sp", x1_scr[tt * 128:(tt + 1) * 128, :], x1[:], x1sem, reads=[x1tok], acc_writes=[t_scr["x1"]])
            st, sttok, _ = st3.next()
            xn, xntok, _ = xn3.next()
            P.op("act", I("activation", out=xn[:], in_=x1[:], func=AF.Square, accum_out=st[:, 0:1]),
                 reads=[x1tok], writes=[xntok, sttok])
            rstd_ops(st[:, 0:1], st[:, 1:2], st[:, 2:3], 1.0 / D, CV_ZERO, [sttok], sttok, sttok)
            P.op("dve", I("tensor_scalar", out=xn[:], in0=x1[:], scalar1=st[:, 2:3], scalar2=None, op0=ALU.mult),
                 reads=[x1tok, sttok], writes=[xntok])
            hs, hstok, hssem = h2s.next()
            for g in range(4):
                tp, tptok = tp3[g % 2]
                for j in range(4):
                    dc = g * 4 + j
                    P.op("pe", I("transpose", tp[:, j * 128:(j + 1) * 128], xn[:, dc * 128:(dc + 1) * 128], ident),
                         reads=[xntok, t_const], writes=[tptok], sig=(j == 3))
                for j in range(4):
                    dc = g * 4 + j
                    P.op("dve", I("tensor_scalar", out=hs[:, dc, :], in0=tp[:, j * 128:(j + 1) * 128],
                                  scalar1=A2T[:, dc:dc + 1], scalar2=B2T[:, dc:dc + 1], op0=ALU.mult, op1=ALU.add),
                         reads=[tptok, t_mod], writes=[hstok])
            P.dma("sp", h2T_scr[:, :, tt * 128:(tt + 1) * 128], hs[:], hssem, reads=[hstok], acc_writes=[t_scr["h2T"]])
    A.release(m_mix)

    after = P.all_events()
    m4 = A.mark()
    T = FFN_T
    NTH = T // 512
    FB = 256
    gt2 = A.alloc("gt2", [128, D], F32)
    t_gt2 = Tok("gt2")
    s_gt2 = P.dsem("d_gt2")
    dmy = Tok("dummy4", after)
    P.dma("sp", gt2[:], mod_scr[:, 5 * D:6 * D].partition_broadcast(128), s_gt2, reads=[t_modscr, dmy], writes=[t_gt2])
    actT = A.alloc("actT", [128, NFC, T], BF16)
    t_act = [Tok("act%d" % i, after) for i in range(NFC)]
    h2b = Ring(P, A, "h2b", 1, [128, 16, T], BF16, after)
    wg = Ring(P, A, "wg", 2, [128, 16, FB], BF16, after)
    wu = Ring(P, A, "wu", 2, [128, 16, FB], BF16, after)
    wd = Ring(P, A, "wd", 6, [128, 11, 512], BF16, after)
    sg = Ring(P, A, "sg", 2, [128, 512], F32, after, dma=False)
    x1p = Ring(P, A, "x1p", 2, [128, 512], F32, after)
    ost = Ring(P, A, "ost", 2, [128, 512], F32, after)
    gb = [(banks[i], PTok("g%d" % i, after)) for i in range(8)]
    gi = [0]

    if STOP_AFTER >= 5:
        for blk in range(SQ // T):
            hb, hbtok, hbsem = h2b.next()
            P.dma("sp", hb[:], h2T_scr[:, :, blk * T:(blk + 1) * T], hbsem, reads=[t_scr["h2T"]], writes=[hbtok])
            for fcb in range(DFF // FB):
                wgt, wgtok, wgsem = wg.next()
                wut, wutok, wusem = wu.next()
                P.dma("pool", wgt[:], w_gate[:, fcb * FB:(fcb + 1) * FB].rearrange("(k p) n -> p k n", p=128),
                      wgsem, writes=[wgtok])
                P.dma("pool", wut[:], w_up[:, fcb * FB:(fcb + 1) * FB].rearrange("(k p) n -> p k n", p=128),
                      wusem, writes=[wutok])
                for fi in range(FB // 128):
                    f = fcb * (FB // 128) + fi
                    for th in range(NTH):
                        gk, gtok = gb[gi[0] % 8]
                        uk, utok = gb[(gi[0] + 1) % 8]
                        gi[0] += 2
                        for (bk_, btok_, wt_, wtok_) in ((gk, gtok, wgt, wgtok), (uk, utok, wut, wutok)):
                            for dc in range(16):
                                P.op("pe", I("matmul", bk_[:], lhsT=wt_[:, dc, fi * 128:(fi + 1) * 128],
                                             rhs=hb[:, dc, th * 512:(th + 1) * 512], start=(dc == 0), stop=(dc == 15)),
                                     reads=[wtok_, hbtok], writes=[btok_], sig=(dc == 15))
                        s_, stok_, _ = sg.next()
                        P.op("act", I("activation", out=s_[:], in_=gk[:], func=AF.Silu), reads=[gtok], writes=[stok_])
                        P.op("dve", I("tensor_tensor", out=actT[:, f, th * 512:(th + 1) * 512], in0=s_[:], in1=uk[:],
                                      op=ALU.mult), reads=[stok_, utok], writes=[t_act[f]])
            for cb in range(4):
                pieces = []
                for q4 in range(4):
                    wdt, wdtok, wdsem = wd.next()
                    P.dma("pool", wdt[:],
                          w_down[q4 * 1408:(q4 + 1) * 1408, cb * 512:(cb + 1) * 512].rearrange("(k p) n -> p k n", p=128),
                          wdsem, writes=[wdtok])
                    pieces.append((wdt, wdtok))
                for tt in range(T // 128):
                    row0 = blk * T + tt * 128
                    yk, ytok = gb[gi[0] % 8]
                    gi[0] += 1
                    for f in range(NFC):
                        wdt, wdtok = pieces[f // 11]
                        P.op("pe", I("matmul", yk[:], lhsT=actT[:, f, tt * 128:(tt + 1) * 128], rhs=wdt[:, f % 11, :],
                                     start=(f == 0), stop=(f == NFC - 1)),
                             reads=[t_act[f], wdtok], writes=[ytok], sig=(f == NFC - 1 or f % 11 == 10))
                    xp, xptok, xpsem = x1p.next()
                    P.dma("sp", xp[:], x1_scr[row0:row0 + 128, cb * 512:(cb + 1) * 512], xpsem,
                          reads=[t_scr["x1"]], writes=[xptok])
                    os_, ostok, ossem = ost.next()
                    P.op("dve", I("tensor_tensor", out=os_[:], in0=yk[:], in1=gt2[:, cb * 512:(cb + 1) * 512], op=ALU.mult),
                         reads=[ytok, t_gt2], writes=[ostok])
                    P.op("dve", I("tensor_tensor", out=os_[:], in0=os_[:], in1=xp[:], op=ALU.add),
                         reads=[xptok], writes=[ostok])
                    P.dma("sp", out_d[row0:row0 + 128, cb * 512:(cb + 1) * 512], os_[:], ossem, reads=[ostok])
    A.release(m4)

    P.final_wait("sp")
    P.emit()
    print("[kernel] sbuf peak %d / %d, waits %d, ops %s" % (
        A.peak, A.hi, P.n_waits, {k: len(v.ops) for k, v in P.E.items()}), flush=True)
    return nc


def _t5_bucket_np(rel):
    nb, me = 16, 8
    base = np.where(rel > 0, nb, 0)
    n = np.abs(rel)
    nf = np.maximum(n, 1).astype(np.float32)
    large = me + (np.log(nf / np.float32(me)) / np.float32(math.log(128 / 8)) * np.float32(nb - me)).astype(np.int32)
    large = np.minimum(large, nb - 1)
    return (base + np.where(n < me, n, large)).astype(np.int64)


def _prep_inputs(inp):
    f = lambda a: np.ascontiguousarray(np.asarray(a, dtype=np.float32))
    x = f(inp["x"]); c = f(inp["c"]); rel_bias = f(inp["rel_bias"])
    w_in = f(inp["w_in"])[0]
    w_q_b = f(inp["w_q_b"])[0]
    w_kv_b = f(inp["w_kv_b"])[0]
    g_q_mla = f(inp["g_q_mla"])[0]; g_k_mla = f(inp["g_k_mla"])[0]
    g_q_a = f(inp["g_q_a"])[0]; g_kv_a = f(inp["g_kv_a"])[0]
    q_d = w_in[:, 0:1024]; k_d = w_in[:, 1024:2048]; v_d = w_in[:, 2048:3072]
    cq = w_in[:, 3072:3584]; ckv = w_in[:, 3584:3840]; kpe = w_in[:, 3840:3904]
    perm = (np.arange(64) + 32) % 64
    w_kv = f(np.concatenate([k_d, v_d, ckv, kpe, kpe[:, perm]], axis=1))
    w_q = f(np.concatenate([q_d, cq], axis=1))
    qb = w_q_b.reshape(512, NH, 192)
    w_qb = f(np.concatenate([qb[:, :, 0:128].reshape(512, -1), qb[:, :, 128:192].reshape(512, -1),
                             qb[:, :, 128:192][:, :, perm].reshape(512, -1)], axis=1))
    kvb = w_kv_b.reshape(256, NH, 256)
    w_kvb = f(np.concatenate([kvb[:, :, 0:128].reshape(256, -1), kvb[:, :, 128:256].reshape(256, -1)], axis=1))
    gcols = np.ones((128, 16), np.float32)
    gcols[:, 0] = np.tile(f(inp["g_q_diff"])[0], 2)
    gcols[:, 1] = np.tile(f(inp["g_k_diff"])[0], 2)
    gcols[:, 2:6] = g_q_a.reshape(4, 128).T
    gcols[:, 6:8] = g_kv_a.reshape(2, 128).T
    gcols[:, 8] = g_q_mla[0:128]
    gcols[0:64, 9] = g_q_mla[128:192]
    gcols[0:64, 10] = g_q_mla[128:192][perm]
    gcols[:, 11] = g_k_mla[0:128]
    gcols[0:64, 12] = g_k_mla[128:192]
    gcols[0:64, 13] = g_k_mla[128:192][perm]
    gsub_bc = f(np.broadcast_to(f(inp["g_subln"])[0][None, :], (128, 128)))
    lam_bc = f(np.broadcast_to(f(inp["lambda_vecs"])[0].reshape(1, 256), (128, 256)))
    cmat = np.zeros((128, 512), np.float32)
    cmat[:, 0:128] = np.eye(128)
    cmat[:, 128:256] = 1.0
    cmat[0:64, 256:320] = 1.0
    cmat[64:128, 320:384] = 1.0
    cmat[0:64, 384] = 1.0
    cmat[64:128, 385] = 1.0
    inv = (1.0 / (np.float32(10000.0) ** (np.arange(0, 64, 2, dtype=np.float32) / np.float32(64)))).astype(np.float32)
    shared = dict(
        w_ada=f(inp["w_ada"])[0], b_ada=f(inp["b_ada"]).reshape(1, -1),
        g1T=f(f(inp["g_norm1"])[0].reshape(16, 128).T), g2T=f(f(inp["g_norm2"])[0].reshape(16, 128).T),
        w_kv=w_kv, w_q=w_q, w_qb=w_qb, w_kvb=w_kvb,
        w_out=f(inp["w_out"])[0], w_gate=f(inp["w_gate"])[0], w_up=f(inp["w_up"])[0], w_down=f(inp["w_down"])[0],
        gcols=gcols, gsub_bc=gsub_bc, lam_bc=lam_bc, cmat=cmat,
    )
    maps = []
    ii = np.arange(128)[:, None]
    tt = np.arange(GW)[None, :]
    for core in range(8):
        b, half = core // 2, core % 2
        xb = x[b]
        pos = np.arange(S, dtype=np.float32)
        if half == 1:
            xb = xb[::-1]
            pos = pos[::-1]
        ang = (pos[:, None] * inv[None, :]).astype(np.float32)
        cos = np.cos(ang).astype(np.float32).T
        sin = np.sin(ang).astype(np.float32).T
        cos_t = f(np.concatenate([cos, cos], axis=0))
        sin_t = f(np.concatenate([-sin, sin], axis=0))
        sgn = 1 if half == 0 else -1
        dloc = ii - tt + 512
        bidx = _t5_bucket_np((sgn * dloc).astype(np.int32))
        gtab = f(np.transpose(rel_bias[bidx], (0, 2, 1)))
        bneg = _t5_bucket_np(np.array([sgn * -1000], np.int32))[0]
        bpos = _t5_bucket_np(np.array([sgn * 1000], np.int32))[0]
        gfar = np.zeros((128, 16), np.float32)
        gfar[:, 0::2] = rel_bias[bneg][None, :]
        gfar[:, 1::2] = rel_bias[bpos][None, :]
        m = dict(shared)
        m.update(xk=f(xb), cT=f(c[b].reshape(16, 128).T), cos_t=cos_t, sin_t=sin_t, gtab=gtab, gfar=gfar)
        maps.append(m)
    return maps


_NC_CACHE = {}


def kernel(**inputs):
    maps = _prep_inputs(inputs)
    if "nc" not in _NC_CACHE:
        _NC_CACHE["nc"] = build_program()
    nc = _NC_CACHE["nc"]
    res = run_bass_kernel_spmd(nc, maps, core_ids=list(range(8)))
    out = np.zeros((B, S, D), np.float32)
    for core in range(8):
        b, half = core // 2, core % 2
        o = np.asarray(res.results[core]["out"], dtype=np.float32)
        if half == 0:
            out[b, 0:SQ] = o
        else:
            out[b, SQ:S] = o[::-1]
    if DEBUG:
        kernel.last_results = res.results
    return out
```

```python
import math
import numpy as np
import concourse.bass as bass
import concourse.mybir as mybir
from concourse.bass_utils import run_bass_kernel_spmd

F32 = mybir.dt.float32
BF16 = mybir.dt.bfloat16
AF = mybir.ActivationFunctionType
ALU = mybir.AluOpType

D = 2048
S = 4096
SQ = 2048
B = 4
NH = 8
DFF = 5632
NFC = DFF // 128
EPS = 1e-6
LAMBDA_INIT = 0.2
C_DIFF = 8.0
C_MLA = 14.0
FFN_T = 512
GW = 1152
ST = 256
NT = ST // 128

DEBUG = False
STOP_AFTER = 99
SUB = 99
AP_MASK = 31
NHA = 8
NSTL = 99


class Ev:
    __slots__ = ("sem", "val", "eng")

    def __init__(self, sem, val, eng):
        self.sem, self.val, self.eng = sem, val, eng


class Tok:
    __slots__ = ("name", "w", "r", "excl")

    def __init__(self, name, after=(), excl=False):
        self.name = name
        self.w = {}
        self.r = {}
        self.excl = excl
        for i, ev in enumerate(after):
            self.w[("init", i)] = ev


def PTok(name, after=()):
    return Tok(name, after, excl=True)


class DSem:
    _n = [0]

    def __init__(self, nc, name):
        DSem._n[0] += 1
        self.h = nc.alloc_semaphore("%s_%d" % (name, DSem._n[0]))
        self.count = 0


class EngState:
    LIMIT = 30000

    def __init__(self, nc, name, same_sync):
        self.nc, self.name, self.same_sync = nc, name, same_sync
        self.nsem = 0
        self.sem = None
        self.count = 0
        self.known = {}
        self.pending = []
        self.ops = []
        self.last = None
        self._newsem()

    def _newsem(self):
        self.sem = self.nc.alloc_semaphore("e_%s_%d" % (self.name, self.nsem))
        self.nsem += 1
        self.count = 0

    def new_event(self, sig):
        if not sig:
            ev = Ev(None, None, self.name)
            self.pending.append(ev)
            return ev
        if self.count >= self.LIMIT:
            self._newsem()
        self.count += 1
        ev = Ev(self.sem, self.count, self.name)
        for p in self.pending:
            p.sem, p.val = ev.sem, ev.val
        self.pending = []
        self.last = ev
        return ev


class Prog:
    def __init__(self, nc):
        self.nc = nc
        self.E = {
            "pe": EngState(nc, "pe", False),
            "act": EngState(nc, "act", True),
            "dve": EngState(nc, "dve", True),
            "pool": EngState(nc, "pool", True),
            "sp": EngState(nc, "sp", True),
        }
        self.dsems = []
        self.n_waits = 0

    def dsem(self, name):
        s = DSem(self.nc, name)
        self.dsems.append(s)
        return s

    def _need(self, E, eng, ev, waits, is_dma_issue):
        if ev is None:
            return
        if ev.sem is None:
            if ev.eng == eng and not is_dma_issue:
                return
            raise RuntimeError("dependency on unsignaled op (%s <- %s)" % (eng, ev.eng))
        if ev.eng == eng and not E.same_sync and not is_dma_issue:
            return
        k = ev.sem.num
        if E.known.get(k, 0) >= ev.val:
            return
        E.known[k] = ev.val
        waits.append((ev.sem, ev.val))

    def _deps(self, E, eng, reads, writes, is_dma_issue=False):
        waits = []
        for t in reads:
            for ev in t.w.values():
                self._need(E, eng, ev, waits, is_dma_issue)
            if t.excl:
                for k, ev in t.r.items():
                    if k != eng:
                        self._need(E, eng, ev, waits, is_dma_issue)
        for t in writes:
            for ev in t.w.values():
                self._need(E, eng, ev, waits, is_dma_issue)
            for ev in t.r.values():
                self._need(E, eng, ev, waits, is_dma_issue)
        self.n_waits += len(waits)
        return waits

    def op(self, eng, fn, reads=(), writes=(), sig=True):
        E = self.E[eng]
        waits = self._deps(E, eng, reads, writes)
        ev = E.new_event(sig)
        for t in reads:
            t.r[eng] = ev
        for t in writes:
            t.w = {eng: ev}
            t.r = {}
        E.ops.append((waits, fn, ("inc", ev.sem) if sig else None))
        return ev

    def dma(self, q, out, in_, sem, reads=(), writes=(), acc_writes=(), **kw):
        E = self.E[q]
        waits = self._deps(E, q, reads, writes, is_dma_issue=True)
        sem.count += 16
        ev = Ev(sem.h, sem.count, None)
        key = ("dma", sem.h.num)
        for t in reads:
            t.r[key] = ev
        for t in writes:
            t.w = {key: ev}
            t.r = {}
        for t in acc_writes:
            t.w[key] = ev
        E.ops.append((waits, (I("dma_start", out=out, in_=in_, **kw)), ("dma", sem.h)))
        return ev

    def all_events(self):
        evs = []
        for E in self.E.values():
            if E.pending:
                raise RuntimeError("pending unsignaled ops on %s at barrier" % E.name)
            if E.last is not None:
                evs.append(E.last)
        for s in self.dsems:
            if s.count:
                evs.append(Ev(s.h, s.count, None))
        return evs

    def final_wait(self, eng="sp"):
        E = self.E[eng]
        waits = []
        for ev in self.all_events():
            if ev.eng == eng:
                continue
            k = ev.sem.num
            if E.known.get(k, 0) >= ev.val:
                continue
            E.known[k] = ev.val
            waits.append((ev.sem, ev.val))
        E.ops.append((waits, None, None))

    def emit(self):
        nc = self.nc
        hooks = {"pe": "tensor", "act": "scalar", "dve": "vector", "pool": "gpsimd", "sp": "sync"}
        with nc.Block() as block:
            for name, attr in hooks.items():
                ops = self.E[name].ops

                def body(e, ops=ops):
                    for waits, fn, post in ops:
                        for (s, v) in waits:
                            e.wait_ge(s, v)
                        if fn is None:
                            continue
                        name_, args_, kw_ = fn
                        ins = getattr(e, name_)(*args_, **kw_)
                        if post is not None:
                            if post[0] == "inc":
                                ins.then_inc(post[1], 1)
                            else:
                                ins.then_inc(post[1], 16)

                getattr(block, attr)(body)


class Arena:
    def __init__(self, nc, lo=16512, hi=229376):
        self.nc, self.off, self.hi = nc, lo, hi
        self.n = 0
        self.peak = lo

    def alloc(self, name, shape, dt):
        nb = int(np.prod(shape[1:])) * (4 if dt == F32 else 2)
        nb = (nb + 63) // 64 * 64
        if self.off + nb > self.hi:
            raise RuntimeError("SBUF arena overflow allocating %s (%d + %d > %d)" % (name, self.off, nb, self.hi))
        self.n += 1
        t = self.nc.alloc_sbuf_tensor_at("%s_%d" % (name, self.n), list(shape), dt, offset=self.off)
        self.off += nb
        self.peak = max(self.peak, self.off)
        return t

    def mark(self):
        return self.off

    def release(self, m):
        self.off = m


class Ring:
    def __init__(self, P, arena, name, n, shape, dt, after=(), dma=True):
        self.n = n
        self.t = [arena.alloc("%s%d" % (name, i), shape, dt) for i in range(n)]
        self.tok = [Tok("%s%d" % (name, i), after) for i in range(n)]
        self.sem = [P.dsem("d_%s%d" % (name, i)) for i in range(n)] if dma else None
        self.i = -1

    def next(self):
        self.i += 1
        k = self.i % self.n
        return self.t[k], self.tok[k], (self.sem[k] if self.sem else None)


def I(name, *args, **kw):
    return (name, args, kw)


def build_program():
    nc = bass.Bass("TRN2", target_bir_lowering=False)
    P = Prog(nc)
    A = Arena(nc)

    def din(name, shape, dt=F32):
        return nc.dram_tensor(name, list(shape), dt, kind="ExternalInput").ap()

    def dscr(name, shape, dt):
        return nc.dram_tensor(name, list(shape), dt, kind=("ExternalOutput" if DEBUG else "Internal")).ap()

    xk = din("xk", [S, D])
    cT_d = din("cT", [128, 16])
    w_ada = din("w_ada", [D, 6 * D])
    b_ada = din("b_ada", [1, 6 * D])
    g1T_d = din("g1T", [128, 16])
    g2T_d = din("g2T", [128, 16])
    w_kv = din("w_kv", [D, 2432])
    w_q = din("w_q", [D, 1536])
    w_qb = din("w_qb", [512, 2048])
    w_kvb = din("w_kvb", [256, 2048])
    w_out = din("w_out", [D, D])
    w_gate = din("w_gate", [D, DFF])
    w_up = din("w_up", [D, DFF])
    w_down = din("w_down", [DFF, D])
    gcols_d = din("gcols", [128, 16])
    gsub_d = din("gsub_bc", [128, 128])
    lam_d = din("lam_bc", [128, 256])
    cos_d = din("cos_t", [64, S])
    sin_d = din("sin_t", [64, S])
    gtab_d = din("gtab", [128, NH, GW])
    gfar_d = din("gfar", [128, 16])
    cmat_d = din("cmat", [128, 512])
    out_d = nc.dram_tensor("out", [SQ, D], F32, kind="ExternalOutput").ap()

    mod_scr = dscr("mod_scr", [1, 6 * D], F32)
    KdT = dscr("KdT", [NH, 128, S], BF16)
    QdT = dscr("QdT", [NH, 128, SQ], BF16)
    Vd = dscr("Vd", [NH, 128, 32, 130], BF16)
    KmT = dscr("KmT", [NH, 128, S], BF16)
    QmnT = dscr("QmnT", [NH, 128, SQ], BF16)
    QmrT = dscr("QmrT", [NH, 64, SQ], BF16)
    Vm = dscr("Vm", [NH, 128, 32, 130], BF16)
    x1_scr = dscr("x1_scr", [SQ, D], F32)
    h2T_scr = dscr("h2T_scr", [128, 16, SQ], BF16)
    dbg_ksc = nc.dram_tensor("dbg_ksc", [128, 32 * 24 + 64 * 32], F32, kind="ExternalOutput").ap() if DEBUG else None

    banks = [nc.alloc_psum_tensor("bank%d" % i, [128, 512], F32) for i in range(8)]

    cmat = A.alloc("cmat", [128, 512], BF16)
    ident = cmat[:, 0:128]
    ones = cmat[:, 128:256]
    bones = cmat[:, 256:384]
    sel2 = cmat[:, 384:386]
    gcols = A.alloc("gcols", [128, 16], F32)
    cv = A.alloc("cv", [128, 8], F32)
    A1T = A.alloc("A1T", [128, 16], F32)
    B1T = A.alloc("B1T", [128, 16], F32)
    A2T = A.alloc("A2T", [128, 16], F32)
    B2T = A.alloc("B2T", [128, 16], F32)
    ksc_d = A.alloc("ksc_d", [128, 32, 16], F32)
    ksc_m = A.alloc("ksc_m", [128, 32, 8], F32)
    kropeT = A.alloc("kropeT", [64, S], BF16)
    t_const = Tok("const")
    t_cv = Tok("cv")
    t_mod = Tok("mod")
    t_ksc = Tok("ksc")
    t_krope = Tok("krope")
    s_const = P.dsem("d_const")

    CV_EPS, CV_LN8, CV_LN192, CV_LN08, CV_NCD, CV_NCM, CV_ZERO = 0, 1, 2, 3, 4, 5, 6
    cvals = [EPS, math.log(0.125), math.log(192 ** -0.5), math.log(1.0 - LAMBDA_INIT), -C_DIFF, -C_MLA, 0.0, 1.0]

    P.dma("pool", cmat[:], cmat_d, s_const, acc_writes=[t_const])
    s_const2 = P.dsem("d_const2")
    P.dma("sp", gcols[:], gcols_d, s_const2, acc_writes=[t_const])
    for i, v in enumerate(cvals):
        P.op("dve", I("memset", cv[:, i:i + 1], float(v)), writes=[t_cv])

    def cvc(i, n=128):
        return cv[0:n, i:i + 1]

    def gc(i, n=128):
        return gcols[0:n, i:i + 1]

    G_QD, G_KD, G_QA, G_KVA, G_QMN, G_QMR, G_QMP, G_KMN, G_KMR, G_KMP = 0, 1, 2, 6, 8, 9, 10, 11, 12, 13

    def rstd_ops(src, tmp, dst, inv_n, ln_bias_col, reads, tmp_tok, dst_tok, npart=128):
        P.op("act", I("activation", out=tmp, in_=src, func=AF.Ln, scale=inv_n, bias=cvc(CV_EPS, npart)),
             reads=list(reads) + [t_cv], writes=[tmp_tok])
        P.op("act", I("activation", out=dst, in_=tmp, func=AF.Exp, scale=-0.5, bias=cvc(ln_bias_col, npart)),
             reads=[t_cv, tmp_tok], writes=[dst_tok])

    m0 = A.mark()
    cT = A.alloc("cT", [128, 16], F32)
    cact = A.alloc("cact", [128, 16], BF16)
    g1T = A.alloc("g1T", [128, 16], F32)
    g2T = A.alloc("g2T", [128, 16], F32)
    modT = A.alloc("modT", [128, 96], F32)
    modrow = A.alloc("modrow", [1, 6 * D], F32)
    brow = A.alloc("brow", [1, 6 * D], F32)
    wr = Ring(P, A, "wada", 2, [128, 16, 1024], BF16)
    t_c = Tok("c")
    t_cact = Tok("cact")
    t_brow = Tok("brow")
    t_modrow = Tok("modrow")
    s_p0 = P.dsem("d_p0")
    s_p0b = P.dsem("d_p0b")
    s_p0c = P.dsem("d_p0c")
    P.dma("sp", cT[:], cT_d, s_p0, acc_writes=[t_c])
    P.dma("sp", g1T[:], g1T_d, s_p0, acc_writes=[t_c])
    P.dma("sp", g2T[:], g2T_d, s_p0, acc_writes=[t_c])
    P.dma("sp", brow[:], b_ada, s_p0b, writes=[t_brow])
    P.op("act", I("activation", out=cact[:], in_=cT[:], func=AF.Silu), reads=[t_c], writes=[t_cact])
    pb = [PTok("p0bank0"), PTok("p0bank1")]
    for nb in range(12):
        wt, wtok, wsem = wr.next()
        P.dma("pool", wt[:], w_ada[:, nb * 1024:(nb + 1) * 1024].rearrange("(kc p) n -> p kc n", p=128),
              wsem, writes=[wtok])
        for half in range(2):
            i = nb * 2 + half
            bk, btok = banks[i % 2], pb[i % 2]
            for kc in range(16):
                P.op("pe", I("matmul", bk[0:1, :], lhsT=cact[:, kc:kc + 1], rhs=wt[:, kc, half * 512:(half + 1) * 512],
                             start=(kc == 0), stop=(kc == 15)),
                     reads=[wtok, t_cact], writes=[btok], sig=(kc == 15))
            c0 = i * 512
            P.op("dve", I("tensor_tensor", out=modrow[0:1, c0:c0 + 512], in0=bk[0:1, :], in1=brow[0:1, c0:c0 + 512],
                          op=ALU.add), reads=[btok, t_brow], writes=[t_modrow])
    t_modscr = Tok("modscr")
    P.dma("sp", mod_scr, modrow[:], s_p0c, reads=[t_modrow], writes=[t_modscr])
    t_modT = Tok("modT")
    s_p0d = P.dsem("d_p0d")
    for j0 in range(0, 96, 16):
        P.dma("sp", modT[:, j0:j0 + 16],
              mod_scr[:, j0 * 128:(j0 + 16) * 128].rearrange("o (j p) -> p (o j)", p=128),
              s_p0d, reads=[t_modscr], acc_writes=[t_modT], allow_slow_non_contiguous=True)
    P.op("dve", I("scalar_tensor_tensor", out=A1T[:], in0=modT[:, 16:32], scalar=1.0, in1=g1T[:],
                  op0=ALU.add, op1=ALU.mult), reads=[t_modT, t_c], writes=[t_mod])
    P.op("dve", I("tensor_copy", out=B1T[:], in_=modT[:, 0:16]), reads=[t_modT], writes=[t_mod])
    P.op("dve", I("scalar_tensor_tensor", out=A2T[:], in0=modT[:, 64:80], scalar=1.0, in1=g2T[:],
                  op0=ALU.add, op1=ALU.mult), reads=[t_modT, t_c], writes=[t_mod])
    P.op("dve", I("tensor_copy", out=B2T[:], in_=modT[:, 48:64]), reads=[t_modT], writes=[t_mod])
    A.release(m0)

    def norm_supertile(env, xrows0, AT, BT):
        hT, hT_tok, _ = env["hT"].next()
        for tt in range(NT):
            xt, xtok, xsem = env["x"].next()
            r = xrows0 + tt * 128
            P.dma("sp", xt[:], xk[r:r + 128, :], xsem, writes=[xtok])
            st, sttok, _ = env["stat"].next()
            xn, xntok, _ = env["xn"].next()
            P.op("act", I("activation", out=xn[:], in_=xt[:], func=AF.Square, accum_out=st[:, 0:1]),
                 reads=[xtok], writes=[xntok, sttok])
            rstd_ops(st[:, 0:1], st[:, 1:2], st[:, 2:3], 1.0 / D, CV_ZERO, [sttok], sttok, sttok)
            P.op("dve", I("tensor_scalar", out=xn[:], in0=xt[:], scalar1=st[:, 2:3], scalar2=None, op0=ALU.mult),
                 reads=[xtok, sttok], writes=[xntok])
            for g in range(4):
                tp, tptok = env["tp"][g % 2]
                for j in range(4):
                    dc = g * 4 + j
                    P.op("pe", I("transpose", tp[:, j * 128:(j + 1) * 128], xn[:, dc * 128:(dc + 1) * 128], ident),
                         reads=[xntok, t_const], writes=[tptok], sig=(j == 3))
                for j in range(4):
                    dc = g * 4 + j
                    P.op("dve", I("tensor_scalar", out=hT[:, dc, tt * 128:(tt + 1) * 128], in0=tp[:, j * 128:(j + 1) * 128],
                                  scalar1=AT[:, dc:dc + 1], scalar2=BT[:, dc:dc + 1], op0=ALU.mult, op1=ALU.add),
                         reads=[tptok, t_mod], writes=[hT_tok])
        return hT, hT_tok

    def proj_fm(wt, wtok, c0, ncol, rhsT, rhs_tok, nk, out_bank, out_tok):
        for k in range(nk):
            P.op("pe", I("matmul", out_bank[0:ncol, 0:ST], lhsT=wt[:, k, c0:c0 + ncol], rhs=rhsT[:, k, :],
                         start=(k == 0), stop=(k == nk - 1)),
                 reads=[wtok, rhs_tok], writes=[out_tok], sig=(k == nk - 1))

    def make_env(after):
        env = {}
        env["x"] = Ring(P, A, "x", 2, [128, D], F32, after)
        env["stat"] = Ring(P, A, "stat", 2, [128, 4], F32, after, dma=False)
        env["xn"] = Ring(P, A, "xn", 2, [128, D], BF16, after, dma=False)
        env["hT"] = Ring(P, A, "hT", 2, [128, 16, ST], BF16, after, dma=False)
        env["tp"] = [(banks[0][:].bitcast(BF16)[:, 0:512], PTok("tp0", after)),
                     (banks[1][:].bitcast(BF16)[:, 0:512], PTok("tp1", after))]
        env["mm"] = [(banks[2 + i], PTok("mm%d" % i, after)) for i in range(3)]
        env["mmi"] = 0
        t_ssaux = PTok("ssaux", after)
        env["ss"] = (banks[5], t_ssaux)
        env["aux"] = (banks[5][:, 256:512], t_ssaux)
        env["rp"] = (banks[6], PTok("rp", after))
        env["rpp"] = (banks[7], PTok("rpp", after))
        return env

    def next_mm(env):
        b, t = env["mm"][env["mmi"] % 3]
        env["mmi"] += 1
        return b, t

    t_scr = {k: Tok(k) for k in ["KdT", "QdT", "Vd", "KmT", "QmnT", "QmrT", "Vm", "x1", "h2T"]}

    after = P.all_events()
    mA = A.mark()
    envA = make_env(after)
    wkv = A.alloc("wkv", [128, 16, 2432], BF16)
    wkvb = A.alloc("wkvb", [128, 2, 2048], BF16)
    t_wA = Tok("wA")
    s_wA = P.dsem("d_wA")
    dmy = Tok("dummyA", after)
    for i in range(4):
        P.dma("pool", wkv[:, i * 4:(i + 1) * 4, :], w_kv[i * 512:(i + 1) * 512, :].rearrange("(k p) n -> p k n", p=128),
              s_wA, acc_writes=[t_wA], reads=[dmy])
    P.dma("pool", wkvb[:], w_kvb.rearrange("(k p) n -> p k n", p=128), s_wA, acc_writes=[t_wA])
    tabk = Ring(P, A, "tabk", 2, [64, 2, ST], F32, after)
    kd_st = Ring(P, A, "kd_st", 2, [128, NH, ST], BF16, after)
    vd_st = Ring(P, A, "vd_st", 2, [128, NH, NT, 130], BF16, after)
    sqT = Ring(P, A, "sqT", 2, [128, ST], BF16, after, dma=False)
    csb = Ring(P, A, "csb", 1, [128, 2, ST], F32, after, dma=False)
    sqc = Ring(P, A, "sqc", 1, [128, 2, ST], BF16, after, dma=False)
    cn = Ring(P, A, "cn", 1, [128, 2, ST], BF16, after, dma=False)
    rbc = Ring(P, A, "rbc", 2, [128, 2, ST], F32, after, dma=False)
    sqpe = Ring(P, A, "sqpe", 1, [64, ST], BF16, after, dma=False)
    rt = Ring(P, A, "rt", 2, [64, 2, ST], F32, after, dma=False)
    auxs = Ring(P, A, "auxs", 2, [128, 2, 64], F32, after, dma=False)

    for i in range(2):
        t_, tok_ = vd_st.t[i], vd_st.tok[i]
        P.op("dve", I("memset", t_[:, :, :, 128:129], 1.0), writes=[tok_])
        P.op("dve", I("memset", t_[:, :, :, 129:130], 0.0), writes=[tok_])

    def v_tokmajor(env, lhs, lhs_tok, nk, wt, wtok, c0, dram, dram_tok, kt0):
        vst, vtok, vsem = vd_st.next()
        for tt in range(NT):
            for half in range(2):
                bk, btok = next_mm(env)
                for k in range(nk):
                    P.op("pe", I("matmul", bk[:], lhsT=lhs[:, k, tt * 128:(tt + 1) * 128],
                                 rhs=wt[:, k, c0 + half * 512:c0 + (half + 1) * 512], start=(k == 0), stop=(k == nk - 1)),
                         reads=[lhs_tok, wtok], writes=[btok], sig=(k == nk - 1))
                P.op("act", I("activation", out=vst[:, half * 4:(half + 1) * 4, tt, 0:128],
                              in_=bk[:].rearrange("p (h e) -> p h e", e=128), func=AF.Copy),
                     reads=[btok], writes=[vtok])
        P.dma("sp", dram[:, :, kt0:kt0 + NT, :].rearrange("h p k e -> p h k e"), vst[:], vsem,
              reads=[vtok], acc_writes=[dram_tok])

    if STOP_AFTER >= 1:
        for st_i in range(S // ST):
            tok0 = st_i * ST
            kt0 = st_i * NT
            hT, hTtok = norm_supertile(envA, tok0, A1T, B1T)
            axb, axtok = envA["aux"]
            tb, tbtok, tbsem = tabk.next()
            P.dma("sp", tb[:, 0, :], cos_d[:, tok0:tok0 + ST], tbsem, writes=[tbtok])
            ev = P.dma("sp", tb[:, 1, :], sin_d[:, tok0:tok0 + ST], tbsem, acc_writes=[tbtok])
            tbtok.w = {("dma", tbsem.h.num): ev}
            P.op("dve", I("tensor_scalar", out=tb[:, 0, :], in0=tb[:, 0, :], scalar1=gc(G_KMR, 64), scalar2=None, op0=ALU.mult),
                 reads=[t_const], writes=[tbtok])
            P.op("dve", I("tensor_scalar", out=tb[:, 1, :], in0=tb[:, 1, :], scalar1=gc(G_KMP, 64), scalar2=None, op0=ALU.mult),
                 reads=[t_const], writes=[tbtok])
            if SUB < 0 or st_i >= NSTL:
                continue
            kst, ksttok, kstsem = kd_st.next()
            for h in range(NHA):
                bk, btok = next_mm(envA)
                proj_fm(wkv, t_wA, h * 128, 128, hT, hTtok, 16, bk, btok)
                sq, sqtok, _ = sqT.next()
                if AP_MASK & 1:
                    P.op("act", I("activation", out=sq[:], in_=bk[:, 0:ST], func=AF.Square), reads=[btok], writes=[sqtok])
                if AP_MASK & 2:
                    P.op("dve", I("tensor_scalar", out=kst[:, h, :], in0=bk[:, 0:ST], scalar1=gc(G_KD), scalar2=None, op0=ALU.mult),
                         reads=[btok, t_const], writes=[ksttok])
                for tt in range(NT):
                    c = tt * 16 + h * 2
                    if AP_MASK & 4:
                        P.op("pe", I("matmul", axb[:, c:c + 2], lhsT=sq[:, tt * 128:(tt + 1) * 128], rhs=sel2,
                                     start=True, stop=True),
                             reads=[sqtok, t_const], writes=[axtok], sig=(tt == NT - 1))
            if AP_MASK & 8:
                P.dma("sp", KdT[:, :, tok0:tok0 + ST].rearrange("h p t -> p h t"), kst[:], kstsem,
                      reads=[ksttok], acc_writes=[t_scr["KdT"]])
            au, autok, _ = auxs.next()
            if AP_MASK & 16:
                rstd_ops(axb[:, 0:NT * 16], au[:, 0, 0:NT * 16], ksc_d[:, kt0:kt0 + NT, :].rearrange("p k c -> p (k c)"),
                         1.0 / 64, CV_LN8, [axtok], autok, t_ksc)
            if SUB < 1 or st_i >= NSTL:
                continue
            v_tokmajor(envA, hT, hTtok, 16, wkv, t_wA, 1024, Vd, t_scr["Vd"], kt0)
            if SUB < 2 or st_i >= NSTL:
                continue
            cs, cstok, _ = csb.next()
            sc_, sctok, _ = sqc.next()
            for c in range(2):
                bk, btok = next_mm(envA)
                proj_fm(wkv, t_wA, 2048 + c * 128, 128, hT, hTtok, 16, bk, btok)
                P.op("act", I("activation", out=sc_[:, c, :], in_=bk[:, 0:ST], func=AF.Square), reads=[btok], writes=[sctok])
                P.op("dve", I("tensor_copy", out=cs[:, c, :], in_=bk[:, 0:ST]), reads=[btok], writes=[cstok])
            ssb, sstok = envA["ss"]
            for c in range(2):
                P.op("pe", I("matmul", ssb[:, 0:ST], lhsT=ones, rhs=sc_[:, c, :], start=(c == 0), stop=(c == 1)),
                     reads=[sctok, t_const], writes=[sstok], sig=(c == 1))
            rb, rbtok, _ = rbc.next()
            rstd_ops(ssb[:, 0:ST], rb[:, 0, :], rb[:, 1, :], 1.0 / 256, CV_ZERO, [sstok], rbtok, rbtok)
            cnt, cntok, _ = cn.next()
            for c in range(2):
                P.op("dve", I("scalar_tensor_tensor", out=cnt[:, c, :], in0=cs[:, c, :], scalar=gc(G_KVA + c), in1=rb[:, 1, :],
                              op0=ALU.mult, op1=ALU.mult), reads=[cstok, rbtok, t_const], writes=[cntok])
            if SUB < 3 or st_i >= NSTL:
                continue
            rpb, rptok = envA["rp"]
            rppb, rpptok = envA["rpp"]
            proj_fm(wkv, t_wA, 2304, 64, hT, hTtok, 16, rpb, rptok)
            proj_fm(wkv, t_wA, 2368, 64, hT, hTtok, 16, rppb, rpptok)
            sp_, sptok, _ = sqpe.next()
            P.op("act", I("activation", out=sp_[:], in_=rpb[0:64, 0:ST], func=AF.Square), reads=[rptok], writes=[sptok])
            r_, rtok_, _ = rt.next()
            P.op("dve", I("tensor_tensor", out=r_[:, 0, :], in0=rpb[0:64, 0:ST], in1=tb[:, 0, :], op=ALU.mult),
                 reads=[rptok, tbtok], writes=[rtok_])
            P.op("dve", I("tensor_tensor", out=r_[:, 1, :], in0=rppb[0:64, 0:ST], in1=tb[:, 1, :], op=ALU.mult),
                 reads=[rpptok, tbtok], writes=[rtok_])
            P.op("dve", I("tensor_tensor", out=kropeT[:, tok0:tok0 + ST], in0=r_[:, 0, :], in1=r_[:, 1, :], op=ALU.add),
                 reads=[rtok_], writes=[t_krope])
            if SUB < 4 or st_i >= NSTL:
                continue
            kst, ksttok, kstsem = kd_st.next()
            for h in range(NH):
                bk, btok = next_mm(envA)
                proj_fm(wkvb, t_wA, h * 128, 128, cnt, cntok, 2, bk, btok)
                sq, sqtok, _ = sqT.next()
                P.op("act", I("activation", out=sq[:], in_=bk[:, 0:ST], func=AF.Square), reads=[btok], writes=[sqtok])
                P.op("dve", I("tensor_scalar", out=kst[:, h, :], in0=bk[:, 0:ST], scalar1=gc(G_KMN), scalar2=None, op0=ALU.mult),
                     reads=[btok, t_const], writes=[ksttok])
                for tt in range(NT):
                    c = 64 + tt * 8 + h
                    P.op("pe", I("matmul", axb[:, c:c + 1], lhsT=sq[:, tt * 128:(tt + 1) * 128], rhs=ones[:, 0:1],
                                 start=True, stop=False), reads=[sqtok, t_const], writes=[axtok], sig=False)
                    P.op("pe", I("matmul", axb[:, c:c + 1], lhsT=sp_[:, tt * 128:(tt + 1) * 128], rhs=ones[0:64, 0:1],
                                 start=False, stop=True), reads=[sptok, t_const], writes=[axtok], sig=(tt == NT - 1))
            P.dma("sp", KmT[:, :, tok0:tok0 + ST].rearrange("h p t -> p h t"), kst[:], kstsem,
                  reads=[ksttok], acc_writes=[t_scr["KmT"]])
            au, autok, _ = auxs.next()
            rstd_ops(axb[:, 64:64 + NT * 8], au[:, 0, 0:NT * 8], ksc_m[:, kt0:kt0 + NT, :].rearrange("p k c -> p (k c)"),
                     1.0 / 192, CV_LN192, [axtok], autok, t_ksc)
            if SUB < 5 or st_i >= NSTL:
                continue
            v_tokmajor(envA, cnt, cntok, 2, wkvb, t_wA, 1024, Vm, t_scr["Vm"], kt0)
    A.release(mA)

    after = P.all_events()
    mB = A.mark()
    envB = make_env(after)
    wq = A.alloc("wq", [128, 16, 1536], BF16)
    wqb = A.alloc("wqb", [128, 4, 2048], BF16)
    t_wB = Tok("wB")
    s_wB = P.dsem("d_wB")
    dmy = Tok("dummyB", after)
    for i in range(4):
        P.dma("pool", wq[:, i * 4:(i + 1) * 4, :], w_q[i * 512:(i + 1) * 512, :].rearrange("(k p) n -> p k n", p=128),
              s_wB, acc_writes=[t_wB], reads=[dmy])
    P.dma("pool", wqb[:], w_qb.rearrange("(k p) n -> p k n", p=128), s_wB, acc_writes=[t_wB])
    tabq = Ring(P, A, "tabq", 2, [64, 2, ST], F32, after)
    qd_st = Ring(P, A, "qd_st", 2, [128, NH, ST], BF16, after)
    qr_st = Ring(P, A, "qr_st", 2, [64, NH, ST], BF16, after)
    sqTb = Ring(P, A, "sqTb", 2, [128, ST], BF16, after, dma=False)
    sqRb = Ring(P, A, "sqRb", 2, [64, ST], BF16, after, dma=False)
    csbB = Ring(P, A, "csbB", 1, [128, 4, ST], F32, after, dma=False)
    sqcB = Ring(P, A, "sqcB", 1, [128, 4, ST], BF16, after, dma=False)
    cnB = Ring(P, A, "cnB", 1, [128, 4, ST], BF16, after, dma=False)
    rbcB = Ring(P, A, "rbcB", 2, [128, 2, ST], F32, after, dma=False)
    rtB = Ring(P, A, "rtB", 2, [64, 2, ST], F32, after, dma=False)

    if STOP_AFTER >= 2:
        for st_i in range(SQ // ST):
            if st_i >= NSTL:
                continue
            tok0 = st_i * ST
            hT, hTtok = norm_supertile(envB, tok0, A1T, B1T)
            ssb, sstok = envB["ss"]
            tb, tbtok, tbsem = tabq.next()
            P.dma("sp", tb[:, 0, :], cos_d[:, tok0:tok0 + ST], tbsem, writes=[tbtok])
            ev = P.dma("sp", tb[:, 1, :], sin_d[:, tok0:tok0 + ST], tbsem, acc_writes=[tbtok])
            tbtok.w = {("dma", tbsem.h.num): ev}
            P.op("dve", I("tensor_scalar", out=tb[:, 0, :], in0=tb[:, 0, :], scalar1=gc(G_QMR, 64), scalar2=None, op0=ALU.mult),
                 reads=[t_const], writes=[tbtok])
            P.op("dve", I("tensor_scalar", out=tb[:, 1, :], in0=tb[:, 1, :], scalar1=gc(G_QMP, 64), scalar2=None, op0=ALU.mult),
                 reads=[t_const], writes=[tbtok])
            qst, qsttok, qstsem = qd_st.next()
            for h in range(NH):
                bk, btok = next_mm(envB)
                proj_fm(wq, t_wB, h * 128, 128, hT, hTtok, 16, bk, btok)
                sq, sqtok, _ = sqTb.next()
                P.op("act", I("activation", out=sq[:], in_=bk[:, 0:ST], func=AF.Square), reads=[btok], writes=[sqtok])
                P.op("pe", I("matmul", ssb[:, 0:ST], lhsT=bones, rhs=sq[:], start=True, stop=True),
                     reads=[sqtok, t_const], writes=[sstok])
                rb, rbtok, _ = rbcB.next()
                rstd_ops(ssb[:, 0:ST], rb[:, 0, :], rb[:, 1, :], 1.0 / 64, CV_ZERO, [sstok], rbtok, rbtok)
                P.op("dve", I("scalar_tensor_tensor", out=qst[:, h, :], in0=bk[:, 0:ST], scalar=gc(G_QD), in1=rb[:, 1, :],
                              op0=ALU.mult, op1=ALU.mult), reads=[btok, rbtok, t_const], writes=[qsttok])
            P.dma("sp", QdT[:, :, tok0:tok0 + ST].rearrange("h p t -> p h t"), qst[:], qstsem,
                  reads=[qsttok], acc_writes=[t_scr["QdT"]])
            cs, cstok, _ = csbB.next()
            sc_, sctok, _ = sqcB.next()
            for c in range(4):
                bk, btok = next_mm(envB)
                proj_fm(wq, t_wB, 1024 + c * 128, 128, hT, hTtok, 16, bk, btok)
                P.op("act", I("activation", out=sc_[:, c, :], in_=bk[:, 0:ST], func=AF.Square), reads=[btok], writes=[sctok])
                P.op("dve", I("tensor_copy", out=cs[:, c, :], in_=bk[:, 0:ST]), reads=[btok], writes=[cstok])
            for c in range(4):
                P.op("pe", I("matmul", ssb[:, 0:ST], lhsT=ones, rhs=sc_[:, c, :], start=(c == 0), stop=(c == 3)),
                     reads=[sctok, t_const], writes=[sstok], sig=(c == 3))
            rb, rbtok, _ = rbcB.next()
            rstd_ops(ssb[:, 0:ST], rb[:, 0, :], rb[:, 1, :], 1.0 / 512, CV_ZERO, [sstok], rbtok, rbtok)
            cnt, cntok, _ = cnB.next()
            for c in range(4):
                P.op("dve", I("scalar_tensor_tensor", out=cnt[:, c, :], in0=cs[:, c, :], scalar=gc(G_QA + c), in1=rb[:, 1, :],
                              op0=ALU.mult, op1=ALU.mult), reads=[cstok, rbtok, t_const], writes=[cntok])
            qst, qsttok, qstsem = qd_st.next()
            qrs, qrstok, qrssem = qr_st.next()
            rpb, rptok = envB["rp"]
            rppb, rpptok = envB["rpp"]
            for h in range(NH):
                bk, btok = next_mm(envB)
                proj_fm(wqb, t_wB, h * 128, 128, cnt, cntok, 4, bk, btok)
                proj_fm(wqb, t_wB, 1024 + h * 64, 64, cnt, cntok, 4, rpb, rptok)
                proj_fm(wqb, t_wB, 1536 + h * 64, 64, cnt, cntok, 4, rppb, rpptok)
                sq, sqtok, _ = sqTb.next()
                sqr, sqrtok, _ = sqRb.next()
                P.op("act", I("activation", out=sq[:], in_=bk[:, 0:ST], func=AF.Square), reads=[btok], writes=[sqtok])
                P.op("act", I("activation", out=sqr[:], in_=rpb[0:64, 0:ST], func=AF.Square), reads=[rptok], writes=[sqrtok])
                P.op("pe", I("matmul", ssb[:, 0:ST], lhsT=ones, rhs=sq[:], start=True, stop=False),
                     reads=[sqtok, t_const], writes=[sstok], sig=False)
                P.op("pe", I("matmul", ssb[:, 0:ST], lhsT=ones[0:64, :], rhs=sqr[:], start=False, stop=True),
                     reads=[sqrtok, t_const], writes=[sstok])
                rb, rbtok, _ = rbcB.next()
                rstd_ops(ssb[:, 0:ST], rb[:, 0, :], rb[:, 1, :], 1.0 / 192, CV_ZERO, [sstok], rbtok, rbtok)
                P.op("dve", I("scalar_tensor_tensor", out=qst[:, h, :], in0=bk[:, 0:ST], scalar=gc(G_QMN), in1=rb[:, 1, :],
                              op0=ALU.mult, op1=ALU.mult), reads=[btok, rbtok, t_const], writes=[qsttok])
                r_, rtok_, _ = rtB.next()
                P.op("dve", I("tensor_tensor", out=r_[:, 0, :], in0=rpb[0:64, 0:ST], in1=tb[:, 0, :], op=ALU.mult),
                     reads=[rptok, tbtok], writes=[rtok_])
                P.op("dve", I("tensor_tensor", out=r_[:, 1, :], in0=rppb[0:64, 0:ST], in1=tb[:, 1, :], op=ALU.mult),
                     reads=[rpptok, tbtok], writes=[rtok_])
                P.op("dve", I("tensor_tensor", out=r_[:, 0, :], in0=r_[:, 0, :], in1=r_[:, 1, :], op=ALU.add),
                     reads=[], writes=[rtok_])
                P.op("dve", I("tensor_tensor", out=qrs[:, h, :], in0=r_[:, 0, :], in1=rb[0:64, 1, :], op=ALU.mult),
                     reads=[rtok_, rbtok], writes=[qrstok])
            P.dma("sp", QmnT[:, :, tok0:tok0 + ST].rearrange("h p t -> p h t"), qst[:], qstsem,
                  reads=[qsttok], acc_writes=[t_scr["QmnT"]])
            P.dma("sp", QmrT[:, :, tok0:tok0 + ST].rearrange("h p t -> p h t"), qrs[:], qrssem,
                  reads=[qrstok], acc_writes=[t_scr["QmrT"]])
    A.release(mB)

    if DEBUG and NSTL >= 99 and SUB >= 99 and STOP_AFTER >= 1:
        s_dbg = P.dsem("d_dbg")
        P.dma("sp", dbg_ksc[:, 0:512], ksc_d[:].rearrange("p k c -> p (k c)"), s_dbg, reads=[t_ksc])
        P.dma("sp", dbg_ksc[:, 512:768], ksc_m[:].rearrange("p k c -> p (k c)"), s_dbg, reads=[t_ksc])

    after = P.all_events()
    m_mix = A.mark()
    mixT = A.alloc("mixT", [128, 16, SQ], BF16)
    t_mix = [Tok("mix%d" % i, after) for i in range(4)]
    m2 = A.mark()
    gtab = A.alloc("gtab", [128, NH, GW], F32)
    gfar = A.alloc("gfar", [128, 16], F32)
    lam = A.alloc("lam", [128, 256], F32)
    lamw = A.alloc("lamw", [128, 8], F32)
    ones_f = A.alloc("ones_f", [1, 128], F32)
    t_g = Tok("gtab")
    s_g = P.dsem("d_g")
    dmy = Tok("dummy2", after)
    P.dma("sp", gtab[:], gtab_d, s_g, acc_writes=[t_g], reads=[dmy])
    P.dma("sp", gfar[:], gfar_d, s_g, acc_writes=[t_g])
    P.dma("sp", lam[:], lam_d, s_g, acc_writes=[t_g])
    t_lam = Tok("lam")
    P.op("dve", I("memset", ones_f[:], 1.0), reads=[dmy], writes=[t_lam])
    P.op("dve", I("tensor_scalar", out=gfar[:], in0=gfar[:], scalar1=-C_DIFF, scalar2=None, op0=ALU.add),
         reads=[t_g], writes=[t_lam])
    P.op("dve", I("scalar_tensor_tensor", out=lam[:, 0:64], in0=lam[:, 0:64], scalar=1.0, in1=lam[:, 64:128],
                  op0=ALU.mult, op1=ALU.mult, accum_out=lamw[:, 0:1]), reads=[t_g], writes=[t_lam])
    P.op("dve", I("scalar_tensor_tensor", out=lam[:, 128:192], in0=lam[:, 128:192], scalar=1.0, in1=lam[:, 192:256],
                  op0=ALU.mult, op1=ALU.mult, accum_out=lamw[:, 1:2]), reads=[t_g], writes=[t_lam])
    P.op("act", I("activation", out=lamw[:, 2:4], in_=lamw[:, 0:2], func=AF.Exp), reads=[t_lam], writes=[t_lam])
    P.op("dve", I("tensor_tensor", out=lamw[:, 4:5], in0=lamw[:, 3:4], in1=lamw[:, 2:3], op=ALU.subtract),
         reads=[], writes=[t_lam])
    P.op("dve", I("tensor_scalar", out=lamw[:, 5:6], in0=lamw[:, 4:5], scalar1=-LAMBDA_INIT, scalar2=None, op0=ALU.add),
         reads=[], writes=[t_lam])
    NEGLAM = lamw[:, 5:6]

    Kb = Ring(P, A, "Kb", 2, [128, S], BF16, after)
    Qb = Ring(P, A, "Qb", 2, [128, SQ], BF16, after, dma=False)
    Qrb = Ring(P, A, "Qrb", 2, [64, SQ], BF16, after, dma=False)
    Vb = Ring(P, A, "Vb", 2, [128, 32, 130], BF16, after, dma=False)
    LA = 2
    PT = Ring(P, A, "PT", LA + 1, [128, 512], BF16, after, dma=False)
    btmp = Ring(P, A, "btmp", 2, [128, 512], F32, after, dma=False)
    recs = Ring(P, A, "recs", 2, [1, 512], F32, after, dma=False)
    bcs = Ring(P, A, "bcs", 2, [128, 512], F32, after, dma=False)
    o1T = Ring(P, A, "o1T", 2, [128, 512], F32, after, dma=False)
    cmb = Ring(P, A, "cmb", 2, [128, 512], F32, after, dma=False)
    sqb = Ring(P, A, "sqb", 2, [128, 512], BF16, after, dma=False)
    lnt = Ring(P, A, "lnt", 2, [128, 2, 512], F32, after, dma=False)
    Sb = [(banks[i], PTok("S%d" % i, after)) for i in (0, 1, 2)]
    Ot = [(banks[3 + i], PTok("Ot%d" % i, after)) for i in range(2)]
    Sm = [(banks[5 + i], PTok("Sm%d" % i, after)) for i in range(2)]
    Eb, Etok = banks[7], PTok("E", after)

    def load_head(srcs):
        res = []
        sem = None
        ev = None
        for ring, dram_ap, scr_tok in srcs:
            t_, tok_, sem_ = ring.next()
            if sem is None:
                sem = sem_
            ev = P.dma("sp", t_[:], dram_ap, sem, reads=[scr_tok], writes=[tok_])
            res.append((t_, tok_))
        for _, tok_ in res:
            tok_.w = {("dma", sem.h.num): ev}
        return res

    specs = []
    for h in range(NH):
        for qc in range(4):
            for mp in range(2):
                specs.append(("d", h, qc, mp))
    for h in range(NH):
        for qc in range(4):
            specs.append(("m", h, qc, 0))
    if STOP_AFTER < 3:
        specs = []
    heads = {}
    o1s = {}

    def head_data(kind, h):
        if (kind, h) not in heads:
            if kind == "d":
                heads[(kind, h)] = load_head([(Kb, KdT[h], t_scr["KdT"]), (Qb, QdT[h], t_scr["QdT"]),
                                              (Vb, Vd[h], t_scr["Vd"])])
            else:
                heads[(kind, h)] = load_head([(Kb, KmT[h], t_scr["KmT"]), (Qb, QmnT[h], t_scr["QmnT"]),
                                              (Vb, Vm[h], t_scr["Vm"]), (Qrb, QmrT[h], t_scr["QmrT"])])
        return heads[(kind, h)]

    steps = [(si, kt) for si in range(len(specs)) for kt in range(32)]
    s_i = [0]

    def emit_qk(i):
        si, kt = steps[i]
        kind, h, qc, mp = specs[si]
        hd = head_data(kind, h)
        (K, Ktok), (Q, Qtok), (V, Vtok) = hd[0], hd[1], hd[2]
        sb_, stok = Sb[s_i[0] % len(Sb)]
        s_i[0] += 1
        if kind == "d":
            r0 = mp * 64
            P.op("pe", I("matmul", sb_[:], lhsT=K[r0:r0 + 64, kt * 128:(kt + 1) * 128],
                         rhs=Q[r0:r0 + 64, qc * 512:(qc + 1) * 512], start=True, stop=True),
                 reads=[Ktok, Qtok], writes=[stok])
        else:
            Qr, Qrtok = hd[3]
            P.op("pe", I("matmul", sb_[:], lhsT=K[:, kt * 128:(kt + 1) * 128], rhs=Q[:, qc * 512:(qc + 1) * 512],
                         start=True, stop=False), reads=[Ktok, Qtok], writes=[stok], sig=False)
            P.op("pe", I("matmul", sb_[:], lhsT=kropeT[:, kt * 128:(kt + 1) * 128], rhs=Qr[:, qc * 512:(qc + 1) * 512],
                         start=False, stop=True), reads=[t_krope, Qrtok], writes=[stok])
        pt, pttok, _ = PT.next()
        if kind == "d":
            scale_ap = ksc_d[:, kt, h * 2 + mp:h * 2 + mp + 1]
            m = kt - 4 * qc
            if -1 <= m <= 4:
                bt, bttok, _ = btmp.next()
                g0 = (4 - m) * 128
                P.op("dve", I("scalar_tensor_tensor", out=bt[:], in0=sb_[:], scalar=scale_ap, in1=gtab[:, h, g0:g0 + 512],
                              op0=ALU.mult, op1=ALU.add), reads=[stok, t_ksc, t_g], writes=[bttok])
                P.op("act", I("activation", out=pt[:], in_=bt[:], func=AF.Exp, bias=cvc(CV_NCD), scale=1.0),
                     reads=[bttok, t_cv], writes=[pttok])
            else:
                side = 0 if m < -1 else 1
                P.op("act", I("activation", out=pt[:], in_=sb_[:], func=AF.Exp,
                              bias=gfar[:, h * 2 + side:h * 2 + side + 1], scale=scale_ap),
                     reads=[stok, t_ksc, t_lam], writes=[pttok])
        else:
            scale_ap = ksc_m[:, kt, h:h + 1]
            P.op("act", I("activation", out=pt[:], in_=sb_[:], func=AF.Exp, bias=cvc(CV_NCM), scale=scale_ap),
                 reads=[stok, t_ksc, t_cv], writes=[pttok])
        return pt, pttok

    def emit_av(i, pt, pttok):
        si, kt = steps[i]
        kind, h, qc, mp = specs[si]
        V, Vtok = head_data(kind, h)[2]
        ob, otok = Ot[si % 2]
        smb, smtok = Sm[si % 2]
        P.op("pe", I("matmul", ob[:], lhsT=V[:, kt, 0:128], rhs=pt[:], start=(kt == 0), stop=(kt == 31)),
             reads=[pttok, Vtok], writes=[otok], sig=(kt == 31))
        P.op("pe", I("matmul", smb[0:1, :], lhsT=ones[:, 0:1], rhs=pt[:], start=(kt == 0), stop=(kt == 31)),
             reads=[pttok, t_const], writes=[smtok], sig=True)

    def evac(si):
        kind, h, qc, mp = specs[si]
        ob, otok = Ot[si % 2]
        smb, smtok = Sm[si % 2]
        qs = slice(qc * 512, (qc + 1) * 512)
        st = {}

        def s1():
            st["rec"] = recs.next()
            rc, rctok, _ = st["rec"]
            P.op("dve", I("reciprocal", out=rc[0:1, :], in_=smb[0:1, :]), reads=[smtok], writes=[rctok])

        def s2():
            rc, rctok, _ = st["rec"]
            P.op("pe", I("matmul", Eb[:], lhsT=ones_f[0:1, :], rhs=rc[0:1, :], start=True, stop=True),
                 reads=[rctok, t_lam], writes=[Etok])

        def s3():
            bc, bctok, _ = bcs.next()
            P.op("act", I("activation", out=bc[:], in_=Eb[:], func=AF.Copy), reads=[Etok], writes=[bctok])
            if kind == "m":
                P.op("dve", I("tensor_tensor", out=mixT[:, 8 + h, qs], in0=ob[:], in1=bc[:], op=ALU.mult),
                     reads=[otok, bctok], writes=[t_mix[qc]])
            elif mp == 0:
                o1s[(h, qc)] = o1T.next()
                o1, o1tok, _ = o1s[(h, qc)]
                P.op("dve", I("tensor_tensor", out=o1[:], in0=ob[:], in1=bc[:], op=ALU.mult),
                     reads=[otok, bctok], writes=[o1tok])
            else:
                o1, o1tok, _ = o1s[(h, qc)]
                st["cm"] = cmb.next()
                cm, cmtok, _ = st["cm"]
                st["sq"] = sqb.next()
                sq, sqtok, _ = st["sq"]
                P.op("dve", I("tensor_tensor", out=cm[:], in0=ob[:], in1=bc[:], op=ALU.mult),
                     reads=[otok, bctok], writes=[cmtok])
                P.op("dve", I("scalar_tensor_tensor", out=cm[:], in0=cm[:], scalar=NEGLAM, in1=o1[:],
                              op0=ALU.mult, op1=ALU.add), reads=[o1tok, t_lam], writes=[cmtok])
                P.op("dve", I("tensor_tensor", out=sq[:], in0=cm[:], in1=cm[:], op=ALU.mult),
                     reads=[cmtok], writes=[sqtok])

        def s4():
            sq, sqtok, _ = st["sq"]
            P.op("pe", I("matmul", Eb[:], lhsT=ones, rhs=sq[:], start=True, stop=True),
                 reads=[sqtok, t_const], writes=[Etok])

        def s5():
            cm, cmtok, _ = st["cm"]
            ln_, lntok, _ = lnt.next()
            rstd_ops(Eb[:], ln_[:, 0, :], ln_[:, 1, :], 1.0 / 128, CV_LN08, [Etok], lntok, lntok)
            P.op("dve", I("scalar_tensor_tensor", out=mixT[:, h, qs], in0=cm[:], scalar=gc(14), in1=ln_[:, 1, :],
                          op0=ALU.mult, op1=ALU.mult), reads=[cmtok, lntok, t_const], writes=[t_mix[qc]])

        if kind == "d" and mp == 1:
            return [s1, s2, s3, s4, s5]
        return [s1, s2, s3]

    pend = []
    live = {}
    nsteps = len(steps)
    GAP = 3
    for i in range(min(LA, nsteps)):
        live[i] = emit_qk(i)
    for i in range(nsteps):
        if i + LA < nsteps:
            live[i + LA] = emit_qk(i + LA)
        pt, pttok = live.pop(i)
        emit_av(i, pt, pttok)
        si, kt = steps[i]
        if kt == 31:
            for n_, f in enumerate(evac(si)):
                pend.append((i + n_ * GAP, f))
            pend.sort(key=lambda x: x[0])
        while pend and pend[0][0] <= i:
            pend.pop(0)[1]()
    for _, f in pend:
        f()
    A.release(m2)

    after = P.all_events()
    m3 = A.mark()
    wo = A.alloc("wo", [128, 16, D], BF16)
    gt1 = A.alloc("gt1", [128, D], F32)
    t_wo = Tok("wo")
    s_wo = P.dsem("d_wo")
    dmy = Tok("dummy3", after)
    for i in range(4):
        P.dma("pool", wo[:, i * 4:(i + 1) * 4, :], w_out[i * 512:(i + 1) * 512, :].rearrange("(k p) n -> p k n", p=128),
              s_wo, acc_writes=[t_wo], reads=[dmy])
    s_gt1 = P.dsem("d_gt1")
    P.dma("sp", gt1[:], mod_scr[:, 2 * D:3 * D].partition_broadcast(128), s_gt1, reads=[t_modscr, dmy], acc_writes=[t_wo])
    x3 = Ring(P, A, "x3", 1, [128, D], F32, after)
    x1r = Ring(P, A, "x1r", 2, [128, D], F32, after)
    st3 = Ring(P, A, "st3", 2, [128, 4], F32, after, dma=False)
    xn3 = Ring(P, A, "xn3", 2, [128, D], BF16, after, dma=False)
    h2s = Ring(P, A, "h2s", 2, [128, 16, 128], BF16, after)
    tp3 = [(banks[0][:].bitcast(BF16)[:, 0:512], PTok("tp3a", after)),
           (banks[5][:].bitcast(BF16)[:, 0:512], PTok("tp3b", after))]
    yb = [(banks[1 + i], PTok("y%d" % i, after)) for i in range(4)]

    if STOP_AFTER >= 4:
        for tt in range(16):
            xt, xtok, xsem = x3.next()
            P.dma("sp", xt[:], xk[tt * 128:(tt + 1) * 128, :], xsem, writes=[xtok])
            x1, x1tok, x1sem = x1r.next()
            for cb in range(4):
                yk, ytok = yb[cb]
                for fc in range(16):
                    P.op("pe", I("matmul", yk[:], lhsT=mixT[:, fc, tt * 128:(tt + 1) * 128],
                                 rhs=wo[:, fc, cb * 512:(cb + 1) * 512], start=(fc == 0), stop=(fc == 15)),
                         reads=[t_mix[tt // 4], t_wo], writes=[ytok], sig=(fc == 15))
                P.op("dve", I("tensor_tensor", out=x1[:, cb * 512:(cb + 1) * 512], in0=yk[:], in1=gt1[:, cb * 512:(cb + 1) * 512],
                              op=ALU.mult), reads=[ytok, t_wo], writes=[x1tok])
                P.op("dve", I("tensor_tensor", out=x1[:, cb * 512:(cb + 1) * 512], in0=x1[:, cb * 512:(cb + 1) * 512],
                              in1=xt[:, cb * 512:(cb + 1) * 512], op=ALU.add), reads=[xtok], writes=[x1tok])
            P.dma("sp", x1_scr[tt * 128:(tt + 1) * 128, :], x1[:], x1sem, reads=[x1tok], acc_writes=[t_scr["x1"]])
            st, sttok, _ = st3.next()
            xn, xntok, _ = xn3.next()
            P.op("act", I("activation", out=xn[:], in_=x1[:], func=AF.Square, accum_out=st[:, 0:1]),
                 reads=[x1tok], writes=[xntok, sttok])
            rstd_ops(st[:, 0:1], st[:, 1:2], st[:, 2:3], 1.0 / D, CV_ZERO, [sttok], sttok, sttok)
            P.op("dve", I("tensor_scalar", out=xn[:], in0=x1[:], scalar1=st[:, 2:3], scalar2=None, op0=ALU.mult),
                 reads=[x1tok, sttok], writes=[xntok])
            hs, hstok, hssem = h2s.next()
            for g in range(4):
                tp, tptok = tp3[g % 2]
                for j in range(4):
                    dc = g * 4 + j
                    P.op("pe", I("transpose", tp[:, j * 128:(j + 1) * 128], xn[:, dc * 128:(dc + 1) * 128], ident),
                         reads=[xntok, t_const], writes=[tptok], sig=(j == 3))
                for j in range(4):
                    dc = g * 4 + j
                    P.op("dve", I("tensor_scalar", out=hs[:, dc, :], in0=tp[:, j * 128:(j + 1) * 128],
                                  scalar1=A2T[:, dc:dc + 1], scalar2=B2T[:, dc:dc + 1], op0=ALU.mult, op1=ALU.add),
                         reads=[tptok, t_mod], writes=[hstok])
            P.dma("sp", h2T_scr[:, :, tt * 128:(tt + 1) * 128], hs[:], hssem, reads=[hstok], acc_writes=[t_scr["h2T"]])
    A.release(m_mix)

    after = P.all_events()
    m4 = A.mark()
    T = FFN_T
    NTH = T // 512
    FB = 256
    gt2 = A.alloc("gt2", [128, D], F32)
    t_gt2 = Tok("gt2")
    s_gt2 = P.dsem("d_gt2")
    dmy = Tok("dummy4", after)
    P.dma("sp", gt2[:], mod_scr[:, 5 * D:6 * D].partition_broadcast(128), s_gt2, reads=[t_modscr, dmy], writes=[t_gt2])
    actT = A.alloc("actT", [128, NFC, T], BF16)
    t_act = [Tok("act%d" % i, after) for i in range(NFC)]
    h2b = Ring(P, A, "h2b", 1, [128, 16, T], BF16, after)
    wg = Ring(P, A, "wg", 2, [128, 16, FB], BF16, after)
    wu = Ring(P, A, "wu", 2, [128, 16, FB], BF16, after)
    wd = Ring(P, A, "wd", 6, [128, 11, 512], BF16, after)
    sg = Ring(P, A, "sg", 2, [128, 512], F32, after, dma=False)
    x1p = Ring(P, A, "x1p", 2, [128, 512], F32, after)
    ost = Ring(P, A, "ost", 2, [128, 512], F32, after)
    gb = [(banks[i], PTok("g%d" % i, after)) for i in range(8)]
    gi = [0]

    if STOP_AFTER >= 5:
        for blk in range(SQ // T):
            hb, hbtok, hbsem = h2b.next()
            P.dma("sp", hb[:], h2T_scr[:, :, blk * T:(blk + 1) * T], hbsem, reads=[t_scr["h2T"]], writes=[hbtok])
            for fcb in range(DFF // FB):
                wgt, wgtok, wgsem = wg.next()
                wut, wutok, wusem = wu.next()
                P.dma("pool", wgt[:], w_gate[:, fcb * FB:(fcb + 1) * FB].rearrange("(k p) n -> p k n", p=128),
                      wgsem, writes=[wgtok])
                P.dma("pool", wut[:], w_up[:, fcb * FB:(fcb + 1) * FB].rearrange("(k p) n -> p k n", p=128),
                      wusem, writes=[wutok])
                for fi in range(FB // 128):
                    f = fcb * (FB // 128) + fi
                    for th in range(NTH):
                        gk, gtok = gb[gi[0] % 8]
                        uk, utok = gb[(gi[0] + 1) % 8]
                        gi[0] += 2
                        for (bk_, btok_, wt_, wtok_) in ((gk, gtok, wgt, wgtok), (uk, utok, wut, wutok)):
                            for dc in range(16):
                                P.op("pe", I("matmul", bk_[:], lhsT=wt_[:, dc, fi * 128:(fi + 1) * 128],
                                             rhs=hb[:, dc, th * 512:(th + 1) * 512], start=(dc == 0), stop=(dc == 15)),
                                     reads=[wtok_, hbtok], writes=[btok_], sig=(dc == 15))
                        s_, stok_, _ = sg.next()
                        P.op("act", I("activation", out=s_[:], in_=gk[:], func=AF.Silu), reads=[gtok], writes=[stok_])
                        P.op("dve", I("tensor_tensor", out=actT[:, f, th * 512:(th + 1) * 512], in0=s_[:], in1=uk[:],
                                      op=ALU.mult), reads=[stok_, utok], writes=[t_act[f]])
            for cb in range(4):
                pieces = []
                for q4 in range(4):
                    wdt, wdtok, wdsem = wd.next()
                    P.dma("pool", wdt[:],
                          w_down[q4 * 1408:(q4 + 1) * 1408, cb * 512:(cb + 1) * 512].rearrange("(k p) n -> p k n", p=128),
                          wdsem, writes=[wdtok])
                    pieces.append((wdt, wdtok))
                for tt in range(T // 128):
                    row0 = blk * T + tt * 128
                    yk, ytok = gb[gi[0] % 8]
                    gi[0] += 1
                    for f in range(NFC):
                        wdt, wdtok = pieces[f // 11]
                        P.op("pe", I("matmul", yk[:], lhsT=actT[:, f, tt * 128:(tt + 1) * 128], rhs=wdt[:, f % 11, :],
                                     start=(f == 0), stop=(f == NFC - 1)),
                             reads=[t_act[f], wdtok], writes=[ytok], sig=(f == NFC - 1 or f % 11 == 10))
                    xp, xptok, xpsem = x1p.next()
                    P.dma("sp", xp[:], x1_scr[row0:row0 + 128, cb * 512:(cb + 1) * 512], xpsem,
                          reads=[t_scr["x1"]], writes=[xptok])
                    os_, ostok, ossem = ost.next()
                    P.op("dve", I("tensor_tensor", out=os_[:], in0=yk[:], in1=gt2[:, cb * 512:(cb + 1) * 512], op=ALU.mult),
                         reads=[ytok, t_gt2], writes=[ostok])
                    P.op("dve", I("tensor_tensor", out=os_[:], in0=os_[:], in1=xp[:], op=ALU.add),
                         reads=[xptok], writes=[ostok])
                    P.dma("sp", out_d[row0:row0 + 128, cb * 512:(cb + 1) * 512], os_[:], ossem, reads=[ostok])
    A.release(m4)

    P.final_wait("sp")
    P.emit()
    print("[kernel] sbuf peak %d / %d, waits %d, ops %s" % (
        A.peak, A.hi, P.n_waits, {k: len(v.ops) for k, v in P.E.items()}), flush=True)
    return nc


def _t5_bucket_np(rel):
    nb, me = 16, 8
    base = np.where(rel > 0, nb, 0)
    n = np.abs(rel)
    nf = np.maximum(n, 1).astype(np.float32)
    large = me + (np.log(nf / np.float32(me)) / np.float32(math.log(128 / 8)) * np.float32(nb - me)).astype(np.int32)
    large = np.minimum(large, nb - 1)
    return (base + np.where(n < me, n, large)).astype(np.int64)


def _prep_inputs(inp):
    f = lambda a: np.ascontiguousarray(np.asarray(a, dtype=np.float32))
    x = f(inp["x"]); c = f(inp["c"]); rel_bias = f(inp["rel_bias"])
    w_in = f(inp["w_in"])[0]
    w_q_b = f(inp["w_q_b"])[0]
    w_kv_b = f(inp["w_kv_b"])[0]
    g_q_mla = f(inp["g_q_mla"])[0]; g_k_mla = f(inp["g_k_mla"])[0]
    g_q_a = f(inp["g_q_a"])[0]; g_kv_a = f(inp["g_kv_a"])[0]
    q_d = w_in[:, 0:1024]; k_d = w_in[:, 1024:2048]; v_d = w_in[:, 2048:3072]
    cq = w_in[:, 3072:3584]; ckv = w_in[:, 3584:3840]; kpe = w_in[:, 3840:3904]
    perm = (np.arange(64) + 32) % 64
    w_kv = f(np.concatenate([k_d, v_d, ckv, kpe, kpe[:, perm]], axis=1))
    w_q = f(np.concatenate([q_d, cq], axis=1))
    qb = w_q_b.reshape(512, NH, 192)
    w_qb = f(np.concatenate([qb[:, :, 0:128].reshape(512, -1), qb[:, :, 128:192].reshape(512, -1),
                             qb[:, :, 128:192][:, :, perm].reshape(512, -1)], axis=1))
    kvb = w_kv_b.reshape(256, NH, 256)
    w_kvb = f(np.concatenate([kvb[:, :, 0:128].reshape(256, -1), kvb[:, :, 128:256].reshape(256, -1)], axis=1))
    gcols = np.ones((128, 16), np.float32)
    gcols[:, 0] = np.tile(f(inp["g_q_diff"])[0], 2)
    gcols[:, 1] = np.tile(f(inp["g_k_diff"])[0], 2)
    gcols[:, 2:6] = g_q_a.reshape(4, 128).T
    gcols[:, 6:8] = g_kv_a.reshape(2, 128).T
    gcols[:, 8] = g_q_mla[0:128]
    gcols[0:64, 9] = g_q_mla[128:192]
    gcols[0:64, 10] = g_q_mla[128:192][perm]
    gcols[:, 11] = g_k_mla[0:128]
    gcols[0:64, 12] = g_k_mla[128:192]
    gcols[0:64, 13] = g_k_mla[128:192][perm]
    gcols[:, 14] = f(inp["g_subln"])[0]
    gsub_bc = f(np.broadcast_to(f(inp["g_subln"])[0][None, :], (128, 128)))
    lam_bc = f(np.broadcast_to(f(inp["lambda_vecs"])[0].reshape(1, 256), (128, 256)))
    cmat = np.zeros((128, 512), np.float32)
    cmat[:, 0:128] = np.eye(128)
    cmat[:, 128:256] = 1.0
    cmat[0:64, 256:320] = 1.0
    cmat[64:128, 320:384] = 1.0
    cmat[0:64, 384] = 1.0
    cmat[64:128, 385] = 1.0
    inv = (1.0 / (np.float32(10000.0) ** (np.arange(0, 64, 2, dtype=np.float32) / np.float32(64)))).astype(np.float32)
    shared = dict(
        w_ada=f(inp["w_ada"])[0], b_ada=f(inp["b_ada"]).reshape(1, -1),
        g1T=f(f(inp["g_norm1"])[0].reshape(16, 128).T), g2T=f(f(inp["g_norm2"])[0].reshape(16, 128).T),
        w_kv=w_kv, w_q=w_q, w_qb=w_qb, w_kvb=w_kvb,
        w_out=f(inp["w_out"])[0], w_gate=f(inp["w_gate"])[0], w_up=f(inp["w_up"])[0], w_down=f(inp["w_down"])[0],
        gcols=gcols, gsub_bc=gsub_bc, lam_bc=lam_bc, cmat=cmat,
    )
    maps = []
    ii = np.arange(128)[:, None]
    tt = np.arange(GW)[None, :]
    for core in range(8):
        b, half = core // 2, core % 2
        xb = x[b]
        pos = np.arange(S, dtype=np.float32)
        if half == 1:
            xb = xb[::-1]
            pos = pos[::-1]
        ang = (pos[:, None] * inv[None, :]).astype(np.float32)
        cos = np.cos(ang).astype(np.float32).T
        sin = np.sin(ang).astype(np.float32).T
        cos_t = f(np.concatenate([cos, cos], axis=0))
        sin_t = f(np.concatenate([-sin, sin], axis=0))
        sgn = 1 if half == 0 else -1
        dloc = ii - tt + 512
        bidx = _t5_bucket_np((sgn * dloc).astype(np.int32))
        gtab = f(np.transpose(rel_bias[bidx], (0, 2, 1)))
        bneg = _t5_bucket_np(np.array([sgn * -1000], np.int32))[0]
        bpos = _t5_bucket_np(np.array([sgn * 1000], np.int32))[0]
        gfar = np.zeros((128, 16), np.float32)
        gfar[:, 0::2] = rel_bias[bneg][None, :]
        gfar[:, 1::2] = rel_bias[bpos][None, :]
        m = dict(shared)
        m.update(xk=f(xb), cT=f(c[b].reshape(16, 128).T), cos_t=cos_t, sin_t=sin_t, gtab=gtab, gfar=gfar)
        maps.append(m)
    return maps


_NC_CACHE = {}


def kernel(**inputs):
    maps = _prep_inputs(inputs)
    if "nc" not in _NC_CACHE:
        _NC_CACHE["nc"] = build_program()
    nc = _NC_CACHE["nc"]
    res = run_bass_kernel_spmd(nc, maps, core_ids=list(range(8)))
    out = np.zeros((B, S, D), np.float32)
    for core in range(8):
        b, half = core // 2, core % 2
        o = np.asarray(res.results[core]["out"], dtype=np.float32)
        if half == 0:
            out[b, 0:SQ] = o
        else:
            out[b, SQ:S] = o[::-1]
    if DEBUG:
        kernel.last_results = res.results
    return out
```

```python
import math
import numpy as np
import concourse.bass as bass
import concourse.mybir as mybir
from concourse.bass_utils import run_bass_kernel_spmd

F32 = mybir.dt.float32
BF16 = mybir.dt.bfloat16
AF = mybir.ActivationFunctionType
ALU = mybir.AluOpType

D = 2048
S = 4096
SQ = 2048
B = 4
NH = 8
DFF = 5632
NFC = DFF // 128
EPS = 1e-6
LAMBDA_INIT = 0.2
C_DIFF = 8.0
C_MLA = 14.0
FFN_T = 512
GW = 1152
ST = 256
NT = ST // 128

DEBUG = False
STOP_AFTER = 99
SUB = 99
AP_MASK = 31
NHA = 8
NSTL = 99


class Ev:
    __slots__ = ("sem", "val", "eng")

    def __init__(self, sem, val, eng):
        self.sem, self.val, self.eng = sem, val, eng


class Tok:
    __slots__ = ("name", "w", "r", "excl")

    def __init__(self, name, after=(), excl=False):
        self.name = name
        self.w = {}
        self.r = {}
        self.excl = excl
        for i, ev in enumerate(after):
            self.w[("init", i)] = ev


def PTok(name, after=()):
    return Tok(name, after, excl=True)


class DSem:
    _n = [0]

    def __init__(self, nc, name):
        DSem._n[0] += 1
        self.h = nc.alloc_semaphore("%s_%d" % (name, DSem._n[0]))
        self.count = 0


class EngState:
    LIMIT = 30000

    def __init__(self, nc, name, same_sync):
        self.nc, self.name, self.same_sync = nc, name, same_sync
        self.nsem = 0
        self.sem = None
        self.count = 0
        self.known = {}
        self.pending = []
        self.ops = []
        self.last = None
        self._newsem()

    def _newsem(self):
        self.sem = self.nc.alloc_semaphore("e_%s_%d" % (self.name, self.nsem))
        self.nsem += 1
        self.count = 0

    def new_event(self, sig):
        if not sig:
            ev = Ev(None, None, self.name)
            self.pending.append(ev)
            return ev
        if self.count >= self.LIMIT:
            self._newsem()
        self.count += 1
        ev = Ev(self.sem, self.count, self.name)
        for p in self.pending:
            p.sem, p.val = ev.sem, ev.val
        self.pending = []
        self.last = ev
        return ev


class Prog:
    def __init__(self, nc):
        self.nc = nc
        self.E = {
            "pe": EngState(nc, "pe", False),
            "act": EngState(nc, "act", True),
            "dve": EngState(nc, "dve", True),
            "pool": EngState(nc, "pool", True),
            "sp": EngState(nc, "sp", True),
        }
        self.dsems = []
        self.n_waits = 0

    def dsem(self, name):
        s = DSem(self.nc, name)
        self.dsems.append(s)
        return s

    def _need(self, E, eng, ev, waits, is_dma_issue):
        if ev is None:
            return
        if ev.sem is None:
            if ev.eng == eng and not is_dma_issue:
                return
            raise RuntimeError("dependency on unsignaled op (%s <- %s)" % (eng, ev.eng))
        if ev.eng == eng and not E.same_sync and not is_dma_issue:
            return
        k = ev.sem.num
        if E.known.get(k, 0) >= ev.val:
            return
        E.known[k] = ev.val
        waits.append((ev.sem, ev.val))

    def _deps(self, E, eng, reads, writes, is_dma_issue=False):
        waits = []
        for t in reads:
            for ev in t.w.values():
                self._need(E, eng, ev, waits, is_dma_issue)
            if t.excl:
                for k, ev in t.r.items():
                    if k != eng:
                        self._need(E, eng, ev, waits, is_dma_issue)
        for t in writes:
            for ev in t.w.values():
                self._need(E, eng, ev, waits, is_dma_issue)
            for ev in t.r.values():
                self._need(E, eng, ev, waits, is_dma_issue)
        self.n_waits += len(waits)
        return waits

    def op(self, eng, fn, reads=(), writes=(), sig=True):
        E = self.E[eng]
        waits = self._deps(E, eng, reads, writes)
        ev = E.new_event(sig)
        for t in reads:
            t.r[eng] = ev
        for t in writes:
            t.w = {eng: ev}
            t.r = {}
        E.ops.append((waits, fn, ("inc", ev.sem) if sig else None))
        return ev

    def dma(self, q, out, in_, sem, reads=(), writes=(), acc_writes=(), **kw):
        E = self.E[q]
        waits = self._deps(E, q, reads, writes, is_dma_issue=True)
        sem.count += 16
        ev = Ev(sem.h, sem.count, None)
        key = ("dma", sem.h.num)
        for t in reads:
            t.r[key] = ev
        for t in writes:
            t.w = {key: ev}
            t.r = {}
        for t in acc_writes:
            t.w[key] = ev
        E.ops.append((waits, (I("dma_start", out=out, in_=in_, **kw)), ("dma", sem.h)))
        return ev

    def all_events(self):
        evs = []
        for E in self.E.values():
            if E.pending:
                raise RuntimeError("pending unsignaled ops on %s at barrier" % E.name)
            if E.last is not None:
                evs.append(E.last)
        for s in self.dsems:
            if s.count:
                evs.append(Ev(s.h, s.count, None))
        return evs

    def final_wait(self, eng="sp"):
        E = self.E[eng]
        waits = []
        for ev in self.all_events():
            if ev.eng == eng:
                continue
            k = ev.sem.num
            if E.known.get(k, 0) >= ev.val:
                continue
            E.known[k] = ev.val
            waits.append((ev.sem, ev.val))
        E.ops.append((waits, None, None))

    def emit(self):
        nc = self.nc
        hooks = {"pe": "tensor", "act": "scalar", "dve": "vector", "pool": "gpsimd", "sp": "sync"}
        with nc.Block() as block:
            for name, attr in hooks.items():
                ops = self.E[name].ops

                def body(e, ops=ops):
                    for waits, fn, post in ops:
                        for (s, v) in waits:
                            e.wait_ge(s, v)
                        if fn is None:
                            continue
                        name_, args_, kw_ = fn
                        ins = getattr(e, name_)(*args_, **kw_)
                        if post is not None:
                            if post[0] == "inc":
                                ins.then_inc(post[1], 1)
                            else:
                                ins.then_inc(post[1], 16)

                getattr(block, attr)(body)


class Arena:
    def __init__(self, nc, lo=16512, hi=229376):
        self.nc, self.off, self.hi = nc, lo, hi
        self.n = 0
        self.peak = lo

    def alloc(self, name, shape, dt):
        nb = int(np.prod(shape[1:])) * (4 if dt == F32 else 2)
        nb = (nb + 63) // 64 * 64
        if self.off + nb > self.hi:
            raise RuntimeError("SBUF arena overflow allocating %s (%d + %d > %d)" % (name, self.off, nb, self.hi))
        self.n += 1
        t = self.nc.alloc_sbuf_tensor_at("%s_%d" % (name, self.n), list(shape), dt, offset=self.off)
        self.off += nb
        self.peak = max(self.peak, self.off)
        return t

    def mark(self):
        return self.off

    def release(self, m):
        self.off = m


class Ring:
    def __init__(self, P, arena, name, n, shape, dt, after=(), dma=True):
        self.n = n
        self.t = [arena.alloc("%s%d" % (name, i), shape, dt) for i in range(n)]
        self.tok = [Tok("%s%d" % (name, i), after) for i in range(n)]
        self.sem = [P.dsem("d_%s%d" % (name, i)) for i in range(n)] if dma else None
        self.i = -1

    def next(self):
        self.i += 1
        k = self.i % self.n
        return self.t[k], self.tok[k], (self.sem[k] if self.sem else None)


def I(name, *args, **kw):
    return (name, args, kw)


def build_program():
    nc = bass.Bass("TRN2", target_bir_lowering=False)
    P = Prog(nc)
    A = Arena(nc)

    def din(name, shape, dt=F32):
        return nc.dram_tensor(name, list(shape), dt, kind="ExternalInput").ap()

    def dscr(name, shape, dt):
        return nc.dram_tensor(name, list(shape), dt, kind=("ExternalOutput" if DEBUG else "Internal")).ap()

    xk = din("xk", [S, D])
    cT_d = din("cT", [128, 16])
    w_ada = din("w_ada", [D, 6 * D])
    b_ada = din("b_ada", [1, 6 * D])
    g1T_d = din("g1T", [128, 16])
    g2T_d = din("g2T", [128, 16])
    w_kv = din("w_kv", [D, 2432])
    w_q = din("w_q", [D, 1536])
    w_qb = din("w_qb", [512, 2048])
    w_kvb = din("w_kvb", [256, 2048])
    w_out = din("w_out", [D, D])
    w_gate = din("w_gate", [D, DFF])
    w_up = din("w_up", [D, DFF])
    w_down = din("w_down", [DFF, D])
    gcols_d = din("gcols", [128, 16])
    gsub_d = din("gsub_bc", [128, 128])
    lam_d = din("lam_bc", [128, 256])
    cos_d = din("cos_t", [64, S])
    sin_d = din("sin_t", [64, S])
    gtab_d = din("gtab", [128, NH, GW])
    gfar_d = din("gfar", [128, 16])
    cmat_d = din("cmat", [128, 512])
    out_d = nc.dram_tensor("out", [SQ, D], F32, kind="ExternalOutput").ap()

    mod_scr = dscr("mod_scr", [1, 6 * D], F32)
    KdT = dscr("KdT", [NH, 128, S], BF16)
    QdT = dscr("QdT", [NH, 128, SQ], BF16)
    Vd = dscr("Vd", [NH, 128, 32, 130], BF16)
    KmT = dscr("KmT", [NH, 128, S], BF16)
    QmnT = dscr("QmnT", [NH, 128, SQ], BF16)
    QmrT = dscr("QmrT", [NH, 64, SQ], BF16)
    Vm = dscr("Vm", [NH, 128, 32, 130], BF16)
    x1_scr = dscr("x1_scr", [SQ, D], F32)
    h2T_scr = dscr("h2T_scr", [128, 16, SQ], BF16)
    dbg_ksc = nc.dram_tensor("dbg_ksc", [128, 32 * 24 + 64 * 32], F32, kind="ExternalOutput").ap() if DEBUG else None

    banks = [nc.alloc_psum_tensor("bank%d" % i, [128, 512], F32) for i in range(8)]

    cmat = A.alloc("cmat", [128, 512], BF16)
    ident = cmat[:, 0:128]
    ones = cmat[:, 128:256]
    bones = cmat[:, 256:384]
    sel2 = cmat[:, 384:386]
    gcols = A.alloc("gcols", [128, 16], F32)
    cv = A.alloc("cv", [128, 8], F32)
    A1T = A.alloc("A1T", [128, 16], F32)
    B1T = A.alloc("B1T", [128, 16], F32)
    A2T = A.alloc("A2T", [128, 16], F32)
    B2T = A.alloc("B2T", [128, 16], F32)
    ksc_d = A.alloc("ksc_d", [128, 32, 16], F32)
    ksc_m = A.alloc("ksc_m", [128, 32, 8], F32)
    kropeT = A.alloc("kropeT", [128, S], BF16)
    t_const = Tok("const")
    t_cv = Tok("cv")
    t_mod = Tok("mod")
    t_ksc = Tok("ksc")
    t_krope = Tok("krope")
    s_const = P.dsem("d_const")

    CV_EPS, CV_LN8, CV_LN192, CV_LN08, CV_NCD, CV_NCM, CV_ZERO = 0, 1, 2, 3, 4, 5, 6
    cvals = [EPS, math.log(0.125), math.log(192 ** -0.5), math.log(1.0 - LAMBDA_INIT), -C_DIFF, -C_MLA, 0.0, 1.0]

    P.dma("pool", cmat[:], cmat_d, s_const, acc_writes=[t_const])
    s_const2 = P.dsem("d_const2")
    P.dma("sp", gcols[:], gcols_d, s_const2, acc_writes=[t_const])
    for i, v in enumerate(cvals):
        P.op("dve", I("memset", cv[:, i:i + 1], float(v)), writes=[t_cv])
    P.op("dve", I("memset", kropeT[64:128, :], 0.0), writes=[t_krope])

    def cvc(i, n=128):
        return cv[0:n, i:i + 1]

    def gc(i, n=128):
        return gcols[0:n, i:i + 1]

    G_QD, G_KD, G_QA, G_KVA, G_QMN, G_QMR, G_QMP, G_KMN, G_KMR, G_KMP = 0, 1, 2, 6, 8, 9, 10, 11, 12, 13

    def rstd_ops(src, tmp, dst, inv_n, ln_bias_col, reads, tmp_tok, dst_tok, npart=128):
        P.op("act", I("activation", out=tmp, in_=src, func=AF.Ln, scale=inv_n, bias=cvc(CV_EPS, npart)),
             reads=list(reads) + [t_cv], writes=[tmp_tok])
        P.op("act", I("activation", out=dst, in_=tmp, func=AF.Exp, scale=-0.5, bias=cvc(ln_bias_col, npart)),
             reads=[t_cv, tmp_tok], writes=[dst_tok])

    m0 = A.mark()
    cT = A.alloc("cT", [128, 16], F32)
    cact = A.alloc("cact", [128, 16], BF16)
    g1T = A.alloc("g1T", [128, 16], F32)
    g2T = A.alloc("g2T", [128, 16], F32)
    modT = A.alloc("modT", [128, 96], F32)
    modrow = A.alloc("modrow", [1, 6 * D], F32)
    brow = A.alloc("brow", [1, 6 * D], F32)
    wr = Ring(P, A, "wada", 2, [128, 16, 1024], BF16)
    t_c = Tok("c")
    t_cact = Tok("cact")
    t_brow = Tok("brow")
    t_modrow = Tok("modrow")
    s_p0 = P.dsem("d_p0")
    s_p0b = P.dsem("d_p0b")
    s_p0c = P.dsem("d_p0c")
    P.dma("sp", cT[:], cT_d, s_p0, acc_writes=[t_c])
    P.dma("sp", g1T[:], g1T_d, s_p0, acc_writes=[t_c])
    P.dma("sp", g2T[:], g2T_d, s_p0, acc_writes=[t_c])
    P.dma("sp", brow[:], b_ada, s_p0b, writes=[t_brow])
    P.op("act", I("activation", out=cact[:], in_=cT[:], func=AF.Silu), reads=[t_c], writes=[t_cact])
    pb = [PTok("p0bank0"), PTok("p0bank1")]
    for nb in range(12):
        wt, wtok, wsem = wr.next()
        P.dma("pool", wt[:], w_ada[:, nb * 1024:(nb + 1) * 1024].rearrange("(kc p) n -> p kc n", p=128),
              wsem, writes=[wtok])
        for half in range(2):
            i = nb * 2 + half
            bk, btok = banks[i % 2], pb[i % 2]
            for kc in range(16):
                P.op("pe", I("matmul", bk[0:1, :], lhsT=cact[:, kc:kc + 1], rhs=wt[:, kc, half * 512:(half + 1) * 512],
                             start=(kc == 0), stop=(kc == 15)),
                     reads=[wtok, t_cact], writes=[btok], sig=(kc == 15))
            c0 = i * 512
            P.op("dve", I("tensor_tensor", out=modrow[0:1, c0:c0 + 512], in0=bk[0:1, :], in1=brow[0:1, c0:c0 + 512],
                          op=ALU.add), reads=[btok, t_brow], writes=[t_modrow])
    t_modscr = Tok("modscr")
    P.dma("sp", mod_scr, modrow[:], s_p0c, reads=[t_modrow], writes=[t_modscr])
    t_modT = Tok("modT")
    s_p0d = P.dsem("d_p0d")
    for j0 in range(0, 96, 16):
        P.dma("sp", modT[:, j0:j0 + 16],
              mod_scr[:, j0 * 128:(j0 + 16) * 128].rearrange("o (j p) -> p (o j)", p=128),
              s_p0d, reads=[t_modscr], acc_writes=[t_modT], allow_slow_non_contiguous=True)
    P.op("dve", I("scalar_tensor_tensor", out=A1T[:], in0=modT[:, 16:32], scalar=1.0, in1=g1T[:],
                  op0=ALU.add, op1=ALU.mult), reads=[t_modT, t_c], writes=[t_mod])
    P.op("dve", I("tensor_copy", out=B1T[:], in_=modT[:, 0:16]), reads=[t_modT], writes=[t_mod])
    P.op("dve", I("scalar_tensor_tensor", out=A2T[:], in0=modT[:, 64:80], scalar=1.0, in1=g2T[:],
                  op0=ALU.add, op1=ALU.mult), reads=[t_modT, t_c], writes=[t_mod])
    P.op("dve", I("tensor_copy", out=B2T[:], in_=modT[:, 48:64]), reads=[t_modT], writes=[t_mod])
    A.release(m0)

    def norm_supertile(env, xrows0, AT, BT):
        hT, hT_tok, _ = env["hT"].next()
        for tt in range(NT):
            xt, xtok, xsem = env["x"].next()
            r = xrows0 + tt * 128
            P.dma("sp", xt[:], xk[r:r + 128, :], xsem, writes=[xtok])
            st, sttok, _ = env["stat"].next()
            xn, xntok, _ = env["xn"].next()
            P.op("act", I("activation", out=xn[:], in_=xt[:], func=AF.Square, accum_out=st[:, 0:1]),
                 reads=[xtok], writes=[xntok, sttok])
            rstd_ops(st[:, 0:1], st[:, 1:2], st[:, 2:3], 1.0 / D, CV_ZERO, [sttok], sttok, sttok)
            P.op("dve", I("tensor_scalar", out=xn[:], in0=xt[:], scalar1=st[:, 2:3], scalar2=None, op0=ALU.mult),
                 reads=[xtok, sttok], writes=[xntok])
            for g in range(4):
                tp, tptok = env["tp"][g % 2]
                for j in range(4):
                    dc = g * 4 + j
                    P.op("pe", I("transpose", tp[:, j * 128:(j + 1) * 128], xn[:, dc * 128:(dc + 1) * 128], ident),
                         reads=[xntok, t_const], writes=[tptok], sig=(j == 3))
                for j in range(4):
                    dc = g * 4 + j
                    P.op("dve", I("tensor_scalar", out=hT[:, dc, tt * 128:(tt + 1) * 128], in0=tp[:, j * 128:(j + 1) * 128],
                                  scalar1=AT[:, dc:dc + 1], scalar2=BT[:, dc:dc + 1], op0=ALU.mult, op1=ALU.add),
                         reads=[tptok, t_mod], writes=[hT_tok])
        return hT, hT_tok

    def proj_fm(wt, wtok, c0, ncol, rhsT, rhs_tok, nk, out_bank, out_tok):
        for k in range(nk):
            P.op("pe", I("matmul", out_bank[0:ncol, 0:ST], lhsT=wt[:, k, c0:c0 + ncol], rhs=rhsT[:, k, :],
                         start=(k == 0), stop=(k == nk - 1)),
                 reads=[wtok, rhs_tok], writes=[out_tok], sig=(k == nk - 1))

    def make_env(after):
        env = {}
        env["x"] = Ring(P, A, "x", 2, [128, D], F32, after)
        env["stat"] = Ring(P, A, "stat", 2, [128, 4], F32, after, dma=False)
        env["xn"] = Ring(P, A, "xn", 2, [128, D], BF16, after, dma=False)
        env["hT"] = Ring(P, A, "hT", 2, [128, 16, ST], BF16, after, dma=False)
        env["tp"] = [(banks[0][:].bitcast(BF16)[:, 0:512], PTok("tp0", after)),
                     (banks[1][:].bitcast(BF16)[:, 0:512], PTok("tp1", after))]
        env["mm"] = [(banks[2 + i], PTok("mm%d" % i, after)) for i in range(3)]
        env["mmi"] = 0
        t_ssaux = PTok("ssaux", after)
        env["ss"] = (banks[5], t_ssaux)
        env["aux"] = (banks[5][:, 256:512], t_ssaux)
        env["rp"] = (banks[6], PTok("rp", after))
        env["rpp"] = (banks[7], PTok("rpp", after))
        return env

    def next_mm(env):
        b, t = env["mm"][env["mmi"] % 3]
        env["mmi"] += 1
        return b, t

    t_scr = {k: Tok(k) for k in ["KdT", "QdT", "Vd", "KmT", "QmnT", "QmrT", "Vm", "x1", "h2T"]}

    after = P.all_events()
    mA = A.mark()
    envA = make_env(after)
    wkv = A.alloc("wkv", [128, 16, 2432], BF16)
    wkvb = A.alloc("wkvb", [128, 2, 2048], BF16)
    t_wA = Tok("wA")
    s_wA = P.dsem("d_wA")
    dmy = Tok("dummyA", after)
    for i in range(4):
        P.dma("pool", wkv[:, i * 4:(i + 1) * 4, :], w_kv[i * 512:(i + 1) * 512, :].rearrange("(k p) n -> p k n", p=128),
              s_wA, acc_writes=[t_wA], reads=[dmy])
    P.dma("pool", wkvb[:], w_kvb.rearrange("(k p) n -> p k n", p=128), s_wA, acc_writes=[t_wA])
    tabk = Ring(P, A, "tabk", 2, [64, 2, ST], F32, after)
    kd_st = Ring(P, A, "kd_st", 2, [128, NH, ST], BF16, after)
    vd_st = Ring(P, A, "vd_st", 2, [128, NH, NT, 130], BF16, after)
    sqT = Ring(P, A, "sqT", 2, [128, ST], BF16, after, dma=False)
    csb = Ring(P, A, "csb", 1, [128, 2, ST], F32, after, dma=False)
    sqc = Ring(P, A, "sqc", 1, [128, 2, ST], BF16, after, dma=False)
    cn = Ring(P, A, "cn", 1, [128, 2, ST], BF16, after, dma=False)
    rbc = Ring(P, A, "rbc", 2, [128, 2, ST], F32, after, dma=False)
    sqpe = Ring(P, A, "sqpe", 1, [64, ST], BF16, after, dma=False)
    rt = Ring(P, A, "rt", 2, [64, 2, ST], F32, after, dma=False)
    auxs = Ring(P, A, "auxs", 2, [128, 2, 64], F32, after, dma=False)

    for i in range(2):
        t_, tok_ = vd_st.t[i], vd_st.tok[i]
        P.op("dve", I("memset", t_[:, :, :, 128:129], 1.0), writes=[tok_])
        P.op("dve", I("memset", t_[:, :, :, 129:130], 0.0), writes=[tok_])

    def v_tokmajor(env, lhs, lhs_tok, nk, wt, wtok, c0, dram, dram_tok, kt0):
        vst, vtok, vsem = vd_st.next()
        for tt in range(NT):
            for half in range(2):
                bk, btok = next_mm(env)
                for k in range(nk):
                    P.op("pe", I("matmul", bk[:], lhsT=lhs[:, k, tt * 128:(tt + 1) * 128],
                                 rhs=wt[:, k, c0 + half * 512:c0 + (half + 1) * 512], start=(k == 0), stop=(k == nk - 1)),
                         reads=[lhs_tok, wtok], writes=[btok], sig=(k == nk - 1))
                P.op("act", I("activation", out=vst[:, half * 4:(half + 1) * 4, tt, 0:128],
                              in_=bk[:].rearrange("p (h e) -> p h e", e=128), func=AF.Copy),
                     reads=[btok], writes=[vtok])
        P.dma("sp", dram[:, :, kt0:kt0 + NT, :].rearrange("h p k e -> p h k e"), vst[:], vsem,
              reads=[vtok], acc_writes=[dram_tok])

    if STOP_AFTER >= 1:
        for st_i in range(S // ST):
            tok0 = st_i * ST
            kt0 = st_i * NT
            hT, hTtok = norm_supertile(envA, tok0, A1T, B1T)
            axb, axtok = envA["aux"]
            tb, tbtok, tbsem = tabk.next()
            P.dma("sp", tb[:, 0, :], cos_d[:, tok0:tok0 + ST], tbsem, writes=[tbtok])
            ev = P.dma("sp", tb[:, 1, :], sin_d[:, tok0:tok0 + ST], tbsem, acc_writes=[tbtok])
            tbtok.w = {("dma", tbsem.h.num): ev}
            P.op("dve", I("tensor_scalar", out=tb[:, 0, :], in0=tb[:, 0, :], scalar1=gc(G_KMR, 64), scalar2=None, op0=ALU.mult),
                 reads=[t_const], writes=[tbtok])
            P.op("dve", I("tensor_scalar", out=tb[:, 1, :], in0=tb[:, 1, :], scalar1=gc(G_KMP, 64), scalar2=None, op0=ALU.mult),
                 reads=[t_const], writes=[tbtok])
            if SUB < 0 or st_i >= NSTL:
                continue
            kst, ksttok, kstsem = kd_st.next()
            for h in range(NHA):
                bk, btok = next_mm(envA)
                proj_fm(wkv, t_wA, h * 128, 128, hT, hTtok, 16, bk, btok)
                sq, sqtok, _ = sqT.next()
                if AP_MASK & 1:
                    P.op("act", I("activation", out=sq[:], in_=bk[:, 0:ST], func=AF.Square), reads=[btok], writes=[sqtok])
                if AP_MASK & 2:
                    P.op("dve", I("tensor_scalar", out=kst[:, h, :], in0=bk[:, 0:ST], scalar1=gc(G_KD), scalar2=None, op0=ALU.mult),
                         reads=[btok, t_const], writes=[ksttok])
                for tt in range(NT):
                    c = tt * 16 + h * 2
                    if AP_MASK & 4:
                        P.op("pe", I("matmul", axb[:, c:c + 2], lhsT=sq[:, tt * 128:(tt + 1) * 128], rhs=sel2,
                                     start=True, stop=True),
                             reads=[sqtok, t_const], writes=[axtok], sig=(tt == NT - 1))
            if AP_MASK & 8:
                P.dma("sp", KdT[:, :, tok0:tok0 + ST].rearrange("h p t -> p h t"), kst[:], kstsem,
                      reads=[ksttok], acc_writes=[t_scr["KdT"]])
            au, autok, _ = auxs.next()
            if AP_MASK & 16:
                rstd_ops(axb[:, 0:NT * 16], au[:, 0, 0:NT * 16], ksc_d[:, kt0:kt0 + NT, :].rearrange("p k c -> p (k c)"),
                         1.0 / 64, CV_LN8, [axtok], autok, t_ksc)
            if SUB < 1 or st_i >= NSTL:
                continue
            v_tokmajor(envA, hT, hTtok, 16, wkv, t_wA, 1024, Vd, t_scr["Vd"], kt0)
            if SUB < 2 or st_i >= NSTL:
                continue
            cs, cstok, _ = csb.next()
            sc_, sctok, _ = sqc.next()
            for c in range(2):
                bk, btok = next_mm(envA)
                proj_fm(wkv, t_wA, 2048 + c * 128, 128, hT, hTtok, 16, bk, btok)
                P.op("act", I("activation", out=sc_[:, c, :], in_=bk[:, 0:ST], func=AF.Square), reads=[btok], writes=[sctok])
                P.op("dve", I("tensor_copy", out=cs[:, c, :], in_=bk[:, 0:ST]), reads=[btok], writes=[cstok])
            ssb, sstok = envA["ss"]
            for c in range(2):
                P.op("pe", I("matmul", ssb[:, 0:ST], lhsT=ones, rhs=sc_[:, c, :], start=(c == 0), stop=(c == 1)),
                     reads=[sctok, t_const], writes=[sstok], sig=(c == 1))
            rb, rbtok, _ = rbc.next()
            rstd_ops(ssb[:, 0:ST], rb[:, 0, :], rb[:, 1, :], 1.0 / 256, CV_ZERO, [sstok], rbtok, rbtok)
            cnt, cntok, _ = cn.next()
            for c in range(2):
                P.op("dve", I("scalar_tensor_tensor", out=cnt[:, c, :], in0=cs[:, c, :], scalar=gc(G_KVA + c), in1=rb[:, 1, :],
                              op0=ALU.mult, op1=ALU.mult), reads=[cstok, rbtok, t_const], writes=[cntok])
            if SUB < 3 or st_i >= NSTL:
                continue
            rpb, rptok = envA["rp"]
            rppb, rpptok = envA["rpp"]
            proj_fm(wkv, t_wA, 2304, 64, hT, hTtok, 16, rpb, rptok)
            proj_fm(wkv, t_wA, 2368, 64, hT, hTtok, 16, rppb, rpptok)
            sp_, sptok, _ = sqpe.next()
            P.op("act", I("activation", out=sp_[:], in_=rpb[0:64, 0:ST], func=AF.Square), reads=[rptok], writes=[sptok])
            r_, rtok_, _ = rt.next()
            P.op("dve", I("tensor_tensor", out=r_[:, 0, :], in0=rpb[0:64, 0:ST], in1=tb[:, 0, :], op=ALU.mult),
                 reads=[rptok, tbtok], writes=[rtok_])
            P.op("dve", I("tensor_tensor", out=r_[:, 1, :], in0=rppb[0:64, 0:ST], in1=tb[:, 1, :], op=ALU.mult),
                 reads=[rpptok, tbtok], writes=[rtok_])
            P.op("dve", I("tensor_tensor", out=kropeT[0:64, tok0:tok0 + ST], in0=r_[:, 0, :], in1=r_[:, 1, :], op=ALU.add),
                 reads=[rtok_], writes=[t_krope])
            if SUB < 4 or st_i >= NSTL:
                continue
            kst, ksttok, kstsem = kd_st.next()
            for h in range(NH):
                bk, btok = next_mm(envA)
                proj_fm(wkvb, t_wA, h * 128, 128, cnt, cntok, 2, bk, btok)
                sq, sqtok, _ = sqT.next()
                P.op("act", I("activation", out=sq[:], in_=bk[:, 0:ST], func=AF.Square), reads=[btok], writes=[sqtok])
                P.op("dve", I("tensor_scalar", out=kst[:, h, :], in0=bk[:, 0:ST], scalar1=gc(G_KMN), scalar2=None, op0=ALU.mult),
                     reads=[btok, t_const], writes=[ksttok])
                for tt in range(NT):
                    c = 64 + tt * 8 + h
                    P.op("pe", I("matmul", axb[:, c:c + 1], lhsT=sq[:, tt * 128:(tt + 1) * 128], rhs=ones[:, 0:1],
                                 start=True, stop=False), reads=[sqtok, t_const], writes=[axtok], sig=False)
                    P.op("pe", I("matmul", axb[:, c:c + 1], lhsT=sp_[:, tt * 128:(tt + 1) * 128], rhs=ones[0:64, 0:1],
                                 start=False, stop=True), reads=[sptok, t_const], writes=[axtok], sig=(tt == NT - 1))
            P.dma("sp", KmT[:, :, tok0:tok0 + ST].rearrange("h p t -> p h t"), kst[:], kstsem,
                  reads=[ksttok], acc_writes=[t_scr["KmT"]])
            au, autok, _ = auxs.next()
            rstd_ops(axb[:, 64:64 + NT * 8], au[:, 0, 0:NT * 8], ksc_m[:, kt0:kt0 + NT, :].rearrange("p k c -> p (k c)"),
                     1.0 / 192, CV_LN192, [axtok], autok, t_ksc)
            if SUB < 5 or st_i >= NSTL:
                continue
            v_tokmajor(envA, cnt, cntok, 2, wkvb, t_wA, 1024, Vm, t_scr["Vm"], kt0)
    A.release(mA)

    after = P.all_events()
    mB = A.mark()
    envB = make_env(after)
    wq = A.alloc("wq", [128, 16, 1536], BF16)
    wqb = A.alloc("wqb", [128, 4, 2048], BF16)
    t_wB = Tok("wB")
    s_wB = P.dsem("d_wB")
    dmy = Tok("dummyB", after)
    for i in range(4):
        P.dma("pool", wq[:, i * 4:(i + 1) * 4, :], w_q[i * 512:(i + 1) * 512, :].rearrange("(k p) n -> p k n", p=128),
              s_wB, acc_writes=[t_wB], reads=[dmy])
    P.dma("pool", wqb[:], w_qb.rearrange("(k p) n -> p k n", p=128), s_wB, acc_writes=[t_wB])
    tabq = Ring(P, A, "tabq", 2, [64, 2, ST], F32, after)
    qd_st = Ring(P, A, "qd_st", 2, [128, NH, ST], BF16, after)
    qr_st = Ring(P, A, "qr_st", 2, [64, NH, ST], BF16, after)
    sqTb = Ring(P, A, "sqTb", 2, [128, ST], BF16, after, dma=False)
    sqRb = Ring(P, A, "sqRb", 2, [64, ST], BF16, after, dma=False)
    csbB = Ring(P, A, "csbB", 1, [128, 4, ST], F32, after, dma=False)
    sqcB = Ring(P, A, "sqcB", 1, [128, 4, ST], BF16, after, dma=False)
    cnB = Ring(P, A, "cnB", 1, [128, 4, ST], BF16, after, dma=False)
    rbcB = Ring(P, A, "rbcB", 2, [128, 2, ST], F32, after, dma=False)
    rtB = Ring(P, A, "rtB", 2, [64, 2, ST], F32, after, dma=False)

    if STOP_AFTER >= 2:
        for st_i in range(SQ // ST):
            if st_i >= NSTL:
                continue
            tok0 = st_i * ST
            hT, hTtok = norm_supertile(envB, tok0, A1T, B1T)
            ssb, sstok = envB["ss"]
            tb, tbtok, tbsem = tabq.next()
            P.dma("sp", tb[:, 0, :], cos_d[:, tok0:tok0 + ST], tbsem, writes=[tbtok])
            ev = P.dma("sp", tb[:, 1, :], sin_d[:, tok0:tok0 + ST], tbsem, acc_writes=[tbtok])
            tbtok.w = {("dma", tbsem.h.num): ev}
            P.op("dve", I("tensor_scalar", out=tb[:, 0, :], in0=tb[:, 0, :], scalar1=gc(G_QMR, 64), scalar2=None, op0=ALU.mult),
                 reads=[t_const], writes=[tbtok])
            P.op("dve", I("tensor_scalar", out=tb[:, 1, :], in0=tb[:, 1, :], scalar1=gc(G_QMP, 64), scalar2=None, op0=ALU.mult),
                 reads=[t_const], writes=[tbtok])
            qst, qsttok, qstsem = qd_st.next()
            for h in range(NH):
                bk, btok = next_mm(envB)
                proj_fm(wq, t_wB, h * 128, 128, hT, hTtok, 16, bk, btok)
                sq, sqtok, _ = sqTb.next()
                P.op("act", I("activation", out=sq[:], in_=bk[:, 0:ST], func=AF.Square), reads=[btok], writes=[sqtok])
                P.op("pe", I("matmul", ssb[:, 0:ST], lhsT=bones, rhs=sq[:], start=True, stop=True),
                     reads=[sqtok, t_const], writes=[sstok])
                rb, rbtok, _ = rbcB.next()
                rstd_ops(ssb[:, 0:ST], rb[:, 0, :], rb[:, 1, :], 1.0 / 64, CV_ZERO, [sstok], rbtok, rbtok)
                P.op("dve", I("scalar_tensor_tensor", out=qst[:, h, :], in0=bk[:, 0:ST], scalar=gc(G_QD), in1=rb[:, 1, :],
                              op0=ALU.mult, op1=ALU.mult), reads=[btok, rbtok, t_const], writes=[qsttok])
            P.dma("sp", QdT[:, :, tok0:tok0 + ST].rearrange("h p t -> p h t"), qst[:], qstsem,
                  reads=[qsttok], acc_writes=[t_scr["QdT"]])
            cs, cstok, _ = csbB.next()
            sc_, sctok, _ = sqcB.next()
            for c in range(4):
                bk, btok = next_mm(envB)
                proj_fm(wq, t_wB, 1024 + c * 128, 128, hT, hTtok, 16, bk, btok)
                P.op("act", I("activation", out=sc_[:, c, :], in_=bk[:, 0:ST], func=AF.Square), reads=[btok], writes=[sctok])
                P.op("dve", I("tensor_copy", out=cs[:, c, :], in_=bk[:, 0:ST]), reads=[btok], writes=[cstok])
            for c in range(4):
                P.op("pe", I("matmul", ssb[:, 0:ST], lhsT=ones, rhs=sc_[:, c, :], start=(c == 0), stop=(c == 3)),
                     reads=[sctok, t_const], writes=[sstok], sig=(c == 3))
            rb, rbtok, _ = rbcB.next()
            rstd_ops(ssb[:, 0:ST], rb[:, 0, :], rb[:, 1, :], 1.0 / 512, CV_ZERO, [sstok], rbtok, rbtok)
            cnt, cntok, _ = cnB.next()
            for c in range(4):
                P.op("dve", I("scalar_tensor_tensor", out=cnt[:, c, :], in0=cs[:, c, :], scalar=gc(G_QA + c), in1=rb[:, 1, :],
                              op0=ALU.mult, op1=ALU.mult), reads=[cstok, rbtok, t_const], writes=[cntok])
            qst, qsttok, qstsem = qd_st.next()
            qrs, qrstok, qrssem = qr_st.next()
            rpb, rptok = envB["rp"]
            rppb, rpptok = envB["rpp"]
            for h in range(NH):
                bk, btok = next_mm(envB)
                proj_fm(wqb, t_wB, h * 128, 128, cnt, cntok, 4, bk, btok)
                proj_fm(wqb, t_wB, 1024 + h * 64, 64, cnt, cntok, 4, rpb, rptok)
                proj_fm(wqb, t_wB, 1536 + h * 64, 64, cnt, cntok, 4, rppb, rpptok)
                sq, sqtok, _ = sqTb.next()
                sqr, sqrtok, _ = sqRb.next()
                P.op("act", I("activation", out=sq[:], in_=bk[:, 0:ST], func=AF.Square), reads=[btok], writes=[sqtok])
                P.op("act", I("activation", out=sqr[:], in_=rpb[0:64, 0:ST], func=AF.Square), reads=[rptok], writes=[sqrtok])
                P.op("pe", I("matmul", ssb[:, 0:ST], lhsT=ones, rhs=sq[:], start=True, stop=False),
                     reads=[sqtok, t_const], writes=[sstok], sig=False)
                P.op("pe", I("matmul", ssb[:, 0:ST], lhsT=ones[0:64, :], rhs=sqr[:], start=False, stop=True),
                     reads=[sqrtok, t_const], writes=[sstok])
                rb, rbtok, _ = rbcB.next()
                rstd_ops(ssb[:, 0:ST], rb[:, 0, :], rb[:, 1, :], 1.0 / 192, CV_ZERO, [sstok], rbtok, rbtok)
                P.op("dve", I("scalar_tensor_tensor", out=qst[:, h, :], in0=bk[:, 0:ST], scalar=gc(G_QMN), in1=rb[:, 1, :],
                              op0=ALU.mult, op1=ALU.mult), reads=[btok, rbtok, t_const], writes=[qsttok])
                r_, rtok_, _ = rtB.next()
                P.op("dve", I("tensor_tensor", out=r_[:, 0, :], in0=rpb[0:64, 0:ST], in1=tb[:, 0, :], op=ALU.mult),
                     reads=[rptok, tbtok], writes=[rtok_])
                P.op("dve", I("tensor_tensor", out=r_[:, 1, :], in0=rppb[0:64, 0:ST], in1=tb[:, 1, :], op=ALU.mult),
                     reads=[rpptok, tbtok], writes=[rtok_])
                P.op("dve", I("tensor_tensor", out=r_[:, 0, :], in0=r_[:, 0, :], in1=r_[:, 1, :], op=ALU.add),
                     reads=[], writes=[rtok_])
                P.op("dve", I("tensor_tensor", out=qrs[:, h, :], in0=r_[:, 0, :], in1=rb[0:64, 1, :], op=ALU.mult),
                     reads=[rtok_, rbtok], writes=[qrstok])
            P.dma("sp", QmnT[:, :, tok0:tok0 + ST].rearrange("h p t -> p h t"), qst[:], qstsem,
                  reads=[qsttok], acc_writes=[t_scr["QmnT"]])
            P.dma("sp", QmrT[:, :, tok0:tok0 + ST].rearrange("h p t -> p h t"), qrs[:], qrssem,
                  reads=[qrstok], acc_writes=[t_scr["QmrT"]])
    A.release(mB)

    if DEBUG and NSTL >= 99 and SUB >= 99 and STOP_AFTER >= 1:
        s_dbg = P.dsem("d_dbg")
        P.dma("sp", dbg_ksc[:, 0:512], ksc_d[:].rearrange("p k c -> p (k c)"), s_dbg, reads=[t_ksc])
        P.dma("sp", dbg_ksc[:, 512:768], ksc_m[:].rearrange("p k c -> p (k c)"), s_dbg, reads=[t_ksc])

    after = P.all_events()
    m_mix = A.mark()
    mix = A.alloc("mix", [128, 16, D], BF16)
    t_mix = [Tok("mix%d" % i, after) for i in range(16)]
    m2 = A.mark()
    gtab = A.alloc("gtab", [128, NH, GW], F32)
    gfar = A.alloc("gfar", [128, 16], F32)
    gsub = A.alloc("gsub", [128, 128], F32)
    lam = A.alloc("lam", [128, 256], F32)
    lamw = A.alloc("lamw", [128, 8], F32)
    t_g = Tok("gtab")
    s_g = P.dsem("d_g")
    dmy = Tok("dummy2", after)
    P.dma("sp", gtab[:], gtab_d, s_g, acc_writes=[t_g], reads=[dmy])
    P.dma("sp", gfar[:], gfar_d, s_g, acc_writes=[t_g])
    P.dma("sp", gsub[:], gsub_d, s_g, acc_writes=[t_g])
    P.dma("sp", lam[:], lam_d, s_g, acc_writes=[t_g])
    t_lam = Tok("lam")
    P.op("dve", I("tensor_scalar", out=gfar[:], in0=gfar[:], scalar1=-C_DIFF, scalar2=None, op0=ALU.add),
         reads=[t_g], writes=[t_lam])
    P.op("dve", I("scalar_tensor_tensor", out=lam[:, 0:64], in0=lam[:, 0:64], scalar=1.0, in1=lam[:, 64:128],
                  op0=ALU.mult, op1=ALU.mult, accum_out=lamw[:, 0:1]), reads=[t_g], writes=[t_lam])
    P.op("dve", I("scalar_tensor_tensor", out=lam[:, 128:192], in0=lam[:, 128:192], scalar=1.0, in1=lam[:, 192:256],
                  op0=ALU.mult, op1=ALU.mult, accum_out=lamw[:, 1:2]), reads=[t_g], writes=[t_lam])
    P.op("act", I("activation", out=lamw[:, 2:4], in_=lamw[:, 0:2], func=AF.Exp), reads=[t_lam], writes=[t_lam])
    P.op("dve", I("tensor_tensor", out=lamw[:, 4:5], in0=lamw[:, 3:4], in1=lamw[:, 2:3], op=ALU.subtract),
         reads=[], writes=[t_lam])
    P.op("dve", I("tensor_scalar", out=lamw[:, 5:6], in0=lamw[:, 4:5], scalar1=-LAMBDA_INIT, scalar2=None, op0=ALU.add),
         reads=[], writes=[t_lam])
    NEGLAM = lamw[:, 5:6]

    Kb = Ring(P, A, "Kb", 4, [128, S], BF16, after)
    Qb = Ring(P, A, "Qb", 2, [128, SQ], BF16, after, dma=False)
    Qrb = Ring(P, A, "Qrb", 2, [128, SQ], BF16, after, dma=False)
    for i in range(4):
        lo = 64 if i % 2 == 0 else 0
        P.op("dve", I("memset", Kb.t[i][lo:lo + 64, :], 0.0), writes=[Kb.tok[i]])
    for i in range(2):
        P.op("dve", I("memset", Qrb.t[i][64:128, :], 0.0), writes=[Qrb.tok[i]])
    Vb = Ring(P, A, "Vb", 2, [128, 32, 130], BF16, after, dma=False)
    LA = 3
    PT = Ring(P, A, "PT", LA + 1, [128, 512], BF16, after, dma=False)
    btmp = Ring(P, A, "btmp", 2, [128, 512], F32, after, dma=False)
    o1n = Ring(P, A, "o1n", 2, [128, 4, 128], F32, after, dma=False)
    cmb = Ring(P, A, "cmb", 8, [128, 128], F32, after, dma=False)
    jk2 = Ring(P, A, "jk2", 2, [128, 128], F32, after, dma=False)
    sm = Ring(P, A, "sm", 16, [128, 8], F32, after, dma=False)
    Sb = [(banks[i], PTok("S%d" % i, after)) for i in (0, 1, 2, 7)]
    Oset = []
    for si in range(2):
        bl = [(banks[3 + si * 2 + i], PTok("O%d_%d" % (si, i), after)) for i in range(2)]
        Oset.append([(bl[j // 2][0], bl[j // 2][1], (j % 2) * 130) for j in range(4)])

    def load_head(srcs):
        res = []
        sem = None
        ev = None
        for ring, dram_ap, scr_tok, rows in srcs:
            t_, tok_, sem_ = ring.next()
            if sem is None:
                sem = sem_
            ev = P.dma("sp", t_[rows[0]:rows[1]], dram_ap, sem, reads=[scr_tok], writes=[tok_])
            res.append((t_, tok_))
        for _, tok_ in res:
            tok_.w = {("dma", sem.h.num): ev}
        return res

    specs = []
    for h in range(NH):
        for qc in range(4):
            for mp in range(2):
                specs.append(("d", h, qc, mp))
    for h in range(NH):
        for qc in range(4):
            specs.append(("m", h, qc, 0))
    if STOP_AFTER < 3:
        specs = []
    heads = {}
    o1s = {}

    def head_data(kind, h):
        if (kind, h) not in heads:
            if kind == "d":
                heads[(kind, h)] = load_head([(Kb, KdT[h, 0:64, :], t_scr["KdT"], (0, 64)), (Qb, QdT[h], t_scr["QdT"], (0, 128)),
                                              (Vb, Vd[h], t_scr["Vd"], (0, 128)), (Kb, KdT[h, 64:128, :], t_scr["KdT"], (64, 128))])
            else:
                heads[(kind, h)] = load_head([(Kb, KmT[h], t_scr["KmT"], (0, 128)), (Qb, QmnT[h], t_scr["QmnT"], (0, 128)),
                                              (Vb, Vm[h], t_scr["Vm"], (0, 128)), (Qrb, QmrT[h], t_scr["QmrT"], (0, 64))])
        return heads[(kind, h)]

    steps = [(si, kt) for si in range(len(specs)) for kt in range(32)]
    s_i = [0]

    def emit_qk(i):
        si, kt = steps[i]
        kind, h, qc, mp = specs[si]
        hd = head_data(kind, h)
        (K, Ktok), (Q, Qtok), (V, Vtok) = hd[0], hd[1], hd[2]
        sb_, stok = Sb[s_i[0] % len(Sb)]
        s_i[0] += 1
        if kind == "d":
            Kp, Kptok = (K, Ktok) if mp == 0 else hd[3]
            P.op("pe", I("matmul", sb_[:], lhsT=Kp[:, kt * 128:(kt + 1) * 128],
                         rhs=Q[:, qc * 512:(qc + 1) * 512], start=True, stop=True),
                 reads=[Kptok, Qtok], writes=[stok])
        else:
            Qr, Qrtok = hd[3]
            P.op("pe", I("matmul", sb_[:], lhsT=K[:, kt * 128:(kt + 1) * 128], rhs=Q[:, qc * 512:(qc + 1) * 512],
                         start=True, stop=False), reads=[Ktok, Qtok], writes=[stok], sig=False)
            P.op("pe", I("matmul", sb_[:], lhsT=kropeT[:, kt * 128:(kt + 1) * 128], rhs=Qr[:, qc * 512:(qc + 1) * 512],
                         start=False, stop=True), reads=[t_krope, Qrtok], writes=[stok])
        pt, pttok, _ = PT.next()
        if kind == "d":
            scale_ap = ksc_d[:, kt, h * 2 + mp:h * 2 + mp + 1]
            m = kt - 4 * qc
            if -1 <= m <= 4:
                bt, bttok, _ = btmp.next()
                g0 = (4 - m) * 128
                P.op("dve", I("scalar_tensor_tensor", out=bt[:], in0=sb_[:], scalar=scale_ap, in1=gtab[:, h, g0:g0 + 512],
                              op0=ALU.mult, op1=ALU.add), reads=[stok, t_ksc, t_g], writes=[bttok])
                P.op("act", I("activation", out=pt[:], in_=bt[:], func=AF.Exp, bias=cvc(CV_NCD), scale=1.0),
                     reads=[bttok, t_cv], writes=[pttok])
            else:
                side = 0 if m < -1 else 1
                P.op("act", I("activation", out=pt[:], in_=sb_[:], func=AF.Exp,
                              bias=gfar[:, h * 2 + side:h * 2 + side + 1], scale=scale_ap),
                     reads=[stok, t_ksc, t_lam], writes=[pttok])
        else:
            scale_ap = ksc_m[:, kt, h:h + 1]
            P.op("act", I("activation", out=pt[:], in_=sb_[:], func=AF.Exp, bias=cvc(CV_NCM), scale=scale_ap),
                 reads=[stok, t_ksc, t_cv], writes=[pttok])
        return pt, pttok

    def emit_av(i, pt, pttok):
        si, kt = steps[i]
        kind, h, qc, mp = specs[si]
        hd = head_data(kind, h)
        V, Vtok = hd[2]
        for j in range(4):
            ob, otok, c0 = Oset[si % 2][j]
            P.op("pe", I("matmul", ob[:, c0:c0 + 130], lhsT=pt[:, j * 128:(j + 1) * 128], rhs=V[:, kt, :],
                         start=(kt == 0 and j % 2 == 0), stop=(kt == 31), skip_group_check=True),
                 reads=[pttok, Vtok], writes=[otok], sig=(j == 3))

    def evac(si):
        kind, h, qc, mp = specs[si]
        part2 = []
        if kind == "d" and mp == 0:
            o1s[(h, qc)] = o1n.next()
        for j in range(4):
            ob, otok, c0 = Oset[si % 2][j]
            s_, stok_, _ = sm.next()
            tt = qc * 4 + j
            P.op("dve", I("reciprocal", out=s_[:, 0:1], in_=ob[:, c0 + 128:c0 + 129]), reads=[otok], writes=[stok_])
            if kind == "m":
                P.op("dve", I("tensor_scalar", out=mix[:, tt, 1024 + h * 128:1024 + (h + 1) * 128], in0=ob[:, c0:c0 + 128],
                              scalar1=s_[:, 0:1], scalar2=None, op0=ALU.mult),
                     reads=[otok, stok_], writes=[t_mix[tt]])
            elif mp == 0:
                o1, o1tok, _ = o1s[(h, qc)]
                P.op("dve", I("tensor_scalar", out=o1[:, j, :], in0=ob[:, c0:c0 + 128], scalar1=s_[:, 0:1], scalar2=None,
                              op0=ALU.mult), reads=[otok, stok_], writes=[o1tok])
            else:
                o1, o1tok, _ = o1s[(h, qc)]
                cm, cmtok, _ = cmb.next()
                jk, jktok, _ = jk2.next()
                P.op("dve", I("tensor_scalar", out=cm[:], in0=ob[:, c0:c0 + 128], scalar1=s_[:, 0:1], scalar2=None,
                              op0=ALU.mult), reads=[otok, stok_], writes=[cmtok])
                P.op("dve", I("scalar_tensor_tensor", out=cm[:], in0=cm[:], scalar=NEGLAM, in1=o1[:, j, :],
                              op0=ALU.mult, op1=ALU.add), reads=[o1tok, t_lam], writes=[cmtok])
                P.op("dve", I("scalar_tensor_tensor", out=jk[:], in0=cm[:], scalar=1.0, in1=cm[:],
                              op0=ALU.mult, op1=ALU.mult, accum_out=s_[:, 1:2]),
                     reads=[cmtok], writes=[jktok, stok_])

                def p2(s_=s_, stok_=stok_, cm=cm, cmtok=cmtok, tt=tt, h=h):
                    rstd_ops(s_[:, 1:2], s_[:, 2:3], s_[:, 3:4], 1.0 / 128, CV_LN08, [stok_], stok_, stok_)
                    P.op("dve", I("scalar_tensor_tensor", out=mix[:, tt, h * 128:(h + 1) * 128], in0=cm[:],
                                  scalar=s_[:, 3:4], in1=gsub[:], op0=ALU.mult, op1=ALU.mult),
                         reads=[cmtok, stok_, t_g], writes=[t_mix[tt]])
                part2.append(p2)
        return part2

    pend = []
    live = {}
    nsteps = len(steps)
    for i in range(min(LA, nsteps)):
        live[i] = emit_qk(i)
    for i in range(nsteps):
        if i + LA < nsteps:
            live[i + LA] = emit_qk(i + LA)
        pt, pttok = live.pop(i)
        emit_av(i, pt, pttok)
        si, kt = steps[i]
        if kt == 31:
            for f in evac(si):
                pend.append((i + 6, f))
        while pend and pend[0][0] <= i:
            pend.pop(0)[1]()
    for _, f in pend:
        f()
    A.release(m2)

    after = P.all_events()
    m3 = A.mark()
    wo = A.alloc("wo", [128, 16, D], BF16)
    gt1 = A.alloc("gt1", [128, D], F32)
    t_wo = Tok("wo")
    s_wo = P.dsem("d_wo")
    dmy = Tok("dummy3", after)
    for i in range(4):
        P.dma("pool", wo[:, i * 4:(i + 1) * 4, :], w_out[i * 512:(i + 1) * 512, :].rearrange("(k p) n -> p k n", p=128),
              s_wo, acc_writes=[t_wo], reads=[dmy])
    s_gt1 = P.dsem("d_gt1")
    P.dma("sp", gt1[:], mod_scr[:, 2 * D:3 * D].partition_broadcast(128), s_gt1, reads=[t_modscr, dmy], acc_writes=[t_wo])
    x3 = Ring(P, A, "x3", 1, [128, D], F32, after)
    x1r = Ring(P, A, "x1r", 2, [128, D], F32, after)
    mixT = Ring(P, A, "mixT", 2, [128, 16, 128], BF16, after, dma=False)
    st3 = Ring(P, A, "st3", 2, [128, 4], F32, after, dma=False)
    xn3 = Ring(P, A, "xn3", 2, [128, D], BF16, after, dma=False)
    h2s = Ring(P, A, "h2s", 2, [128, 16, 128], BF16, after)
    tp3 = [(banks[0][:].bitcast(BF16)[:, 0:512], PTok("tp3a", after)),
           (banks[5][:].bitcast(BF16)[:, 0:512], PTok("tp3b", after))]
    yb = [(banks[1 + i], PTok("y%d" % i, after)) for i in range(4)]

    if STOP_AFTER >= 4:
        for tt in range(16):
            mt, mttok, _ = mixT.next()
            for g in range(4):
                tp, tptok = tp3[g % 2]
                for j in range(4):
                    fc = g * 4 + j
                    P.op("pe", I("transpose", tp[:, j * 128:(j + 1) * 128], mix[:, tt, fc * 128:(fc + 1) * 128], ident),
                         reads=[t_mix[tt], t_const], writes=[tptok], sig=(j == 3))
                P.op("act", I("activation", out=mt[:, g * 4:(g + 1) * 4, :], in_=tp.rearrange("p (a b) -> p a b", b=128),
                              func=AF.Copy), reads=[tptok], writes=[mttok])
            xt, xtok, xsem = x3.next()
            P.dma("sp", xt[:], xk[tt * 128:(tt + 1) * 128, :], xsem, writes=[xtok])
            x1, x1tok, x1sem = x1r.next()
            for cb in range(4):
                yk, ytok = yb[cb]
                for fc in range(16):
                    P.op("pe", I("matmul", yk[:], lhsT=mt[:, fc, :], rhs=wo[:, fc, cb * 512:(cb + 1) * 512],
                                 start=(fc == 0), stop=(fc == 15)),
                         reads=[mttok, t_wo], writes=[ytok], sig=(fc == 15))
                P.op("dve", I("tensor_tensor", out=x1[:, cb * 512:(cb + 1) * 512], in0=yk[:], in1=gt1[:, cb * 512:(cb + 1) * 512],
                              op=ALU.mult), reads=[ytok, t_wo], writes=[x1tok])
                P.op("dve", I("tensor_tensor", out=x1[:, cb * 512:(cb + 1) * 512], in0=x1[:, cb * 512:(cb + 1) * 512],
                              in1=xt[:, cb * 512:(cb + 1) * 512], op=ALU.add), reads=[xtok], writes=[x1tok])
            P.dma("sp", x1_scr[tt * 128:(tt + 1) * 128, :], x1[:], x1sem, reads=[x1tok], acc_writes=[t_scr["x1"]])
            st, sttok, _ = st3.next()
            xn, xntok, _ = xn3.next()
            P.op("act", I("activation", out=xn[:], in_=x1[:], func=AF.Square, accum_out=st[:, 0:1]),
                 reads=[x1tok], writes=[xntok, sttok])
            rstd_ops(st[:, 0:1], st[:, 1:2], st[:, 2:3], 1.0 / D, CV_ZERO, [sttok], sttok, sttok)
            P.op("dve", I("tensor_scalar", out=xn[:], in0=x1[:], scalar1=st[:, 2:3], scalar2=None, op0=ALU.mult),
                 reads=[x1tok, sttok], writes=[xntok])
            hs, hstok, hssem = h2s.next()
            for g in range(4):
                tp, tptok = tp3[g % 2]
                for j in range(4):
                    dc = g * 4 + j
                    P.op("pe", I("transpose", tp[:, j * 128:(j + 1) * 128], xn[:, dc * 128:(dc + 1) * 128], ident),
                         reads=[xntok, t_const], writes=[tptok], sig=(j == 3))
                for j in range(4):
                    dc = g * 4 + j
                    P.op("dve", I("tensor_scalar", out=hs[:, dc, :], in0=tp[:, j * 128:(j + 1) * 128],
                                  scalar1=A2T[:, dc:dc + 1], scalar2=B2T[:, dc:dc + 1], op0=ALU.mult, op1=ALU.add),
                         reads=[tptok, t_mod], writes=[hstok])
            P.dma("sp", h2T_scr[:, :, tt * 128:(tt + 1) * 128], hs[:], hssem, reads=[hstok], acc_writes=[t_scr["h2T"]])
    A.release(m_mix)

    after = P.all_events()
    m4 = A.mark()
    T = FFN_T
    NTH = T // 512
    FB = 256
    gt2 = A.alloc("gt2", [128, D], F32)
    t_gt2 = Tok("gt2")
    s_gt2 = P.dsem("d_gt2")
    dmy = Tok("dummy4", after)
    P.dma("sp", gt2[:], mod_scr[:, 5 * D:6 * D].partition_broadcast(128), s_gt2, reads=[t_modscr, dmy], writes=[t_gt2])
    actT = A.alloc("actT", [128, NFC, T], BF16)
    t_act = [Tok("act%d" % i, after) for i in range(NFC)]
    h2b = Ring(P, A, "h2b", 1, [128, 16, T], BF16, after)
    wg = Ring(P, A, "wg", 2, [128, 16, FB], BF16, after)
    wu = Ring(P, A, "wu", 2, [128, 16, FB], BF16, after)
    wd = Ring(P, A, "wd", 6, [128, 11, 512], BF16, after)
    sg = Ring(P, A, "sg", 2, [128, 512], F32, after, dma=False)
    x1p = Ring(P, A, "x1p", 2, [128, 512], F32, after)
    ost = Ring(P, A, "ost", 2, [128, 512], F32, after)
    gb = [(banks[i], PTok("g%d" % i, after)) for i in range(8)]
    gi = [0]

    if STOP_AFTER >= 5:
        for blk in range(SQ // T):
            hb, hbtok, hbsem = h2b.next()
            P.dma("sp", hb[:], h2T_scr[:, :, blk * T:(blk + 1) * T], hbsem, reads=[t_scr["h2T"]], writes=[hbtok])
            for fcb in range(DFF // FB):
                wgt, wgtok, wgsem = wg.next()
                wut, wutok, wusem = wu.next()
                P.dma("pool", wgt[:], w_gate[:, fcb * FB:(fcb + 1) * FB].rearrange("(k p) n -> p k n", p=128),
                      wgsem, writes=[wgtok])
                P.dma("pool", wut[:], w_up[:, fcb * FB:(fcb + 1) * FB].rearrange("(k p) n -> p k n", p=128),
                      wusem, writes=[wutok])
                for fi in range(FB // 128):
                    f = fcb * (FB // 128) + fi
                    for th in range(NTH):
                        gk, gtok = gb[gi[0] % 8]
                        uk, utok = gb[(gi[0] + 1) % 8]
                        gi[0] += 2
                        for (bk_, btok_, wt_, wtok_) in ((gk, gtok, wgt, wgtok), (uk, utok, wut, wutok)):
                            for dc in range(16):
                                P.op("pe", I("matmul", bk_[:], lhsT=wt_[:, dc, fi * 128:(fi + 1) * 128],
                                             rhs=hb[:, dc, th * 512:(th + 1) * 512], start=(dc == 0), stop=(dc == 15)),
                                     reads=[wtok_, hbtok], writes=[btok_], sig=(dc == 15))
                        s_, stok_, _ = sg.next()
                        P.op("act", I("activation", out=s_[:], in_=gk[:], func=AF.Silu), reads=[gtok], writes=[stok_])
                        P.op("dve", I("tensor_tensor", out=actT[:, f, th * 512:(th + 1) * 512], in0=s_[:], in1=uk[:],
                                      op=ALU.mult), reads=[stok_, utok], writes=[t_act[f]])
            for cb in range(4):
                pieces = []
                for q4 in range(4):
                    wdt, wdtok, wdsem = wd.next()
                    P.dma("pool", wdt[:],
                          w_down[q4 * 1408:(q4 + 1) * 1408, cb * 512:(cb + 1) * 512].rearrange("(k p) n -> p k n", p=128),
                          wdsem, writes=[wdtok])
                    pieces.append((wdt, wdtok))
                for tt in range(T // 128):
                    row0 = blk * T + tt * 128
                    yk, ytok = gb[gi[0] % 8]
                    gi[0] += 1
                    for f in range(NFC):
                        wdt, wdtok = pieces[f // 11]
                        P.op("pe", I("matmul", yk[:], lhsT=actT[:, f, tt * 128:(tt + 1) * 128], rhs=wdt[:, f % 11, :],
                                     start=(f == 0), stop=(f == NFC - 1)),
                             reads=[t_act[f], wdtok], writes=[ytok], sig=(f == NFC - 1 or f % 11 == 10))
                    xp, xptok, xpsem = x1p.next()
                    P.dma("sp", xp[:], x1_scr[row0:row0 + 128, cb * 512:(cb + 1) * 512], xpsem,
                          reads=[t_scr["x1"]], writes=[xptok])
                    os_, ostok, ossem = ost.next()
                    P.op("dve", I("tensor_tensor", out=os_[:], in0=yk[:], in1=gt2[:, cb * 512:(cb + 1) * 512], op=ALU.mult),
                         reads=[ytok, t_gt2], writes=[ostok])
                    P.op("dve", I("tensor_tensor", out=os_[:], in0=os_[:], in1=xp[:], op=ALU.add),
                         reads=[xptok], writes=[ostok])
                    P.dma("sp", out_d[row0:row0 + 128, cb * 512:(cb + 1) * 512], os_[:], ossem, reads=[ostok])
    A.release(m4)

    P.final_wait("sp")
    P.emit()
    print("[kernel] sbuf peak %d / %d, waits %d, ops %s" % (
        A.peak, A.hi, P.n_waits, {k: len(v.ops) for k, v in P.E.items()}), flush=True)
    return nc


def _t5_bucket_np(rel):
    nb, me = 16, 8
    base = np.where(rel > 0, nb, 0)
    n = np.abs(rel)
    nf = np.maximum(n, 1).astype(np.float32)
    large = me + (np.log(nf / np.float32(me)) / np.float32(math.log(128 / 8)) * np.float32(nb - me)).astype(np.int32)
    large = np.minimum(large, nb - 1)
    return (base + np.where(n < me, n, large)).astype(np.int64)


def _prep_inputs(inp):
    f = lambda a: np.ascontiguousarray(np.asarray(a, dtype=np.float32))
    x = f(inp["x"]); c = f(inp["c"]); rel_bias = f(inp["rel_bias"])
    w_in = f(inp["w_in"])[0]
    w_q_b = f(inp["w_q_b"])[0]
    w_kv_b = f(inp["w_kv_b"])[0]
    g_q_mla = f(inp["g_q_mla"])[0]; g_k_mla = f(inp["g_k_mla"])[0]
    g_q_a = f(inp["g_q_a"])[0]; g_kv_a = f(inp["g_kv_a"])[0]
    q_d = w_in[:, 0:1024]; k_d = w_in[:, 1024:2048]; v_d = w_in[:, 2048:3072]
    cq = w_in[:, 3072:3584]; ckv = w_in[:, 3584:3840]; kpe = w_in[:, 3840:3904]
    perm = (np.arange(64) + 32) % 64
    w_kv = f(np.concatenate([k_d, v_d, ckv, kpe, kpe[:, perm]], axis=1))
    w_q = f(np.concatenate([q_d, cq], axis=1))
    qb = w_q_b.reshape(512, NH, 192)
    w_qb = f(np.concatenate([qb[:, :, 0:128].reshape(512, -1), qb[:, :, 128:192].reshape(512, -1),
                             qb[:, :, 128:192][:, :, perm].reshape(512, -1)], axis=1))
    kvb = w_kv_b.reshape(256, NH, 256)
    w_kvb = f(np.concatenate([kvb[:, :, 0:128].reshape(256, -1), kvb[:, :, 128:256].reshape(256, -1)], axis=1))
    gcols = np.ones((128, 16), np.float32)
    gcols[:, 0] = np.tile(f(inp["g_q_diff"])[0], 2)
    gcols[:, 1] = np.tile(f(inp["g_k_diff"])[0], 2)
    gcols[:, 2:6] = g_q_a.reshape(4, 128).T
    gcols[:, 6:8] = g_kv_a.reshape(2, 128).T
    gcols[:, 8] = g_q_mla[0:128]
    gcols[0:64, 9] = g_q_mla[128:192]
    gcols[0:64, 10] = g_q_mla[128:192][perm]
    gcols[:, 11] = g_k_mla[0:128]
    gcols[0:64, 12] = g_k_mla[128:192]
    gcols[0:64, 13] = g_k_mla[128:192][perm]
    gsub_bc = f(np.broadcast_to(f(inp["g_subln"])[0][None, :], (128, 128)))
    lam_bc = f(np.broadcast_to(f(inp["lambda_vecs"])[0].reshape(1, 256), (128, 256)))
    cmat = np.zeros((128, 512), np.float32)
    cmat[:, 0:128] = np.eye(128)
    cmat[:, 128:256] = 1.0
    cmat[0:64, 256:320] = 1.0
    cmat[64:128, 320:384] = 1.0
    cmat[0:64, 384] = 1.0
    cmat[64:128, 385] = 1.0
    inv = (1.0 / (np.float32(10000.0) ** (np.arange(0, 64, 2, dtype=np.float32) / np.float32(64)))).astype(np.float32)
    shared = dict(
        w_ada=f(inp["w_ada"])[0], b_ada=f(inp["b_ada"]).reshape(1, -1),
        g1T=f(f(inp["g_norm1"])[0].reshape(16, 128).T), g2T=f(f(inp["g_norm2"])[0].reshape(16, 128).T),
        w_kv=w_kv, w_q=w_q, w_qb=w_qb, w_kvb=w_kvb,
        w_out=f(inp["w_out"])[0], w_gate=f(inp["w_gate"])[0], w_up=f(inp["w_up"])[0], w_down=f(inp["w_down"])[0],
        gcols=gcols, gsub_bc=gsub_bc, lam_bc=lam_bc, cmat=cmat,
    )
    maps = []
    ii = np.arange(128)[:, None]
    tt = np.arange(GW)[None, :]
    for core in range(8):
        b, half = core // 2, core % 2
        xb = x[b]
        pos = np.arange(S, dtype=np.float32)
        if half == 1:
            xb = xb[::-1]
            pos = pos[::-1]
        ang = (pos[:, None] * inv[None, :]).astype(np.float32)
        cos = np.cos(ang).astype(np.float32).T
        sin = np.sin(ang).astype(np.float32).T
        cos_t = f(np.concatenate([cos, cos], axis=0))
        sin_t = f(np.concatenate([-sin, sin], axis=0))
        sgn = 1 if half == 0 else -1
        dloc = ii - tt + 512
        bidx = _t5_bucket_np((sgn * dloc).astype(np.int32))
        gtab = f(np.transpose(rel_bias[bidx], (0, 2, 1)))
        bneg = _t5_bucket_np(np.array([sgn * -1000], np.int32))[0]
        bpos = _t5_bucket_np(np.array([sgn * 1000], np.int32))[0]
        gfar = np.zeros((128, 16), np.float32)
        gfar[:, 0::2] = rel_bias[bneg][None, :]
        gfar[:, 1::2] = rel_bias[bpos][None, :]
        m = dict(shared)
        m.update(xk=f(xb), cT=f(c[b].reshape(16, 128).T), cos_t=cos_t, sin_t=sin_t, gtab=gtab, gfar=gfar)
        maps.append(m)
    return maps


_NC_CACHE = {}


def kernel(**inputs):
    maps = _prep_inputs(inputs)
    if "nc" not in _NC_CACHE:
        _NC_CACHE["nc"] = build_program()
    nc = _NC_CACHE["nc"]
    res = run_bass_kernel_spmd(nc, maps, core_ids=list(range(8)))
    out = np.zeros((B, S, D), np.float32)
    for core in range(8):
        b, half = core // 2, core % 2
        o = np.asarray(res.results[core]["out"], dtype=np.float32)
        if half == 0:
            out[b, 0:SQ] = o
        else:
            out[b, SQ:S] = o[::-1]
    if DEBUG:
        kernel.last_results = res.results
    return out
```

```python
import math
import numpy as np
import concourse.bass as bass
import concourse.mybir as mybir
from concourse.bass_utils import run_bass_kernel_spmd

F32 = mybir.dt.float32
BF16 = mybir.dt.bfloat16
AF = mybir.ActivationFunctionType
ALU = mybir.AluOpType

D = 2048
S = 4096
SQ = 2048
B = 4
NH = 8
DFF = 5632
NFC = DFF // 128
EPS = 1e-6
LAMBDA_INIT = 0.2
C_DIFF = 8.0
C_MLA = 14.0
FFN_T = 512
GW = 1152
ST = 256
NT = ST // 128

DEBUG = False
STOP_AFTER = 99
SUB = 99
AP_MASK = 31
NHA = 8
NSTL = 99


class Ev:
    __slots__ = ("sem", "val", "eng")

    def __init__(self, sem, val, eng):
        self.sem, self.val, self.eng = sem, val, eng


class Tok:
    __slots__ = ("name", "w", "r", "excl")

    def __init__(self, name, after=(), excl=False):
        self.name = name
        self.w = {}
        self.r = {}
        self.excl = excl
        for i, ev in enumerate(after):
            self.w[("init", i)] = ev


def PTok(name, after=()):
    return Tok(name, after, excl=True)


class DSem:
    _n = [0]

    def __init__(self, nc, name):
        DSem._n[0] += 1
        self.h = nc.alloc_semaphore("%s_%d" % (name, DSem._n[0]))
        self.count = 0


class EngState:
    LIMIT = 30000

    def __init__(self, nc, name, same_sync):
        self.nc, self.name, self.same_sync = nc, name, same_sync
        self.nsem = 0
        self.sem = None
        self.count = 0
        self.known = {}
        self.pending = []
        self.ops = []
        self.last = None
        self._newsem()

    def _newsem(self):
        self.sem = self.nc.alloc_semaphore("e_%s_%d" % (self.name, self.nsem))
        self.nsem += 1
        self.count = 0

    def new_event(self, sig):
        if not sig:
            ev = Ev(None, None, self.name)
            self.pending.append(ev)
            return ev
        if self.count >= self.LIMIT:
            self._newsem()
        self.count += 1
        ev = Ev(self.sem, self.count, self.name)
        for p in self.pending:
            p.sem, p.val = ev.sem, ev.val
        self.pending = []
        self.last = ev
        return ev


class Prog:
    def __init__(self, nc):
        self.nc = nc
        self.E = {
            "pe": EngState(nc, "pe", False),
            "act": EngState(nc, "act", True),
            "dve": EngState(nc, "dve", True),
            "pool": EngState(nc, "pool", True),
            "sp": EngState(nc, "sp", True),
        }
        self.dsems = []
        self.n_waits = 0

    def dsem(self, name):
        s = DSem(self.nc, name)
        self.dsems.append(s)
        return s

    def _need(self, E, eng, ev, waits, is_dma_issue):
        if ev is None:
            return
        if ev.sem is None:
            if ev.eng == eng and not is_dma_issue:
                return
            raise RuntimeError("dependency on unsignaled op (%s <- %s)" % (eng, ev.eng))
        if ev.eng == eng and not E.same_sync and not is_dma_issue:
            return
        k = ev.sem.num
        if E.known.get(k, 0) >= ev.val:
            return
        E.known[k] = ev.val
        waits.append((ev.sem, ev.val))

    def _deps(self, E, eng, reads, writes, is_dma_issue=False):
        waits = []
        for t in reads:
            for ev in t.w.values():
                self._need(E, eng, ev, waits, is_dma_issue)
            if t.excl:
                for k, ev in t.r.items():
                    if k != eng:
                        self._need(E, eng, ev, waits, is_dma_issue)
        for t in writes:
            for ev in t.w.values():
                self._need(E, eng, ev, waits, is_dma_issue)
            for ev in t.r.values():
                self._need(E, eng, ev, waits, is_dma_issue)
        self.n_waits += len(waits)
        return waits

    def op(self, eng, fn, reads=(), writes=(), sig=True):
        E = self.E[eng]
        waits = self._deps(E, eng, reads, writes)
        ev = E.new_event(sig)
        for t in reads:
            t.r[eng] = ev
        for t in writes:
            t.w = {eng: ev}
            t.r = {}
        E.ops.append((waits, fn, ("inc", ev.sem) if sig else None))
        return ev

    def dma(self, q, out, in_, sem, reads=(), writes=(), acc_writes=(), **kw):
        E = self.E[q]
        waits = self._deps(E, q, reads, writes, is_dma_issue=True)
        sem.count += 16
        ev = Ev(sem.h, sem.count, None)
        key = ("dma", sem.h.num)
        for t in reads:
            t.r[key] = ev
        for t in writes:
            t.w = {key: ev}
            t.r = {}
        for t in acc_writes:
            t.w[key] = ev
        E.ops.append((waits, (I("dma_start", out=out, in_=in_, **kw)), ("dma", sem.h)))
        return ev

    def all_events(self):
        evs = []
        for E in self.E.values():
            if E.pending:
                raise RuntimeError("pending unsignaled ops on %s at barrier" % E.name)
            if E.last is not None:
                evs.append(E.last)
        for s in self.dsems:
            if s.count:
                evs.append(Ev(s.h, s.count, None))
        return evs

    def final_wait(self, eng="sp"):
        E = self.E[eng]
        waits = []
        for ev in self.all_events():
            if ev.eng == eng:
                continue
            k = ev.sem.num
            if E.known.get(k, 0) >= ev.val:
                continue
            E.known[k] = ev.val
            waits.append((ev.sem, ev.val))
        E.ops.append((waits, None, None))

    def emit(self):
        nc = self.nc
        hooks = {"pe": "tensor", "act": "scalar", "dve": "vector", "pool": "gpsimd", "sp": "sync"}
        with nc.Block() as block:
            for name, attr in hooks.items():
                ops = self.E[name].ops

                def body(e, ops=ops):
                    for waits, fn, post in ops:
                        for (s, v) in waits:
                            e.wait_ge(s, v)
                        if fn is None:
                            continue
                        name_, args_, kw_ = fn
                        ins = getattr(e, name_)(*args_, **kw_)
                        if post is not None:
                            if post[0] == "inc":
                                ins.then_inc(post[1], 1)
                            else:
                                ins.then_inc(post[1], 16)

                getattr(block, attr)(body)


class Arena:
    def __init__(self, nc, lo=16512, hi=229376):
        self.nc, self.off, self.hi = nc, lo, hi
        self.n = 0
        self.peak = lo

    def alloc(self, name, shape, dt):
        nb = int(np.prod(shape[1:])) * (4 if dt == F32 else 2)
        nb = (nb + 63) // 64 * 64
        if self.off + nb > self.hi:
            raise RuntimeError("SBUF arena overflow allocating %s (%d + %d > %d)" % (name, self.off, nb, self.hi))
        self.n += 1
        t = self.nc.alloc_sbuf_tensor_at("%s_%d" % (name, self.n), list(shape), dt, offset=self.off)
        self.off += nb
        self.peak = max(self.peak, self.off)
        return t

    def mark(self):
        return self.off

    def release(self, m):
        self.off = m


class Ring:
    def __init__(self, P, arena, name, n, shape, dt, after=(), dma=True):
        self.n = n
        self.t = [arena.alloc("%s%d" % (name, i), shape, dt) for i in range(n)]
        self.tok = [Tok("%s%d" % (name, i), after) for i in range(n)]
        self.sem = [P.dsem("d_%s%d" % (name, i)) for i in range(n)] if dma else None
        self.i = -1

    def next(self):
        self.i += 1
        k = self.i % self.n
        return self.t[k], self.tok[k], (self.sem[k] if self.sem else None)


def I(name, *args, **kw):
    return (name, args, kw)


def build_program():
    nc = bass.Bass("TRN2", target_bir_lowering=False)
    P = Prog(nc)
    A = Arena(nc)

    def din(name, shape, dt=F32):
        return nc.dram_tensor(name, list(shape), dt, kind="ExternalInput").ap()

    def dscr(name, shape, dt):
        return nc.dram_tensor(name, list(shape), dt, kind=("ExternalOutput" if DEBUG else "Internal")).ap()

    xk = din("xk", [S, D])
    cT_d = din("cT", [128, 16])
    w_ada = din("w_ada", [D, 6 * D])
    b_ada = din("b_ada", [1, 6 * D])
    g1T_d = din("g1T", [128, 16])
    g2T_d = din("g2T", [128, 16])
    w_kv = din("w_kv", [D, 2432])
    w_q = din("w_q", [D, 1536])
    w_qb = din("w_qb", [512, 2048])
    w_kvb = din("w_kvb", [256, 2048])
    w_out = din("w_out", [D, D])
    w_gate = din("w_gate", [D, DFF])
    w_up = din("w_up", [D, DFF])
    w_down = din("w_down", [DFF, D])
    gcols_d = din("gcols", [128, 16])
    gsub_d = din("gsub_bc", [128, 128])
    lam_d = din("lam_bc", [128, 256])
    cos_d = din("cos_t", [64, S])
    sin_d = din("sin_t", [64, S])
    gtab_d = din("gtab", [128, NH, GW])
    gfar_d = din("gfar", [128, 16])
    cmat_d = din("cmat", [128, 512])
    out_d = nc.dram_tensor("out", [SQ, D], F32, kind="ExternalOutput").ap()

    mod_scr = dscr("mod_scr", [1, 6 * D], F32)
    KdT = dscr("KdT", [NH, 128, S], BF16)
    QdT = dscr("QdT", [NH, 128, SQ], BF16)
    Vd = dscr("Vd", [NH, 128, 32, 130], BF16)
    KmT = dscr("KmT", [NH, 128, S], BF16)
    QmnT = dscr("QmnT", [NH, 128, SQ], BF16)
    QmrT = dscr("QmrT", [NH, 64, SQ], BF16)
    Vm = dscr("Vm", [NH, 128, 32, 130], BF16)
    x1_scr = dscr("x1_scr", [SQ, D], F32)
    h2T_scr = dscr("h2T_scr", [128, 16, SQ], BF16)
    dbg_ksc = nc.dram_tensor("dbg_ksc", [128, 32 * 24 + 64 * 32], F32, kind="ExternalOutput").ap() if DEBUG else None

    banks = [nc.alloc_psum_tensor("bank%d" % i, [128, 512], F32) for i in range(8)]

    cmat = A.alloc("cmat", [128, 512], BF16)
    ident = cmat[:, 0:128]
    ones = cmat[:, 128:256]
    bones = cmat[:, 256:384]
    sel2 = cmat[:, 384:386]
    gcols = A.alloc("gcols", [128, 16], F32)
    cv = A.alloc("cv", [128, 8], F32)
    A1T = A.alloc("A1T", [128, 16], F32)
    B1T = A.alloc("B1T", [128, 16], F32)
    A2T = A.alloc("A2T", [128, 16], F32)
    B2T = A.alloc("B2T", [128, 16], F32)
    ksc_d = A.alloc("ksc_d", [128, 32, 16], F32)
    ksc_m = A.alloc("ksc_m", [128, 32, 8], F32)
    kropeT = A.alloc("kropeT", [128, S], BF16)
    t_const = Tok("const")
    t_cv = Tok("cv")
    t_mod = Tok("mod")
    t_ksc = Tok("ksc")
    t_krope = Tok("krope")
    s_const = P.dsem("d_const")

    CV_EPS, CV_LN8, CV_LN192, CV_LN08, CV_NCD, CV_NCM, CV_ZERO = 0, 1, 2, 3, 4, 5, 6
    cvals = [EPS, math.log(0.125), math.log(192 ** -0.5), math.log(1.0 - LAMBDA_INIT), -C_DIFF, -C_MLA, 0.0, 1.0]

    P.dma("pool", cmat[:], cmat_d, s_const, acc_writes=[t_const])
    s_const2 = P.dsem("d_const2")
    P.dma("sp", gcols[:], gcols_d, s_const2, acc_writes=[t_const])
    for i, v in enumerate(cvals):
        P.op("dve", I("memset", cv[:, i:i + 1], float(v)), writes=[t_cv])
    P.op("dve", I("memset", kropeT[64:128, :], 0.0), writes=[t_krope])

    def cvc(i, n=128):
        return cv[0:n, i:i + 1]

    def gc(i, n=128):
        return gcols[0:n, i:i + 1]

    G_QD, G_KD, G_QA, G_KVA, G_QMN, G_QMR, G_QMP, G_KMN, G_KMR, G_KMP = 0, 1, 2, 6, 8, 9, 10, 11, 12, 13

    def rstd_ops(src, tmp, dst, inv_n, ln_bias_col, reads, tmp_tok, dst_tok, npart=128):
        P.op("act", I("activation", out=tmp, in_=src, func=AF.Ln, scale=inv_n, bias=cvc(CV_EPS, npart)),
             reads=list(reads) + [t_cv], writes=[tmp_tok])
        P.op("act", I("activation", out=dst, in_=tmp, func=AF.Exp, scale=-0.5, bias=cvc(ln_bias_col, npart)),
             reads=[t_cv, tmp_tok], writes=[dst_tok])

    m0 = A.mark()
    cT = A.alloc("cT", [128, 16], F32)
    cact = A.alloc("cact", [128, 16], BF16)
    g1T = A.alloc("g1T", [128, 16], F32)
    g2T = A.alloc("g2T", [128, 16], F32)
    modT = A.alloc("modT", [128, 96], F32)
    modrow = A.alloc("modrow", [1, 6 * D], F32)
    brow = A.alloc("brow", [1, 6 * D], F32)
    wr = Ring(P, A, "wada", 2, [128, 16, 1024], BF16)
    t_c = Tok("c")
    t_cact = Tok("cact")
    t_brow = Tok("brow")
    t_modrow = Tok("modrow")
    s_p0 = P.dsem("d_p0")
    s_p0b = P.dsem("d_p0b")
    s_p0c = P.dsem("d_p0c")
    P.dma("sp", cT[:], cT_d, s_p0, acc_writes=[t_c])
    P.dma("sp", g1T[:], g1T_d, s_p0, acc_writes=[t_c])
    P.dma("sp", g2T[:], g2T_d, s_p0, acc_writes=[t_c])
    P.dma("sp", brow[:], b_ada, s_p0b, writes=[t_brow])
    P.op("act", I("activation", out=cact[:], in_=cT[:], func=AF.Silu), reads=[t_c], writes=[t_cact])
    pb = [PTok("p0bank0"), PTok("p0bank1")]
    for nb in range(12):
        wt, wtok, wsem = wr.next()
        P.dma("pool", wt[:], w_ada[:, nb * 1024:(nb + 1) * 1024].rearrange("(kc p) n -> p kc n", p=128),
              wsem, writes=[wtok])
        for half in range(2):
            i = nb * 2 + half
            bk, btok = banks[i % 2], pb[i % 2]
            for kc in range(16):
                P.op("pe", I("matmul", bk[0:1, :], lhsT=cact[:, kc:kc + 1], rhs=wt[:, kc, half * 512:(half + 1) * 512],
                             start=(kc == 0), stop=(kc == 15)),
                     reads=[wtok, t_cact], writes=[btok], sig=(kc == 15))
            c0 = i * 512
            P.op("dve", I("tensor_tensor", out=modrow[0:1, c0:c0 + 512], in0=bk[0:1, :], in1=brow[0:1, c0:c0 + 512],
                          op=ALU.add), reads=[btok, t_brow], writes=[t_modrow])
    t_modscr = Tok("modscr")
    P.dma("sp", mod_scr, modrow[:], s_p0c, reads=[t_modrow], writes=[t_modscr])
    t_modT = Tok("modT")
    s_p0d = P.dsem("d_p0d")
    for j0 in range(0, 96, 16):
        P.dma("sp", modT[:, j0:j0 + 16],
              mod_scr[:, j0 * 128:(j0 + 16) * 128].rearrange("o (j p) -> p (o j)", p=128),
              s_p0d, reads=[t_modscr], acc_writes=[t_modT], allow_slow_non_contiguous=True)
    P.op("dve", I("scalar_tensor_tensor", out=A1T[:], in0=modT[:, 16:32], scalar=1.0, in1=g1T[:],
                  op0=ALU.add, op1=ALU.mult), reads=[t_modT, t_c], writes=[t_mod])
    P.op("dve", I("tensor_copy", out=B1T[:], in_=modT[:, 0:16]), reads=[t_modT], writes=[t_mod])
    P.op("dve", I("scalar_tensor_tensor", out=A2T[:], in0=modT[:, 64:80], scalar=1.0, in1=g2T[:],
                  op0=ALU.add, op1=ALU.mult), reads=[t_modT, t_c], writes=[t_mod])
    P.op("dve", I("tensor_copy", out=B2T[:], in_=modT[:, 48:64]), reads=[t_modT], writes=[t_mod])
    A.release(m0)

    def norm_supertile(env, xrows0, AT, BT):
        hT, hT_tok, _ = env["hT"].next()
        for tt in range(NT):
            xt, xtok, xsem = env["x"].next()
            r = xrows0 + tt * 128
            P.dma("sp", xt[:], xk[r:r + 128, :], xsem, writes=[xtok])
            st, sttok, _ = env["stat"].next()
            xn, xntok, _ = env["xn"].next()
            P.op("act", I("activation", out=xn[:], in_=xt[:], func=AF.Square, accum_out=st[:, 0:1]),
                 reads=[xtok], writes=[xntok, sttok])
            rstd_ops(st[:, 0:1], st[:, 1:2], st[:, 2:3], 1.0 / D, CV_ZERO, [sttok], sttok, sttok)
            P.op("dve", I("tensor_scalar", out=xn[:], in0=xt[:], scalar1=st[:, 2:3], scalar2=None, op0=ALU.mult),
                 reads=[xtok, sttok], writes=[xntok])
            for g in range(4):
                tp, tptok = env["tp"][g % 2]
                for j in range(4):
                    dc = g * 4 + j
                    P.op("pe", I("transpose", tp[:, j * 128:(j + 1) * 128], xn[:, dc * 128:(dc + 1) * 128], ident),
                         reads=[xntok, t_const], writes=[tptok], sig=(j == 3))
                for j in range(4):
                    dc = g * 4 + j
                    P.op("dve", I("tensor_scalar", out=hT[:, dc, tt * 128:(tt + 1) * 128], in0=tp[:, j * 128:(j + 1) * 128],
                                  scalar1=AT[:, dc:dc + 1], scalar2=BT[:, dc:dc + 1], op0=ALU.mult, op1=ALU.add),
                         reads=[tptok, t_mod], writes=[hT_tok])
        return hT, hT_tok

    def proj_fm(wt, wtok, c0, ncol, rhsT, rhs_tok, nk, out_bank, out_tok):
        for k in range(nk):
            P.op("pe", I("matmul", out_bank[0:ncol, 0:ST], lhsT=wt[:, k, c0:c0 + ncol], rhs=rhsT[:, k, :],
                         start=(k == 0), stop=(k == nk - 1)),
                 reads=[wtok, rhs_tok], writes=[out_tok], sig=(k == nk - 1))

    def make_env(after):
        env = {}
        env["x"] = Ring(P, A, "x", 2, [128, D], F32, after)
        env["stat"] = Ring(P, A, "stat", 2, [128, 4], F32, after, dma=False)
        env["xn"] = Ring(P, A, "xn", 2, [128, D], BF16, after, dma=False)
        env["hT"] = Ring(P, A, "hT", 2, [128, 16, ST], BF16, after, dma=False)
        env["tp"] = [(banks[0][:].bitcast(BF16)[:, 0:512], PTok("tp0", after)),
                     (banks[1][:].bitcast(BF16)[:, 0:512], PTok("tp1", after))]
        env["mm"] = [(banks[2 + i], PTok("mm%d" % i, after)) for i in range(3)]
        env["mmi"] = 0
        t_ssaux = PTok("ssaux", after)
        env["ss"] = (banks[5], t_ssaux)
        env["aux"] = (banks[5][:, 256:512], t_ssaux)
        env["rp"] = (banks[6], PTok("rp", after))
        env["rpp"] = (banks[7], PTok("rpp", after))
        return env

    def next_mm(env):
        b, t = env["mm"][env["mmi"] % 3]
        env["mmi"] += 1
        return b, t

    t_scr = {k: Tok(k) for k in ["KdT", "QdT", "Vd", "KmT", "QmnT", "QmrT", "Vm", "x1", "h2T"]}

    after = P.all_events()
    mA = A.mark()
    envA = make_env(after)
    wkv = A.alloc("wkv", [128, 16, 2432], BF16)
    wkvb = A.alloc("wkvb", [128, 2, 2048], BF16)
    t_wA = Tok("wA")
    s_wA = P.dsem("d_wA")
    dmy = Tok("dummyA", after)
    for i in range(4):
        P.dma("pool", wkv[:, i * 4:(i + 1) * 4, :], w_kv[i * 512:(i + 1) * 512, :].rearrange("(k p) n -> p k n", p=128),
              s_wA, acc_writes=[t_wA], reads=[dmy])
    P.dma("pool", wkvb[:], w_kvb.rearrange("(k p) n -> p k n", p=128), s_wA, acc_writes=[t_wA])
    tabk = Ring(P, A, "tabk", 2, [64, 2, ST], F32, after)
    kd_st = Ring(P, A, "kd_st", 2, [128, NH, ST], BF16, after)
    vd_st = Ring(P, A, "vd_st", 2, [128, NH, NT, 130], BF16, after)
    sqT = Ring(P, A, "sqT", 2, [128, ST], BF16, after, dma=False)
    csb = Ring(P, A, "csb", 1, [128, 2, ST], F32, after, dma=False)
    sqc = Ring(P, A, "sqc", 1, [128, 2, ST], BF16, after, dma=False)
    cn = Ring(P, A, "cn", 1, [128, 2, ST], BF16, after, dma=False)
    rbc = Ring(P, A, "rbc", 2, [128, 2, ST], F32, after, dma=False)
    sqpe = Ring(P, A, "sqpe", 1, [64, ST], BF16, after, dma=False)
    rt = Ring(P, A, "rt", 2, [64, 2, ST], F32, after, dma=False)
    auxs = Ring(P, A, "auxs", 2, [128, 2, 64], F32, after, dma=False)

    for i in range(2):
        t_, tok_ = vd_st.t[i], vd_st.tok[i]
        P.op("dve", I("memset", t_[:, :, :, 128:129], 1.0), writes=[tok_])
        P.op("dve", I("memset", t_[:, :, :, 129:130], 0.0), writes=[tok_])

    def v_tokmajor(env, lhs, lhs_tok, nk, wt, wtok, c0, dram, dram_tok, kt0):
        vst, vtok, vsem = vd_st.next()
        for tt in range(NT):
            for half in range(2):
                bk, btok = next_mm(env)
                for k in range(nk):
                    P.op("pe", I("matmul", bk[:], lhsT=lhs[:, k, tt * 128:(tt + 1) * 128],
                                 rhs=wt[:, k, c0 + half * 512:c0 + (half + 1) * 512], start=(k == 0), stop=(k == nk - 1)),
                         reads=[lhs_tok, wtok], writes=[btok], sig=(k == nk - 1))
                P.op("act", I("activation", out=vst[:, half * 4:(half + 1) * 4, tt, 0:128],
                              in_=bk[:].rearrange("p (h e) -> p h e", e=128), func=AF.Copy),
                     reads=[btok], writes=[vtok])
        P.dma("sp", dram[:, :, kt0:kt0 + NT, :].rearrange("h p k e -> p h k e"), vst[:], vsem,
              reads=[vtok], acc_writes=[dram_tok])

    def passA_body(st_i, hT, hTtok):
        tok0 = st_i * ST
        kt0 = st_i * NT
        axb, axtok = envA["aux"]
        ssb, sstok = envA["ss"]
        rpb, rptok = envA["rp"]
        rppb, rpptok = envA["rpp"]
        tb, tbtok, tbsem = tabk.next()
        P.dma("sp", tb[:, 0, :], cos_d[:, tok0:tok0 + ST], tbsem, writes=[tbtok])
        ev = P.dma("sp", tb[:, 1, :], sin_d[:, tok0:tok0 + ST], tbsem, acc_writes=[tbtok])
        tbtok.w = {("dma", tbsem.h.num): ev}
        P.op("dve", I("tensor_scalar", out=tb[:, 0, :], in0=tb[:, 0, :], scalar1=gc(G_KMR, 64), scalar2=None, op0=ALU.mult),
             reads=[t_const], writes=[tbtok])
        P.op("dve", I("tensor_scalar", out=tb[:, 1, :], in0=tb[:, 1, :], scalar1=gc(G_KMP, 64), scalar2=None, op0=ALU.mult),
             reads=[t_const], writes=[tbtok])
        cs, cstok, _ = csb.next()
        sc_, sctok, _ = sqc.next()
        for c in range(2):
            bk, btok = next_mm(envA)
            proj_fm(wkv, t_wA, 2048 + c * 128, 128, hT, hTtok, 16, bk, btok)
            P.op("act", I("activation", out=sc_[:, c, :], in_=bk[:, 0:ST], func=AF.Square), reads=[btok], writes=[sctok])
            P.op("dve", I("tensor_copy", out=cs[:, c, :], in_=bk[:, 0:ST]), reads=[btok], writes=[cstok])
        proj_fm(wkv, t_wA, 2304, 64, hT, hTtok, 16, rpb, rptok)
        proj_fm(wkv, t_wA, 2368, 64, hT, hTtok, 16, rppb, rpptok)
        sp_, sptok, _ = sqpe.next()
        P.op("act", I("activation", out=sp_[:], in_=rpb[0:64, 0:ST], func=AF.Square), reads=[rptok], writes=[sptok])
        r_, rtok_, _ = rt.next()
        P.op("dve", I("tensor_tensor", out=r_[:, 0, :], in0=rpb[0:64, 0:ST], in1=tb[:, 0, :], op=ALU.mult),
             reads=[rptok, tbtok], writes=[rtok_])
        P.op("dve", I("tensor_tensor", out=r_[:, 1, :], in0=rppb[0:64, 0:ST], in1=tb[:, 1, :], op=ALU.mult),
             reads=[rpptok, tbtok], writes=[rtok_])
        P.op("dve", I("tensor_tensor", out=kropeT[0:64, tok0:tok0 + ST], in0=r_[:, 0, :], in1=r_[:, 1, :], op=ALU.add),
             reads=[rtok_], writes=[t_krope])
        for c in range(2):
            P.op("pe", I("matmul", ssb[:, 0:ST], lhsT=ones, rhs=sc_[:, c, :], start=(c == 0), stop=(c == 1)),
                 reads=[sctok, t_const], writes=[sstok], sig=(c == 1))
        rb, rbtok, _ = rbc.next()
        rstd_ops(ssb[:, 0:ST], rb[:, 0, :], rb[:, 1, :], 1.0 / 256, CV_ZERO, [sstok], rbtok, rbtok)
        cnt, cntok, _ = cn.next()
        for c in range(2):
            P.op("dve", I("scalar_tensor_tensor", out=cnt[:, c, :], in0=cs[:, c, :], scalar=gc(G_KVA + c), in1=rb[:, 1, :],
                          op0=ALU.mult, op1=ALU.mult), reads=[cstok, rbtok, t_const], writes=[cntok])
        kst, ksttok, kstsem = kd_st.next()
        later = None
        for h in range(NH):
            bk, btok = next_mm(envA)
            proj_fm(wkv, t_wA, h * 128, 128, hT, hTtok, 16, bk, btok)
            sq, sqtok, _ = sqT.next()
            P.op("act", I("activation", out=sq[:], in_=bk[:, 0:ST], func=AF.Square), reads=[btok], writes=[sqtok])
            P.op("dve", I("tensor_scalar", out=kst[:, h, :], in0=bk[:, 0:ST], scalar1=gc(G_KD), scalar2=None, op0=ALU.mult),
                 reads=[btok, t_const], writes=[ksttok])
            if later:
                later()

            def later(sq=sq, sqtok=sqtok, h=h):
                for tt in range(NT):
                    c = tt * 16 + h * 2
                    P.op("pe", I("matmul", axb[:, c:c + 2], lhsT=sq[:, tt * 128:(tt + 1) * 128], rhs=sel2,
                                 start=True, stop=True),
                         reads=[sqtok, t_const], writes=[axtok], sig=(tt == NT - 1))
        P.dma("sp", KdT[:, :, tok0:tok0 + ST].rearrange("h p t -> p h t"), kst[:], kstsem,
              reads=[ksttok], acc_writes=[t_scr["KdT"]])
        v_tokmajor(envA, hT, hTtok, 16, wkv, t_wA, 1024, Vd, t_scr["Vd"], kt0)
        later()
        au, autok, _ = auxs.next()
        rstd_ops(axb[:, 0:NT * 16], au[:, 0, 0:NT * 16], ksc_d[:, kt0:kt0 + NT, :].rearrange("p k c -> p (k c)"),
                 1.0 / 64, CV_LN8, [axtok], autok, t_ksc)
        kst, ksttok, kstsem = kd_st.next()
        later = None
        for h in range(NH):
            bk, btok = next_mm(envA)
            proj_fm(wkvb, t_wA, h * 128, 128, cnt, cntok, 2, bk, btok)
            sq, sqtok, _ = sqT.next()
            P.op("act", I("activation", out=sq[:], in_=bk[:, 0:ST], func=AF.Square), reads=[btok], writes=[sqtok])
            P.op("dve", I("tensor_scalar", out=kst[:, h, :], in0=bk[:, 0:ST], scalar1=gc(G_KMN), scalar2=None, op0=ALU.mult),
                 reads=[btok, t_const], writes=[ksttok])
            if later:
                later()

            def later(sq=sq, sqtok=sqtok, h=h):
                for tt in range(NT):
                    c = 64 + tt * 8 + h
                    P.op("pe", I("matmul", axb[:, c:c + 1], lhsT=sq[:, tt * 128:(tt + 1) * 128], rhs=ones[:, 0:1],
                                 start=True, stop=False), reads=[sqtok, t_const], writes=[axtok], sig=False)
                    P.op("pe", I("matmul", axb[:, c:c + 1], lhsT=sp_[:, tt * 128:(tt + 1) * 128], rhs=ones[0:64, 0:1],
                                 start=False, stop=True), reads=[sptok, t_const], writes=[axtok], sig=(tt == NT - 1))
        P.dma("sp", KmT[:, :, tok0:tok0 + ST].rearrange("h p t -> p h t"), kst[:], kstsem,
              reads=[ksttok], acc_writes=[t_scr["KmT"]])
        v_tokmajor(envA, cnt, cntok, 2, wkvb, t_wA, 1024, Vm, t_scr["Vm"], kt0)
        later()
        au, autok, _ = auxs.next()
        rstd_ops(axb[:, 64:64 + NT * 8], au[:, 0, 0:NT * 8], ksc_m[:, kt0:kt0 + NT, :].rearrange("p k c -> p (k c)"),
                 1.0 / 192, CV_LN192, [axtok], autok, t_ksc)

    if STOP_AFTER >= 1:
        NSA = S // ST
        nxt = norm_supertile(envA, 0, A1T, B1T)
        for st_i in range(NSA):
            hT, hTtok = nxt
            if st_i + 1 < NSA:
                nxt = norm_supertile(envA, (st_i + 1) * ST, A1T, B1T)
            passA_body(st_i, hT, hTtok)
    A.release(mA)

    after = P.all_events()
    mB = A.mark()
    envB = make_env(after)
    wq = A.alloc("wq", [128, 16, 1536], BF16)
    wqb = A.alloc("wqb", [128, 4, 2048], BF16)
    t_wB = Tok("wB")
    s_wB = P.dsem("d_wB")
    dmy = Tok("dummyB", after)
    for i in range(4):
        P.dma("pool", wq[:, i * 4:(i + 1) * 4, :], w_q[i * 512:(i + 1) * 512, :].rearrange("(k p) n -> p k n", p=128),
              s_wB, acc_writes=[t_wB], reads=[dmy])
    P.dma("pool", wqb[:], w_qb.rearrange("(k p) n -> p k n", p=128), s_wB, acc_writes=[t_wB])
    tabq = Ring(P, A, "tabq", 2, [64, 2, ST], F32, after)
    qd_st = Ring(P, A, "qd_st", 2, [128, NH, ST], BF16, after)
    qr_st = Ring(P, A, "qr_st", 2, [64, NH, ST], BF16, after)
    sqTb = Ring(P, A, "sqTb", 2, [128, ST], BF16, after, dma=False)
    sqRb = Ring(P, A, "sqRb", 2, [64, ST], BF16, after, dma=False)
    csbB = Ring(P, A, "csbB", 1, [128, 4, ST], F32, after, dma=False)
    sqcB = Ring(P, A, "sqcB", 1, [128, 4, ST], BF16, after, dma=False)
    cnB = Ring(P, A, "cnB", 1, [128, 4, ST], BF16, after, dma=False)
    rbcB = Ring(P, A, "rbcB", 2, [128, 2, ST], F32, after, dma=False)
    rtB = Ring(P, A, "rtB", 2, [64, 2, ST], F32, after, dma=False)

    def passB_body(st_i, hT, hTtok):
        tok0 = st_i * ST
        ssb, sstok = envB["ss"]
        rpb, rptok = envB["rp"]
        rppb, rpptok = envB["rpp"]
        tb, tbtok, tbsem = tabq.next()
        P.dma("sp", tb[:, 0, :], cos_d[:, tok0:tok0 + ST], tbsem, writes=[tbtok])
        ev = P.dma("sp", tb[:, 1, :], sin_d[:, tok0:tok0 + ST], tbsem, acc_writes=[tbtok])
        tbtok.w = {("dma", tbsem.h.num): ev}
        P.op("dve", I("tensor_scalar", out=tb[:, 0, :], in0=tb[:, 0, :], scalar1=gc(G_QMR, 64), scalar2=None, op0=ALU.mult),
             reads=[t_const], writes=[tbtok])
        P.op("dve", I("tensor_scalar", out=tb[:, 1, :], in0=tb[:, 1, :], scalar1=gc(G_QMP, 64), scalar2=None, op0=ALU.mult),
             reads=[t_const], writes=[tbtok])
        cs, cstok, _ = csbB.next()
        sc_, sctok, _ = sqcB.next()
        for c in range(4):
            bk, btok = next_mm(envB)
            proj_fm(wq, t_wB, 1024 + c * 128, 128, hT, hTtok, 16, bk, btok)
            P.op("act", I("activation", out=sc_[:, c, :], in_=bk[:, 0:ST], func=AF.Square), reads=[btok], writes=[sctok])
            P.op("dve", I("tensor_copy", out=cs[:, c, :], in_=bk[:, 0:ST]), reads=[btok], writes=[cstok])
        cq_state = {}

        def cq_norm():
            for c in range(4):
                P.op("pe", I("matmul", ssb[:, 0:ST], lhsT=ones, rhs=sc_[:, c, :], start=(c == 0), stop=(c == 3)),
                     reads=[sctok, t_const], writes=[sstok], sig=(c == 3))
            rb, rbtok, _ = rbcB.next()
            rstd_ops(ssb[:, 0:ST], rb[:, 0, :], rb[:, 1, :], 1.0 / 512, CV_ZERO, [sstok], rbtok, rbtok)
            cnt, cntok, _ = cnB.next()
            for c in range(4):
                P.op("dve", I("scalar_tensor_tensor", out=cnt[:, c, :], in0=cs[:, c, :], scalar=gc(G_QA + c), in1=rb[:, 1, :],
                              op0=ALU.mult, op1=ALU.mult), reads=[cstok, rbtok, t_const], writes=[cntok])
            cq_state["cnt"] = (cnt, cntok)
        qst, qsttok, qstsem = qd_st.next()
        later = None
        for h in range(NH):
            bk, btok = next_mm(envB)
            proj_fm(wq, t_wB, h * 128, 128, hT, hTtok, 16, bk, btok)
            sq, sqtok, _ = sqTb.next()
            P.op("act", I("activation", out=sq[:], in_=bk[:, 0:ST], func=AF.Square), reads=[btok], writes=[sqtok])
            if h == 0:
                cq_norm()
            if later:
                later()

            def later(sq=sq, sqtok=sqtok, h=h, bk=bk, btok=btok):
                P.op("pe", I("matmul", ssb[:, 0:ST], lhsT=bones, rhs=sq[:], start=True, stop=True),
                     reads=[sqtok, t_const], writes=[sstok])
                rb, rbtok, _ = rbcB.next()
                rstd_ops(ssb[:, 0:ST], rb[:, 0, :], rb[:, 1, :], 1.0 / 64, CV_ZERO, [sstok], rbtok, rbtok)
                P.op("dve", I("scalar_tensor_tensor", out=qst[:, h, :], in0=bk[:, 0:ST], scalar=gc(G_QD), in1=rb[:, 1, :],
                              op0=ALU.mult, op1=ALU.mult), reads=[btok, rbtok, t_const], writes=[qsttok])
        cnt, cntok = cq_state["cnt"]
        qst2, qsttok2, qstsem2 = qd_st.next()
        qrs, qrstok, qrssem = qr_st.next()
        later2 = None
        for h in range(NH):
            bk, btok = next_mm(envB)
            proj_fm(wqb, t_wB, h * 128, 128, cnt, cntok, 4, bk, btok)
            if h == 0:
                later()
                P.dma("sp", QdT[:, :, tok0:tok0 + ST].rearrange("h p t -> p h t"), qst[:], qstsem,
                      reads=[qsttok], acc_writes=[t_scr["QdT"]])
            if later2:
                later2()
            proj_fm(wqb, t_wB, 1024 + h * 64, 64, cnt, cntok, 4, rpb, rptok)
            proj_fm(wqb, t_wB, 1536 + h * 64, 64, cnt, cntok, 4, rppb, rpptok)
            sq, sqtok, _ = sqTb.next()
            sqr, sqrtok, _ = sqRb.next()
            P.op("act", I("activation", out=sq[:], in_=bk[:, 0:ST], func=AF.Square), reads=[btok], writes=[sqtok])
            P.op("act", I("activation", out=sqr[:], in_=rpb[0:64, 0:ST], func=AF.Square), reads=[rptok], writes=[sqrtok])
            r_, rtok_, _ = rtB.next()
            P.op("dve", I("tensor_tensor", out=r_[:, 0, :], in0=rpb[0:64, 0:ST], in1=tb[:, 0, :], op=ALU.mult),
                 reads=[rptok, tbtok], writes=[rtok_])
            P.op("dve", I("tensor_tensor", out=r_[:, 1, :], in0=rppb[0:64, 0:ST], in1=tb[:, 1, :], op=ALU.mult),
                 reads=[rpptok, tbtok], writes=[rtok_])
            P.op("dve", I("tensor_tensor", out=r_[:, 0, :], in0=r_[:, 0, :], in1=r_[:, 1, :], op=ALU.add),
                 reads=[], writes=[rtok_])

            def later2(sq=sq, sqtok=sqtok, sqr=sqr, sqrtok=sqrtok, h=h, bk=bk, btok=btok, r_=r_, rtok_=rtok_):
                P.op("pe", I("matmul", ssb[:, 0:ST], lhsT=ones, rhs=sq[:], start=True, stop=False),
                     reads=[sqtok, t_const], writes=[sstok], sig=False)
                P.op("pe", I("matmul", ssb[:, 0:ST], lhsT=ones[0:64, :], rhs=sqr[:], start=False, stop=True),
                     reads=[sqrtok, t_const], writes=[sstok])
                rb, rbtok, _ = rbcB.next()
                rstd_ops(ssb[:, 0:ST], rb[:, 0, :], rb[:, 1, :], 1.0 / 192, CV_ZERO, [sstok], rbtok, rbtok)
                P.op("dve", I("scalar_tensor_tensor", out=qst2[:, h, :], in0=bk[:, 0:ST], scalar=gc(G_QMN), in1=rb[:, 1, :],
                              op0=ALU.mult, op1=ALU.mult), reads=[btok, rbtok, t_const], writes=[qsttok2])
                P.op("dve", I("tensor_tensor", out=qrs[:, h, :], in0=r_[:, 0, :], in1=rb[0:64, 1, :], op=ALU.mult),
                     reads=[rtok_, rbtok], writes=[qrstok])
        later2()
        P.dma("sp", QmnT[:, :, tok0:tok0 + ST].rearrange("h p t -> p h t"), qst2[:], qstsem2,
              reads=[qsttok2], acc_writes=[t_scr["QmnT"]])
        P.dma("sp", QmrT[:, :, tok0:tok0 + ST].rearrange("h p t -> p h t"), qrs[:], qrssem,
              reads=[qrstok], acc_writes=[t_scr["QmrT"]])

    if STOP_AFTER >= 2:
        NSB = SQ // ST
        nxt = norm_supertile(envB, 0, A1T, B1T)
        for st_i in range(NSB):
            hT, hTtok = nxt
            if st_i + 1 < NSB:
                nxt = norm_supertile(envB, (st_i + 1) * ST, A1T, B1T)
            passB_body(st_i, hT, hTtok)
    A.release(mB)

    if DEBUG and NSTL >= 99 and SUB >= 99 and STOP_AFTER >= 1:
        s_dbg = P.dsem("d_dbg")
        P.dma("sp", dbg_ksc[:, 0:512], ksc_d[:].rearrange("p k c -> p (k c)"), s_dbg, reads=[t_ksc])
        P.dma("sp", dbg_ksc[:, 512:768], ksc_m[:].rearrange("p k c -> p (k c)"), s_dbg, reads=[t_ksc])

    after = P.all_events()
    m_mix = A.mark()
    mix = A.alloc("mix", [128, 16, D], BF16)
    t_mix = [Tok("mix%d" % i, after) for i in range(16)]
    m2 = A.mark()
    gtab = A.alloc("gtab", [128, NH, GW], F32)
    gfar = A.alloc("gfar", [128, 16], F32)
    gsub = A.alloc("gsub", [128, 128], F32)
    lam = A.alloc("lam", [128, 256], F32)
    lamw = A.alloc("lamw", [128, 8], F32)
    t_g = Tok("gtab")
    s_g = P.dsem("d_g")
    dmy = Tok("dummy2", after)
    P.dma("sp", gtab[:], gtab_d, s_g, acc_writes=[t_g], reads=[dmy])
    P.dma("sp", gfar[:], gfar_d, s_g, acc_writes=[t_g])
    P.dma("sp", gsub[:], gsub_d, s_g, acc_writes=[t_g])
    P.dma("sp", lam[:], lam_d, s_g, acc_writes=[t_g])
    t_lam = Tok("lam")
    P.op("dve", I("tensor_scalar", out=gfar[:], in0=gfar[:], scalar1=-C_DIFF, scalar2=None, op0=ALU.add),
         reads=[t_g], writes=[t_lam])
    P.op("dve", I("scalar_tensor_tensor", out=lam[:, 0:64], in0=lam[:, 0:64], scalar=1.0, in1=lam[:, 64:128],
                  op0=ALU.mult, op1=ALU.mult, accum_out=lamw[:, 0:1]), reads=[t_g], writes=[t_lam])
    P.op("dve", I("scalar_tensor_tensor", out=lam[:, 128:192], in0=lam[:, 128:192], scalar=1.0, in1=lam[:, 192:256],
                  op0=ALU.mult, op1=ALU.mult, accum_out=lamw[:, 1:2]), reads=[t_g], writes=[t_lam])
    P.op("act", I("activation", out=lamw[:, 2:4], in_=lamw[:, 0:2], func=AF.Exp), reads=[t_lam], writes=[t_lam])
    P.op("dve", I("tensor_tensor", out=lamw[:, 4:5], in0=lamw[:, 3:4], in1=lamw[:, 2:3], op=ALU.subtract),
         reads=[], writes=[t_lam])
    P.op("dve", I("tensor_scalar", out=lamw[:, 5:6], in0=lamw[:, 4:5], scalar1=-LAMBDA_INIT, scalar2=None, op0=ALU.add),
         reads=[], writes=[t_lam])
    NEGLAM = lamw[:, 5:6]

    Kb = Ring(P, A, "Kb", 4, [128, S], BF16, after)
    Qb = Ring(P, A, "Qb", 2, [128, SQ], BF16, after, dma=False)
    Qrb = Ring(P, A, "Qrb", 2, [128, SQ], BF16, after, dma=False)
    for i in range(4):
        lo = 64 if i % 2 == 0 else 0
        P.op("dve", I("memset", Kb.t[i][lo:lo + 64, :], 0.0), writes=[Kb.tok[i]])
    for i in range(2):
        P.op("dve", I("memset", Qrb.t[i][64:128, :], 0.0), writes=[Qrb.tok[i]])
    Vb = Ring(P, A, "Vb", 2, [128, 32, 130], BF16, after, dma=False)
    LA = 3
    PT = Ring(P, A, "PT", LA + 1, [128, 512], BF16, after, dma=False)
    btmp = Ring(P, A, "btmp", 2, [128, 512], F32, after, dma=False)
    o1n = Ring(P, A, "o1n", 2, [128, 4, 128], F32, after, dma=False)
    cmb = Ring(P, A, "cmb", 8, [128, 128], F32, after, dma=False)
    jk2 = Ring(P, A, "jk2", 2, [128, 128], F32, after, dma=False)
    sm = Ring(P, A, "sm", 16, [128, 8], F32, after, dma=False)
    Sb = [(banks[i], PTok("S%d" % i, after)) for i in (0, 1, 2, 7)]
    Oset = []
    for si in range(2):
        bl = [(banks[3 + si * 2 + i], PTok("O%d_%d" % (si, i), after)) for i in range(2)]
        Oset.append([(bl[j // 2][0], bl[j // 2][1], (j % 2) * 130) for j in range(4)])

    def load_head(srcs):
        res = []
        sem = None
        ev = None
        for ring, dram_ap, scr_tok, rows in srcs:
            t_, tok_, sem_ = ring.next()
            if sem is None:
                sem = sem_
            ev = P.dma("sp", t_[rows[0]:rows[1]], dram_ap, sem, reads=[scr_tok], writes=[tok_])
            res.append((t_, tok_))
        for _, tok_ in res:
            tok_.w = {("dma", sem.h.num): ev}
        return res

    specs = []
    for h in range(NH):
        for qc in range(4):
            for mp in range(2):
                specs.append(("d", h, qc, mp))
    for h in range(NH):
        for qc in range(4):
            specs.append(("m", h, qc, 0))
    if STOP_AFTER < 3:
        specs = []
    heads = {}
    o1s = {}

    def head_data(kind, h):
        if (kind, h) not in heads:
            if kind == "d":
                heads[(kind, h)] = load_head([(Kb, KdT[h, 0:64, :], t_scr["KdT"], (0, 64)), (Qb, QdT[h], t_scr["QdT"], (0, 128)),
                                              (Vb, Vd[h], t_scr["Vd"], (0, 128)), (Kb, KdT[h, 64:128, :], t_scr["KdT"], (64, 128))])
            else:
                heads[(kind, h)] = load_head([(Kb, KmT[h], t_scr["KmT"], (0, 128)), (Qb, QmnT[h], t_scr["QmnT"], (0, 128)),
                                              (Vb, Vm[h], t_scr["Vm"], (0, 128)), (Qrb, QmrT[h], t_scr["QmrT"], (0, 64))])
        return heads[(kind, h)]

    steps = [(si, kt) for si in range(len(specs)) for kt in range(32)]
    s_i = [0]

    def emit_qk(i):
        si, kt = steps[i]
        kind, h, qc, mp = specs[si]
        hd = head_data(kind, h)
        (K, Ktok), (Q, Qtok), (V, Vtok) = hd[0], hd[1], hd[2]
        sb_, stok = Sb[s_i[0] % len(Sb)]
        s_i[0] += 1
        if kind == "d":
            Kp, Kptok = (K, Ktok) if mp == 0 else hd[3]
            P.op("pe", I("matmul", sb_[:], lhsT=Kp[:, kt * 128:(kt + 1) * 128],
                         rhs=Q[:, qc * 512:(qc + 1) * 512], start=True, stop=True),
                 reads=[Kptok, Qtok], writes=[stok])
        else:
            Qr, Qrtok = hd[3]
            P.op("pe", I("matmul", sb_[:], lhsT=K[:, kt * 128:(kt + 1) * 128], rhs=Q[:, qc * 512:(qc + 1) * 512],
                         start=True, stop=False), reads=[Ktok, Qtok], writes=[stok], sig=False)
            P.op("pe", I("matmul", sb_[:], lhsT=kropeT[:, kt * 128:(kt + 1) * 128], rhs=Qr[:, qc * 512:(qc + 1) * 512],
                         start=False, stop=True), reads=[t_krope, Qrtok], writes=[stok])
        pt, pttok, _ = PT.next()
        if kind == "d":
            scale_ap = ksc_d[:, kt, h * 2 + mp:h * 2 + mp + 1]
            m = kt - 4 * qc
            if -1 <= m <= 4:
                bt, bttok, _ = btmp.next()
                g0 = (4 - m) * 128
                P.op("dve", I("scalar_tensor_tensor", out=bt[:], in0=sb_[:], scalar=scale_ap, in1=gtab[:, h, g0:g0 + 512],
                              op0=ALU.mult, op1=ALU.add), reads=[stok, t_ksc, t_g], writes=[bttok])
                P.op("act", I("activation", out=pt[:], in_=bt[:], func=AF.Exp, bias=cvc(CV_NCD), scale=1.0),
                     reads=[bttok, t_cv], writes=[pttok])
            else:
                side = 0 if m < -1 else 1
                P.op("act", I("activation", out=pt[:], in_=sb_[:], func=AF.Exp,
                              bias=gfar[:, h * 2 + side:h * 2 + side + 1], scale=scale_ap),
                     reads=[stok, t_ksc, t_lam], writes=[pttok])
        else:
            scale_ap = ksc_m[:, kt, h:h + 1]
            P.op("act", I("activation", out=pt[:], in_=sb_[:], func=AF.Exp, bias=cvc(CV_NCM), scale=scale_ap),
                 reads=[stok, t_ksc, t_cv], writes=[pttok])
        return pt, pttok

    def emit_av(i, pt, pttok):
        si, kt = steps[i]
        kind, h, qc, mp = specs[si]
        hd = head_data(kind, h)
        V, Vtok = hd[2]
        for j in range(4):
            ob, otok, c0 = Oset[si % 2][j]
            P.op("pe", I("matmul", ob[:, c0:c0 + 130], lhsT=pt[:, j * 128:(j + 1) * 128], rhs=V[:, kt, :],
                         start=(kt == 0 and j % 2 == 0), stop=(kt == 31), skip_group_check=True),
                 reads=[pttok, Vtok], writes=[otok], sig=(j == 3))

    def evac(si):
        kind, h, qc, mp = specs[si]
        part2 = []
        if kind == "d" and mp == 0:
            o1s[(h, qc)] = o1n.next()
        for j in range(4):
            ob, otok, c0 = Oset[si % 2][j]
            s_, stok_, _ = sm.next()
            tt = qc * 4 + j
            P.op("dve", I("reciprocal", out=s_[:, 0:1], in_=ob[:, c0 + 128:c0 + 129]), reads=[otok], writes=[stok_])
            if kind == "m":
                P.op("dve", I("tensor_scalar", out=mix[:, tt, 1024 + h * 128:1024 + (h + 1) * 128], in0=ob[:, c0:c0 + 128],
                              scalar1=s_[:, 0:1], scalar2=None, op0=ALU.mult),
                     reads=[otok, stok_], writes=[t_mix[tt]])
            elif mp == 0:
                o1, o1tok, _ = o1s[(h, qc)]
                P.op("dve", I("tensor_scalar", out=o1[:, j, :], in0=ob[:, c0:c0 + 128], scalar1=s_[:, 0:1], scalar2=None,
                              op0=ALU.mult), reads=[otok, stok_], writes=[o1tok])
            else:
                o1, o1tok, _ = o1s[(h, qc)]
                cm, cmtok, _ = cmb.next()
                jk, jktok, _ = jk2.next()
                P.op("dve", I("tensor_scalar", out=cm[:], in0=ob[:, c0:c0 + 128], scalar1=s_[:, 0:1], scalar2=None,
                              op0=ALU.mult), reads=[otok, stok_], writes=[cmtok])
                P.op("dve", I("scalar_tensor_tensor", out=cm[:], in0=cm[:], scalar=NEGLAM, in1=o1[:, j, :],
                              op0=ALU.mult, op1=ALU.add), reads=[o1tok, t_lam], writes=[cmtok])
                P.op("dve", I("scalar_tensor_tensor", out=jk[:], in0=cm[:], scalar=1.0, in1=cm[:],
                              op0=ALU.mult, op1=ALU.mult, accum_out=s_[:, 1:2]),
                     reads=[cmtok], writes=[jktok, stok_])

                def p2(s_=s_, stok_=stok_, cm=cm, cmtok=cmtok, tt=tt, h=h):
                    rstd_ops(s_[:, 1:2], s_[:, 2:3], s_[:, 3:4], 1.0 / 128, CV_LN08, [stok_], stok_, stok_)
                    P.op("dve", I("scalar_tensor_tensor", out=mix[:, tt, h * 128:(h + 1) * 128], in0=cm[:],
                                  scalar=s_[:, 3:4], in1=gsub[:], op0=ALU.mult, op1=ALU.mult),
                         reads=[cmtok, stok_, t_g], writes=[t_mix[tt]])
                part2.append(p2)
        return part2

    pend = []
    live = {}
    nsteps = len(steps)
    for i in range(min(LA, nsteps)):
        live[i] = emit_qk(i)
    for i in range(nsteps):
        if i + LA < nsteps:
            live[i + LA] = emit_qk(i + LA)
        pt, pttok = live.pop(i)
        emit_av(i, pt, pttok)
        si, kt = steps[i]
        if kt == 31:
            for f in evac(si):
                pend.append((i + 6, f))
        while pend and pend[0][0] <= i:
            pend.pop(0)[1]()
    for _, f in pend:
        f()
    A.release(m2)

    after = P.all_events()
    m3 = A.mark()
    wo = A.alloc("wo", [128, 16, D], BF16)
    gt1 = A.alloc("gt1", [128, D], F32)
    t_wo = Tok("wo")
    s_wo = P.dsem("d_wo")
    dmy = Tok("dummy3", after)
    for i in range(4):
        P.dma("pool", wo[:, i * 4:(i + 1) * 4, :], w_out[i * 512:(i + 1) * 512, :].rearrange("(k p) n -> p k n", p=128),
              s_wo, acc_writes=[t_wo], reads=[dmy])
    s_gt1 = P.dsem("d_gt1")
    P.dma("sp", gt1[:], mod_scr[:, 2 * D:3 * D].partition_broadcast(128), s_gt1, reads=[t_modscr, dmy], acc_writes=[t_wo])
    x3 = Ring(P, A, "x3", 1, [128, D], F32, after)
    x1r = Ring(P, A, "x1r", 2, [128, D], F32, after)
    mixT = Ring(P, A, "mixT", 2, [128, 16, 128], BF16, after, dma=False)
    st3 = Ring(P, A, "st3", 2, [128, 4], F32, after, dma=False)
    xn3 = Ring(P, A, "xn3", 2, [128, D], BF16, after, dma=False)
    h2s = Ring(P, A, "h2s", 2, [128, 16, 128], BF16, after)
    tp3 = [(banks[0][:].bitcast(BF16)[:, 0:512], PTok("tp3a", after)),
           (banks[5][:].bitcast(BF16)[:, 0:512], PTok("tp3b", after))]
    yb = [(banks[1 + i], PTok("y%d" % i, after)) for i in range(4)]

    if STOP_AFTER >= 4:
        for tt in range(16):
            mt, mttok, _ = mixT.next()
            for g in range(4):
                tp, tptok = tp3[g % 2]
                for j in range(4):
                    fc = g * 4 + j
                    P.op("pe", I("transpose", tp[:, j * 128:(j + 1) * 128], mix[:, tt, fc * 128:(fc + 1) * 128], ident),
                         reads=[t_mix[tt], t_const], writes=[tptok], sig=(j == 3))
                P.op("act", I("activation", out=mt[:, g * 4:(g + 1) * 4, :], in_=tp.rearrange("p (a b) -> p a b", b=128),
                              func=AF.Copy), reads=[tptok], writes=[mttok])
            xt, xtok, xsem = x3.next()
            P.dma("sp", xt[:], xk[tt * 128:(tt + 1) * 128, :], xsem, writes=[xtok])
            x1, x1tok, x1sem = x1r.next()
            for cb in range(4):
                yk, ytok = yb[cb]
                for fc in range(16):
                    P.op("pe", I("matmul", yk[:], lhsT=mt[:, fc, :], rhs=wo[:, fc, cb * 512:(cb + 1) * 512],
                                 start=(fc == 0), stop=(fc == 15)),
                         reads=[mttok, t_wo], writes=[ytok], sig=(fc == 15))
                P.op("dve", I("tensor_tensor", out=x1[:, cb * 512:(cb + 1) * 512], in0=yk[:], in1=gt1[:, cb * 512:(cb + 1) * 512],
                              op=ALU.mult), reads=[ytok, t_wo], writes=[x1tok])
                P.op("dve", I("tensor_tensor", out=x1[:, cb * 512:(cb + 1) * 512], in0=x1[:, cb * 512:(cb + 1) * 512],
                              in1=xt[:, cb * 512:(cb + 1) * 512], op=ALU.add), reads=[xtok], writes=[x1tok])
            P.dma("sp", x1_scr[tt * 128:(tt + 1) * 128, :], x1[:], x1sem, reads=[x1tok], acc_writes=[t_scr["x1"]])
            st, sttok, _ = st3.next()
            xn, xntok, _ = xn3.next()
            P.op("act", I("activation", out=xn[:], in_=x1[:], func=AF.Square, accum_out=st[:, 0:1]),
                 reads=[x1tok], writes=[xntok, sttok])
            rstd_ops(st[:, 0:1], st[:, 1:2], st[:, 2:3], 1.0 / D, CV_ZERO, [sttok], sttok, sttok)
            P.op("dve", I("tensor_scalar", out=xn[:], in0=x1[:], scalar1=st[:, 2:3], scalar2=None, op0=ALU.mult),
                 reads=[x1tok, sttok], writes=[xntok])
            hs, hstok, hssem = h2s.next()
            for g in range(4):
                tp, tptok = tp3[g % 2]
                for j in range(4):
                    dc = g * 4 + j
                    P.op("pe", I("transpose", tp[:, j * 128:(j + 1) * 128], xn[:, dc * 128:(dc + 1) * 128], ident),
                         reads=[xntok, t_const], writes=[tptok], sig=(j == 3))
                for j in range(4):
                    dc = g * 4 + j
                    P.op("dve", I("tensor_scalar", out=hs[:, dc, :], in0=tp[:, j * 128:(j + 1) * 128],
                                  scalar1=A2T[:, dc:dc + 1], scalar2=B2T[:, dc:dc + 1], op0=ALU.mult, op1=ALU.add),
                         reads=[tptok, t_mod], writes=[hstok])
            P.dma("sp", h2T_scr[:, :, tt * 128:(tt + 1) * 128], hs[:], hssem, reads=[hstok], acc_writes=[t_scr["h2T"]])
    A.release(m_mix)

    after = P.all_events()
    m4 = A.mark()
    T = FFN_T
    NTH = T // 512
    FB = 256
    gt2 = A.alloc("gt2", [128, D], F32)
    t_gt2 = Tok("gt2")
    s_gt2 = P.dsem("d_gt2")
    dmy = Tok("dummy4", after)
    P.dma("sp", gt2[:], mod_scr[:, 5 * D:6 * D].partition_broadcast(128), s_gt2, reads=[t_modscr, dmy], writes=[t_gt2])
    actT = A.alloc("actT", [128, NFC, T], BF16)
    t_act = [Tok("act%d" % i, after) for i in range(NFC)]
    h2b = Ring(P, A, "h2b", 1, [128, 16, T], BF16, after)
    wg = Ring(P, A, "wg", 2, [128, 16, FB], BF16, after)
    wu = Ring(P, A, "wu", 2, [128, 16, FB], BF16, after)
    wd = Ring(P, A, "wd", 6, [128, 11, 512], BF16, after)
    sg = Ring(P, A, "sg", 2, [128, 512], F32, after, dma=False)
    x1p = Ring(P, A, "x1p", 2, [128, 512], F32, after)
    ost = Ring(P, A, "ost", 2, [128, 512], F32, after)
    gb = [(banks[i], PTok("g%d" % i, after)) for i in range(8)]
    gi = [0]

    if STOP_AFTER >= 5:
        for blk in range(SQ // T):
            hb, hbtok, hbsem = h2b.next()
            P.dma("sp", hb[:], h2T_scr[:, :, blk * T:(blk + 1) * T], hbsem, reads=[t_scr["h2T"]], writes=[hbtok])
            for fcb in range(DFF // FB):
                wgt, wgtok, wgsem = wg.next()
                wut, wutok, wusem = wu.next()
                P.dma("pool", wgt[:], w_gate[:, fcb * FB:(fcb + 1) * FB].rearrange("(k p) n -> p k n", p=128),
                      wgsem, writes=[wgtok])
                P.dma("pool", wut[:], w_up[:, fcb * FB:(fcb + 1) * FB].rearrange("(k p) n -> p k n", p=128),
                      wusem, writes=[wutok])
                for fi in range(FB // 128):
                    f = fcb * (FB // 128) + fi
                    for th in range(NTH):
                        gk, gtok = gb[gi[0] % 8]
                        uk, utok = gb[(gi[0] + 1) % 8]
                        gi[0] += 2
                        for (bk_, btok_, wt_, wtok_) in ((gk, gtok, wgt, wgtok), (uk, utok, wut, wutok)):
                            for dc in range(16):
                                P.op("pe", I("matmul", bk_[:], lhsT=wt_[:, dc, fi * 128:(fi + 1) * 128],
                                             rhs=hb[:, dc, th * 512:(th + 1) * 512], start=(dc == 0), stop=(dc == 15)),
                                     reads=[wtok_, hbtok], writes=[btok_], sig=(dc == 15))
                        s_, stok_, _ = sg.next()
                        P.op("act", I("activation", out=s_[:], in_=gk[:], func=AF.Silu), reads=[gtok], writes=[stok_])
                        P.op("dve", I("tensor_tensor", out=actT[:, f, th * 512:(th + 1) * 512], in0=s_[:], in1=uk[:],
                                      op=ALU.mult), reads=[stok_, utok], writes=[t_act[f]])
            for cb in range(4):
                pieces = []
                for q4 in range(4):
                    wdt, wdtok, wdsem = wd.next()
                    P.dma("pool", wdt[:],
                          w_down[q4 * 1408:(q4 + 1) * 1408, cb * 512:(cb + 1) * 512].rearrange("(k p) n -> p k n", p=128),
                          wdsem, writes=[wdtok])
                    pieces.append((wdt, wdtok))
                for tt in range(T // 128):
                    row0 = blk * T + tt * 128
                    yk, ytok = gb[gi[0] % 8]
                    gi[0] += 1
                    for f in range(NFC):
                        wdt, wdtok = pieces[f // 11]
                        P.op("pe", I("matmul", yk[:], lhsT=actT[:, f, tt * 128:(tt + 1) * 128], rhs=wdt[:, f % 11, :],
                                     start=(f == 0), stop=(f == NFC - 1)),
                             reads=[t_act[f], wdtok], writes=[ytok], sig=(f == NFC - 1 or f % 11 == 10))
                    xp, xptok, xpsem = x1p.next()
                    P.dma("sp", xp[:], x1_scr[row0:row0 + 128, cb * 512:(cb + 1) * 512], xpsem,
                          reads=[t_scr["x1"]], writes=[xptok])
                    os_, ostok, ossem = ost.next()
                    P.op("dve", I("tensor_tensor", out=os_[:], in0=yk[:], in1=gt2[:, cb * 512:(cb + 1) * 512], op=ALU.mult),
                         reads=[ytok, t_gt2], writes=[ostok])
                    P.op("dve", I("tensor_tensor", out=os_[:], in0=os_[:], in1=xp[:], op=ALU.add),
                         reads=[xptok], writes=[ostok])
                    P.dma("sp", out_d[row0:row0 + 128, cb * 512:(cb + 1) * 512], os_[:], ossem, reads=[ostok])
    A.release(m4)

    P.final_wait("sp")
    P.emit()
    print("[kernel] sbuf peak %d / %d, waits %d, ops %s" % (
        A.peak, A.hi, P.n_waits, {k: len(v.ops) for k, v in P.E.items()}), flush=True)
    return nc


def _t5_bucket_np(rel):
    nb, me = 16, 8
    base = np.where(rel > 0, nb, 0)
    n = np.abs(rel)
    nf = np.maximum(n, 1).astype(np.float32)
    large = me + (np.log(nf / np.float32(me)) / np.float32(math.log(128 / 8)) * np.float32(nb - me)).astype(np.int32)
    large = np.minimum(large, nb - 1)
    return (base + np.where(n < me, n, large)).astype(np.int64)


def _prep_inputs(inp):
    f = lambda a: np.ascontiguousarray(np.asarray(a, dtype=np.float32))
    x = f(inp["x"]); c = f(inp["c"]); rel_bias = f(inp["rel_bias"])
    w_in = f(inp["w_in"])[0]
    w_q_b = f(inp["w_q_b"])[0]
    w_kv_b = f(inp["w_kv_b"])[0]
    g_q_mla = f(inp["g_q_mla"])[0]; g_k_mla = f(inp["g_k_mla"])[0]
    g_q_a = f(inp["g_q_a"])[0]; g_kv_a = f(inp["g_kv_a"])[0]
    q_d = w_in[:, 0:1024]; k_d = w_in[:, 1024:2048]; v_d = w_in[:, 2048:3072]
    cq = w_in[:, 3072:3584]; ckv = w_in[:, 3584:3840]; kpe = w_in[:, 3840:3904]
    perm = (np.arange(64) + 32) % 64
    w_kv = f(np.concatenate([k_d, v_d, ckv, kpe, kpe[:, perm]], axis=1))
    w_q = f(np.concatenate([q_d, cq], axis=1))
    qb = w_q_b.reshape(512, NH, 192)
    w_qb = f(np.concatenate([qb[:, :, 0:128].reshape(512, -1), qb[:, :, 128:192].reshape(512, -1),
                             qb[:, :, 128:192][:, :, perm].reshape(512, -1)], axis=1))
    kvb = w_kv_b.reshape(256, NH, 256)
    w_kvb = f(np.concatenate([kvb[:, :, 0:128].reshape(256, -1), kvb[:, :, 128:256].reshape(256, -1)], axis=1))
    gcols = np.ones((128, 16), np.float32)
    gcols[:, 0] = np.tile(f(inp["g_q_diff"])[0], 2)
    gcols[:, 1] = np.tile(f(inp["g_k_diff"])[0], 2)
    gcols[:, 2:6] = g_q_a.reshape(4, 128).T
    gcols[:, 6:8] = g_kv_a.reshape(2, 128).T
    gcols[:, 8] = g_q_mla[0:128]
    gcols[0:64, 9] = g_q_mla[128:192]
    gcols[0:64, 10] = g_q_mla[128:192][perm]
    gcols[:, 11] = g_k_mla[0:128]
    gcols[0:64, 12] = g_k_mla[128:192]
    gcols[0:64, 13] = g_k_mla[128:192][perm]
    gsub_bc = f(np.broadcast_to(f(inp["g_subln"])[0][None, :], (128, 128)))
    lam_bc = f(np.broadcast_to(f(inp["lambda_vecs"])[0].reshape(1, 256), (128, 256)))
    cmat = np.zeros((128, 512), np.float32)
    cmat[:, 0:128] = np.eye(128)
    cmat[:, 128:256] = 1.0
    cmat[0:64, 256:320] = 1.0
    cmat[64:128, 320:384] = 1.0
    cmat[0:64, 384] = 1.0
    cmat[64:128, 385] = 1.0
    inv = (1.0 / (np.float32(10000.0) ** (np.arange(0, 64, 2, dtype=np.float32) / np.float32(64)))).astype(np.float32)
    shared = dict(
        w_ada=f(inp["w_ada"])[0], b_ada=f(inp["b_ada"]).reshape(1, -1),
        g1T=f(f(inp["g_norm1"])[0].reshape(16, 128).T), g2T=f(f(inp["g_norm2"])[0].reshape(16, 128).T),
        w_kv=w_kv, w_q=w_q, w_qb=w_qb, w_kvb=w_kvb,
        w_out=f(inp["w_out"])[0], w_gate=f(inp["w_gate"])[0], w_up=f(inp["w_up"])[0], w_down=f(inp["w_down"])[0],
        gcols=gcols, gsub_bc=gsub_bc, lam_bc=lam_bc, cmat=cmat,
    )
    maps = []
    ii = np.arange(128)[:, None]
    tt = np.arange(GW)[None, :]
    for core in range(8):
        b, half = core // 2, core % 2
        xb = x[b]
        pos = np.arange(S, dtype=np.float32)
        if half == 1:
            xb = xb[::-1]
            pos = pos[::-1]
        ang = (pos[:, None] * inv[None, :]).astype(np.float32)
        cos = np.cos(ang).astype(np.float32).T
        sin = np.sin(ang).astype(np.float32).T
        cos_t = f(np.concatenate([cos, cos], axis=0))
        sin_t = f(np.concatenate([-sin, sin], axis=0))
        sgn = 1 if half == 0 else -1
        dloc = ii - tt + 512
        bidx = _t5_bucket_np((sgn * dloc).astype(np.int32))
        gtab = f(np.transpose(rel_bias[bidx], (0, 2, 1)))
        bneg = _t5_bucket_np(np.array([sgn * -1000], np.int32))[0]
        bpos = _t5_bucket_np(np.array([sgn * 1000], np.int32))[0]
        gfar = np.zeros((128, 16), np.float32)
        gfar[:, 0::2] = rel_bias[bneg][None, :]
        gfar[:, 1::2] = rel_bias[bpos][None, :]
        m = dict(shared)
        m.update(xk=f(xb), cT=f(c[b].reshape(16, 128).T), cos_t=cos_t, sin_t=sin_t, gtab=gtab, gfar=gfar)
        maps.append(m)
    return maps


_NC_CACHE = {}


def kernel(**inputs):
    maps = _prep_inputs(inputs)
    if "nc" not in _NC_CACHE:
        _NC_CACHE["nc"] = build_program()
    nc = _NC_CACHE["nc"]
    res = run_bass_kernel_spmd(nc, maps, core_ids=list(range(8)))
    out = np.zeros((B, S, D), np.float32)
    for core in range(8):
        b, half = core // 2, core % 2
        o = np.asarray(res.results[core]["out"], dtype=np.float32)
        if half == 0:
            out[b, 0:SQ] = o
        else:
            out[b, SQ:S] = o[::-1]
    if DEBUG:
        kernel.last_results = res.results
    return out
```
